# Optimizing a Trainium2 kernel written in Bass

```python
import math
import jax, jax.numpy as jnp
from jax import lax
import numpy as np

D_MODEL = 1024
BATCH = 8
SEQ = 2048
DEPTH = 2
DEC_BATCH = 128
DEC_SEQ = 4
PAST_LEN = 16384
PAGE_SIZE = 128

N_MIXERS = 2
N_A = (DEPTH + 1) // 2
N_B = DEPTH // 2
W_LRU = 1536
N_BLK = 16
BLK_W = W_LRU // N_BLK
CONV_W = 4
LRU_C = 8.0
EXPAND = 128
H_B = D_MODEL // EXPAND
DK_B = EXPAND
DV_B = D_MODEL // H_B
INNER_B = H_B * DK_B
CHUNK = 64
EPS = 1e-6

kernel_name = "hawk_hgrn2_hybrid_step"


def _rmsnorm(x, gain):
    xf = x.astype(jnp.float32)
    y = xf * lax.rsqrt(jnp.mean(xf * xf, axis=-1, keepdims=True) + EPS)
    return (y * gain.astype(jnp.float32)).astype(x.dtype)


def _causal_conv(xb, w, b, buf):
    T = xb.shape[1]
    xp = jnp.concatenate([buf.astype(xb.dtype), xb], axis=1)
    out = b
    for k in range(CONV_W):
        out = out + w[k] * xp[:, k:k + T]
    return out, xp[:, -(CONV_W - 1):]


def _lru_combine(e1, e2):
    a1, b1 = e1
    a2, b2 = e2
    return a1 * a2, a2 * b1 + b2


def _rg_lru(xc, w_r, b_r, w_i, b_i, lam, h0):
    Bsz, T, W = xc.shape
    xf = xc.astype(jnp.float32)
    xh = xf.reshape(Bsz, T, N_BLK, BLK_W)
    r = jax.nn.sigmoid(jnp.einsum('btni,nij->btnj', xh, w_r.astype(jnp.float32)).reshape(Bsz, T, W) + b_r)
    ig = jax.nn.sigmoid(jnp.einsum('btni,nij->btnj', xh, w_i.astype(jnp.float32)).reshape(Bsz, T, W) + b_i)
    log_a = -LRU_C * jax.nn.softplus(-lam.astype(jnp.float32)) * r
    a = jnp.exp(log_a)
    bterm = jnp.sqrt(-jnp.expm1(2.0 * log_a)) * (ig * xf)
    bterm = bterm.at[:, 0].add(a[:, 0] * h0.astype(jnp.float32))
    _, h = lax.associative_scan(_lru_combine, (a, bterm), axis=1)
    return h, h[:, -1]


def _lru_layer(x, g_norm, w_in, conv_w, conv_b, w_r, b_r, w_i, b_i, lam, w_out, h0, buf):
    xn = _rmsnorm(x, g_norm)
    u = xn @ w_in
    xb, gate = u[..., :W_LRU], u[..., W_LRU:]
    xc, new_buf = _causal_conv(xb, conv_w, conv_b, buf)
    h, h_last = _rg_lru(xc, w_r, b_r, w_i, b_i, lam, h0)
    y = (h * jax.nn.silu(gate.astype(jnp.float32))).astype(x.dtype)
    return x + y @ w_out, h_last.astype(x.dtype), new_buf.astype(x.dtype)


def _hgrn2_chunked(q, k, v, log_g, S0):
    Bsz, T, H, DK = q.shape
    DV = v.shape[-1]
    C = math.gcd(T, CHUNK)
    n = T // C

    def to_chunks(z):
        return z.reshape(Bsz, n, C, H, z.shape[-1]).transpose(1, 0, 3, 2, 4)

    qs, ks, vs, gs = to_chunks(q), to_chunks(k), to_chunks(v), to_chunks(log_g)
    causal = jnp.tril(jnp.ones((C, C), dtype=bool))

    def step(S, inp):
        qc, kc, vc, gc = inp
        cum = jnp.cumsum(gc, axis=2)
        inter = jnp.einsum('bhtk,bhkv->bhtv', qc * jnp.exp(cum), S)
        diff = cum[:, :, :, None, :] - cum[:, :, None, :, :]
        decay = jnp.exp(jnp.where(causal[:, :, None], diff, -jnp.inf))
        attn = jnp.einsum('bhtk,bhsk,bhtsk->bhts', qc, kc, decay)
        intra = jnp.einsum('bhts,bhsv->bhtv', attn, vc)
        last = cum[:, :, -1:, :]
        S_new = jnp.exp(last[:, :, 0, :])[..., None] * S + jnp.einsum(
            'bhsk,bhsv->bhkv', kc * jnp.exp(last - cum), vc)
        return S_new, inter + intra

    S_T, o = lax.scan(step, S0.astype(jnp.float32), (qs, ks, vs, gs))
    o = o.transpose(1, 0, 3, 2, 4).reshape(Bsz, T, H, DV)
    return o, S_T


def _hgrn_layer(x, g_norm, w_in, lb, o_gain, w_out, S0):
    Bsz, T, _ = x.shape
    xn = _rmsnorm(x, g_norm)
    u = (xn @ w_in).astype(jnp.float32)
    q, f, iv, gate = jnp.split(u, 4, axis=-1)
    q = jax.nn.silu(q).reshape(Bsz, T, H_B, DK_B)
    lbh = lb.reshape(H_B, DK_B)
    log_g = jnp.logaddexp(jnp.log(lbh), jnp.log1p(-lbh) + jax.nn.log_sigmoid(f.reshape(Bsz, T, H_B, DK_B)))
    k = -jnp.expm1(log_g)
    v = iv.reshape(Bsz, T, H_B, DV_B)
    o, S_T = _hgrn2_chunked(q, k, v, log_g, S0)
    o = o * lax.rsqrt(jnp.mean(o * o, axis=-1, keepdims=True) + EPS) * o_gain.reshape(H_B, DV_B).astype(jnp.float32)
    y = (o.reshape(Bsz, T, INNER_B) * jax.nn.silu(gate)).astype(x.dtype)
    return x + y @ w_out, S_T.astype(x.dtype)


def _trunk(x, h0s, buf0s, S0s, norm_gain, a_w_in, a_conv_w, a_conv_b, a_w_r, a_b_r, a_w_i, a_b_i,
           a_lambda, a_w_out, b_w_in, b_lb_logits, b_o_gain, b_w_out, final_gain):
    sm = jax.nn.softmax(b_lb_logits.astype(jnp.float32), axis=0)
    lb_all = jnp.cumsum(sm, axis=0) - sm[0]
    hs, bufs, Ss = [], [], []
    for i in range(DEPTH):
        j = i // N_MIXERS
        if i % N_MIXERS == 0:
            x, h, bf = _lru_layer(x, norm_gain[i], a_w_in[j], a_conv_w[j], a_conv_b[j], a_w_r[j], a_b_r[j],
                                  a_w_i[j], a_b_i[j], a_lambda[j], a_w_out[j], h0s[j], buf0s[j])
            hs.append(h)
            bufs.append(bf)
        else:
            x, S = _hgrn_layer(x, norm_gain[i], b_w_in[j], lb_all[i], b_o_gain[j], b_w_out[j], S0s[j])
            Ss.append(S)
    return _rmsnorm(x, final_gain), jnp.stack(hs), jnp.stack(bufs), jnp.stack(Ss)


def setup_inputs(seed: int = 0) -> dict:
    key = jax.random.key(seed)
    ks = jax.random.split(key, 24)
    nrm = jax.random.normal
    f32 = jnp.float32
    a_sig = jax.random.uniform(ks[12], (N_A, W_LRU), f32, 0.9, 0.999)
    return {
        "x_prompt": nrm(ks[0], (BATCH, SEQ, D_MODEL), f32),
        "x_sample": nrm(ks[1], (DEC_BATCH, DEC_SEQ, D_MODEL), f32),
        "state_lru_h": 0.5 * nrm(ks[2], (N_A, DEC_BATCH, W_LRU), f32),
        "state_lru_conv": nrm(ks[3], (N_A, DEC_BATCH, CONV_W - 1, W_LRU), f32),
        "state_hgrn": 0.5 * nrm(ks[4], (N_B, DEC_BATCH, H_B, DK_B, DV_B), f32),
        "norm_gain": 1.0 + 0.1 * nrm(ks[5], (DEPTH, D_MODEL), f32),
        "a_w_in": nrm(ks[6], (N_A, D_MODEL, 2 * W_LRU), f32) * D_MODEL ** -0.5,
        "a_conv_w": nrm(ks[7], (N_A, CONV_W, W_LRU), f32) * CONV_W ** -0.5,
        "a_conv_b": 0.01 * nrm(ks[8], (N_A, W_LRU), f32),
        "a_w_r": nrm(ks[9], (N_A, N_BLK, BLK_W, BLK_W), f32) * BLK_W ** -0.5,
        "a_b_r": 0.01 * nrm(ks[10], (N_A, W_LRU), f32),
        "a_w_i": nrm(ks[11], (N_A, N_BLK, BLK_W, BLK_W), f32) * BLK_W ** -0.5,
        "a_b_i": 0.01 * nrm(ks[13], (N_A, W_LRU), f32),
        "a_lambda": jnp.log(a_sig) - jnp.log1p(-a_sig),
        "a_w_out": nrm(ks[14], (N_A, W_LRU, D_MODEL), f32) * W_LRU ** -0.5,
        "b_w_in": nrm(ks[15], (N_B, D_MODEL, 4 * INNER_B), f32) * D_MODEL ** -0.5,
        "b_lb_logits": nrm(ks[16], (DEPTH, INNER_B), f32),
        "b_o_gain": 1.0 + 0.1 * nrm(ks[17], (N_B, INNER_B), f32),
        "b_w_out": nrm(ks[18], (N_B, INNER_B, D_MODEL), f32) * INNER_B ** -0.5,
        "final_gain": 1.0 + 0.1 * nrm(ks[19], (D_MODEL,), f32),
    }


def reference(x_prompt, x_sample, state_lru_h, state_lru_conv, state_hgrn, norm_gain, a_w_in, a_conv_w,
              a_conv_b, a_w_r, a_b_r, a_w_i, a_b_i, a_lambda, a_w_out, b_w_in, b_lb_logits, b_o_gain,
              b_w_out, final_gain):
    dt = x_prompt.dtype
    h0p = jnp.zeros((N_A, BATCH, W_LRU), dt)
    buf0p = jnp.zeros((N_A, BATCH, CONV_W - 1, W_LRU), dt)
    S0p = jnp.zeros((N_B, BATCH, H_B, DK_B, DV_B), dt)
    y_prompt, hp, bp, Sp = _trunk(x_prompt, h0p, buf0p, S0p, norm_gain, a_w_in, a_conv_w, a_conv_b, a_w_r,
                                  a_b_r, a_w_i, a_b_i, a_lambda, a_w_out, b_w_in, b_lb_logits, b_o_gain,
                                  b_w_out, final_gain)
    y_sample, hs, bs, Ss = _trunk(x_sample, state_lru_h, state_lru_conv, state_hgrn, norm_gain, a_w_in,
                                  a_conv_w, a_conv_b, a_w_r, a_b_r, a_w_i, a_b_i, a_lambda, a_w_out, b_w_in,
                                  b_lb_logits, b_o_gain, b_w_out, final_gain)
    return (y_prompt, y_sample, hp, bp, Sp, hs, bs, Ss)
```

```python
import numpy as np
import concourse.bass as bass
import concourse.mybir as mybir
from concourse.bass_utils import run_bass_kernel_spmd

F32 = mybir.dt.float32
BF16 = mybir.dt.bfloat16
AF = mybir.ActivationFunctionType
ALU = mybir.AluOpType

D = 1024
W = 1536
SEQ = 2048
NT = 512
NS = 64
NSEQ = 16
EPS = 1e-6
NCORES = 8
SB_BASE = 16512
SB_LIMIT = 16512 + 212863


class Buf:
    __slots__ = ("name", "lw", "rd", "sem", "dcount", "excl")

    def __init__(self, name):
        self.name = name
        self.excl = False
        self.lw = None
        self.rd = []
        self.sem = None
        self.dcount = 0


class Op:
    __slots__ = ("eng", "fn", "deps", "idx", "signal", "count", "dma", "buf", "reads", "writes", "ndma", "calls", "nobar")


class Rec:
    def __init__(self):
        self.calls = []

    def __getattr__(self, name):
        def f(*a, **k):
            self.calls.append((name, a, k))
            return len(self.calls) - 1
        return f


COMPUTE = ("pe", "act", "dve", "pool")
ENGS = ("pe", "act", "dve", "pool", "sp")


class Prog:
    def __init__(self, nc):
        self.nc = nc
        self.ops = {e: [] for e in ENGS}
        self.dma_since_barrier = []
        self.dma_nobar = []
        self.eng_obj = {"pe": nc.tensor, "act": nc.scalar, "dve": nc.vector, "pool": nc.gpsimd, "sp": nc.sync}
        self.bufs = {}

    def buf(self, name):
        b = self.bufs.get(name)
        if b is None:
            b = Buf(name)
            self.bufs[name] = b
        return b

    def add(self, eng, fn, R=(), W=(), dma=None, ndma=1, bar=True):
        op = Op()
        op.nobar = False
        op.eng = eng
        op.fn = fn
        rec = Rec()
        fn(rec)
        op.calls = rec.calls
        op.signal = False
        op.count = 0
        op.dma = dma is not None
        op.buf = dma
        if dma is not None:
            if dma.sem is None:
                dma.sem = self.nc.alloc_semaphore("dsem_" + dma.name)
            dma.dcount += 16 * ndma
            op.count = dma.dcount
            op.ndma = ndma
        R = [self.buf(b) if isinstance(b, str) else b for b in R]
        W = [self.buf(b) if isinstance(b, str) else b for b in W]
        op.reads = R
        op.writes = W
        deps = {}
        for b in R:
            if b.lw is not None:
                deps[id(b.lw)] = (b.lw, True)
            if b.excl:
                for r in b.rd:
                    if r.eng != eng and id(r) not in deps:
                        deps[id(r)] = (r, False)
        for b in W:
            if b.lw is not None and id(b.lw) not in deps:
                deps[id(b.lw)] = (b.lw, False)
            for r in b.rd:
                if id(r) not in deps:
                    deps[id(r)] = (r, False)
        for b in W:
            b.lw = op
            b.rd = []
        for b in R:
            if b.lw is not op:
                b.rd.append(op)
        op.deps = list(deps.values())
        op.idx = len(self.ops[eng])
        self.ops[eng].append(op)
        if op.dma:
            op.nobar = not bar
            if bar:
                self.dma_since_barrier.append(op)
            else:
                self.dma_nobar.append(op)
        return op

    def barrier(self, all_dma=False):
        if all_dma:
            self.dma_since_barrier += self.dma_nobar
            self.dma_nobar = []
        lasts = []
        for e in ENGS:
            for o in reversed(self.ops[e]):
                if o.dma and getattr(o, "nobar", False) and not all_dma:
                    continue
                lasts.append(o)
                break
        last_by_buf = {}
        for o in self.dma_since_barrier:
            last_by_buf[id(o.buf)] = o
        dmas = list(last_by_buf.values())
        self.dma_since_barrier = []
        for e in ENGS:
            op = Op()
            op.nobar = False
            op.eng = e
            op.fn = None
            op.signal = False
            op.count = 0
            op.dma = False
            op.buf = None
            op.reads = []
            op.writes = []
            op.deps = [(o, True) for o in lasts if o.eng != e or o.dma] + [(o, True) for o in dmas]
            op.idx = len(self.ops[e])
            self.ops[e].append(op)

    def emit(self):
        nc = self.nc
        def needs(op, dep, raw):
            if dep.dma:
                return True
            if dep.fn is None:
                return False
            if dep.eng != op.eng:
                return True
            if op.eng == "pe":
                return False
            return True

        for e in ENGS:
            for op in self.ops[e]:
                op.deps = [(d, r) for (d, r) in op.deps if needs(op, d, r)]
                for d, r in op.deps:
                    d.signal = True
        sems = {}
        for e in COMPUTE:
            sems[e] = nc.alloc_semaphore("sem_" + e)
            c = 0
            for op in self.ops[e]:
                if op.dma:
                    continue
                if op.signal and op.fn is not None:
                    c += 1
                    op.count = c
        for e in ENGS:
            eng = self.eng_obj[e]
            waited = {}
            for op in self.ops[e]:
                for d, r in op.deps:
                    if d.dma:
                        sem = d.buf.sem
                        val = d.count
                        if val == 0:
                            raise RuntimeError("dma dep emitted before producer: %s" % d.buf.name)
                    else:
                        sem = sems[d.eng]
                        val = d.count
                    key = id(sem)
                    if waited.get(key, 0) >= val:
                        continue
                    waited[key] = val
                    eng.wait_ge(sem, val)
                if op.fn is None:
                    continue
                res = [getattr(eng, name)(*a, **k) for (name, a, k) in op.calls]
                if op.dma:
                    assert len(res) == op.ndma, (op.buf.name, len(res), op.ndma)
                    for ins in res:
                        ins.then_inc(op.buf.sem, 16)
                elif op.signal:
                    res[-1].then_inc(sems[e], 1)


class Alloc:
    def __init__(self, nc):
        self.nc = nc
        self.ptr = SB_BASE
        self.n = 0

    def __call__(self, shape, dtype, at=None):
        esz = 2 if dtype == BF16 else 4
        nbytes = esz
        for s in shape[1:]:
            nbytes *= s
        nbytes = (nbytes + 31) // 32 * 32
        if at is None:
            at = self.ptr
            self.ptr += nbytes
        assert at + nbytes <= SB_LIMIT, ("sbuf overflow", at, nbytes)
        self.n += 1
        return self.nc.alloc_sbuf_tensor_at("t%d" % self.n, list(shape), dtype, offset=at)


def build_nc():
    nc = bass.Bass("TRN2", target_bir_lowering=False)
    NTOK = SEQ + NS
    xin = nc.dram_tensor("xin", [D, NTOK], F32, kind="ExternalInput").ap()
    w_in0 = nc.dram_tensor("w_in0", [D, 2 * W], F32, kind="ExternalInput").ap()
    w_out0 = nc.dram_tensor("w_out0", [W, D], F32, kind="ExternalInput").ap()
    w_in1 = nc.dram_tensor("w_in1", [D, 4 * D], F32, kind="ExternalInput").ap()
    w_out1 = nc.dram_tensor("w_out1", [D, D], F32, kind="ExternalInput").ap()
    diagw_d = nc.dram_tensor("diagw", [128, 48, 128], F32, kind="ExternalInput").ap()
    wbd_d = nc.dram_tensor("wbd", [128, 4, 2, 3, 384], F32, kind="ExternalInput").ap()
    pvec_d = nc.dram_tensor("pvec", [128, 96], F32, kind="ExternalInput").ap()
    cst_d = nc.dram_tensor("cst", [128, 976], F32, kind="ExternalInput").ap()
    h0_d = nc.dram_tensor("h0", [128, 12, NSEQ], F32, kind="ExternalInput").ap()
    cv0_d = nc.dram_tensor("cv0", [128, 12, NSEQ, 3], F32, kind="ExternalInput").ap()
    s0_d = nc.dram_tensor("s0", [128, NSEQ, 8, 128], F32, kind="ExternalInput").ap()
    yout = nc.dram_tensor("yout", [D, NTOK], F32, kind="ExternalOutput").ap()
    hout_d = nc.dram_tensor("hout", [128, 12, 17], F32, kind="ExternalOutput").ap()
    cvout_d = nc.dram_tensor("cvout", [128, 12, 17, 3], F32, kind="ExternalOutput").ap()
    sout_d = nc.dram_tensor("sout", [128, 17, 8, 128], F32, kind="ExternalOutput").ap()

    xin_v = xin.rearrange("(c p) n -> p c n", p=128)
    yout_v = yout.rearrange("(c p) n -> p c n", p=128)
    w_in0_v = w_in0.rearrange("(c p) n -> p c n", p=128)
    w_out0_v = w_out0.rearrange("(c p) n -> p c n", p=128)
    w_in1_v = w_in1.rearrange("(c p) n -> p c n", p=128)
    w_out1_v = w_out1.rearrange("(c p) n -> p c n", p=128)

    P = Prog(nc)
    A = Alloc(nc)
    B = P.buf

    xt = A([128, 8, NT], F32)
    xn = A([128, 8, NT], BF16)
    xsq = A([128, 2, NT], BF16)
    stat = A([128, 2, NT], F32)
    NSLOT = 15
    slots = A([128, NSLOT, NT], F32)
    ybuf = A([128, 12, NT], BF16)
    WA = A([128, 4, 8, 384], BF16)
    WV = A([128, 2, 8, 256], BF16)
    WO = A([128, 5, 12, 128], BF16)
    ident = A([128, 128], BF16)
    ones = A([128, 128], BF16)
    maskP = A([128, 128], BF16)
    maskS = A([128, 128], BF16)
    cst = A([128, 976], F32)
    pvec = A([128, 96], F32)
    der = A([128, 96], F32)
    halo = A([128, 12, 4], BF16)
    hcar = A([128, 12], F32)
    WD = A([128, 2, 12, 128], BF16)
    WBD = A([128, 2, 2, 3, 384], BF16)
    S = A([128, 8, 128], F32)
    Sb = A([128, 8, 128], BF16)
    hstage = A([128, 12, 17], F32)
    cstage = A([128, 12, 17, 3], F32)
    ARENA0 = A.ptr

    ps = [nc.alloc_psum_tensor("ps%d" % i, [128, 512], F32) for i in range(8)]
    psb = [B("ps%d" % i) for i in range(8)]
    for b_ in psb:
        b_.excl = True

    cmP = cst[:, 384:896]
    cmS = cst[:, 896:960]
    rowmask = cst[:, 960:976]
    G0, G1, GF, CB, BR, BI, LAM, L0C, L1C, OG = 0, 8, 16, 24, 36, 48, 60, 72, 80, 88
    NBR, NBI, CC, C2, LB, LNOM, T0, T1, T2, T3 = 0, 12, 24, 36, 48, 56, 64, 72, 80, 88

    def ld(dst, src, bname, eng="sp"):
        P.add(eng, lambda e: [e.dma_start(out=dst, in_=src)], W=[bname], dma=B(bname))

    ld(cst[:], cst_d, "cst")
    ld(pvec[:], pvec_d, "pvec")
    P.add("dve", lambda e: e.tensor_copy(ident[:], cst[:, 0:128]), R=["cst"], W=["ident"])
    P.add("dve", lambda e: e.tensor_copy(maskP[:], cst[:, 128:256]), R=["cst"], W=["maskP"])
    P.add("dve", lambda e: e.tensor_copy(maskS[:], cst[:, 256:384]), R=["cst"], W=["maskS"])
    P.add("dve", lambda e: e.memset(ones[:], 1.0), W=["ones"])
    P.add("dve", lambda e: e.memset(hcar[:], 0.0), W=["hcar"])
    P.add("dve", lambda e: e.memset(halo[:], 0.0), W=["halo"])
    P.add("dve", lambda e: e.memset(S[:], 0.0), W=["S"])
    P.add("dve", lambda e: e.memset(Sb[:], 0.0), W=["Sb"])
    P.add("dve", lambda e: e.tensor_scalar(der[:, NBR:NBR + 12], pvec[:, BR:BR + 12], -1.0, None, ALU.mult), R=["pvec"], W=["der_a"])
    P.add("dve", lambda e: e.tensor_scalar(der[:, NBI:NBI + 12], pvec[:, BI:BI + 12], -1.0, None, ALU.mult), R=["pvec"], W=["der_a"])
    P.add("act", lambda e: e.activation(der[:, CC:CC + 12], pvec[:, LAM:LAM + 12], AF.Exp, scale=-1.0), R=["pvec"], W=["der_c"])
    P.add("act", lambda e: e.activation(der[:, CC:CC + 12], der[:, CC:CC + 12], AF.Ln, bias=1.0), R=["der_c"], W=["der_c"])
    P.add("dve", lambda e: e.tensor_scalar(der[:, C2:C2 + 12], der[:, CC:CC + 12], -16.0, None, ALU.mult), R=["der_c"], W=["der_c2"])
    P.add("dve", lambda e: e.tensor_scalar(der[:, CC:CC + 12], der[:, CC:CC + 12], -8.0, None, ALU.mult), R=["der_c", "der_c2"], W=["der_c"])
    P.add("dve", lambda e: e.tensor_tensor(der[:, T0:T0 + 8], pvec[:, L0C:L0C + 8], pvec[:, L1C:L1C + 8], ALU.max), R=["pvec"], W=["der_t0"])
    P.add("dve", lambda e: e.tensor_tensor(der[:, T1:T1 + 8], pvec[:, L0C:L0C + 8], der[:, T0:T0 + 8], ALU.subtract), R=["pvec", "der_t0"], W=["der_t1"])
    P.add("dve", lambda e: e.tensor_tensor(der[:, T2:T2 + 8], pvec[:, L1C:L1C + 8], der[:, T0:T0 + 8], ALU.subtract), R=["pvec", "der_t0"], W=["der_t2"])
    P.add("act", lambda e: e.activation(der[:, T1:T1 + 8], der[:, T1:T1 + 8], AF.Exp), R=["der_t1"], W=["der_t1"])
    P.add("act", lambda e: e.activation(der[:, T2:T2 + 8], der[:, T2:T2 + 8], AF.Exp), R=["der_t2"], W=["der_t2"])
    P.add("dve", lambda e: e.tensor_tensor(der[:, T3:T3 + 8], der[:, T1:T1 + 8], der[:, T2:T2 + 8], ALU.add), R=["der_t1", "der_t2"], W=["der_t3"])
    P.add("dve", lambda e: e.reciprocal(der[:, T3:T3 + 8], der[:, T3:T3 + 8]), R=["der_t3"], W=["der_t3"])
    P.add("dve", lambda e: e.tensor_tensor(der[:, T1:T1 + 8], der[:, T1:T1 + 8], der[:, T3:T3 + 8], ALU.mult), R=["der_t1", "der_t3"], W=["der_t1"])
    P.add("dve", lambda e: e.tensor_tensor(der[:, T2:T2 + 8], der[:, T2:T2 + 8], der[:, T3:T3 + 8], ALU.mult), R=["der_t2", "der_t3"], W=["der_t2"])
    P.add("dve", lambda e: e.tensor_tensor(der[:, T3:T3 + 8], der[:, T1:T1 + 8], der[:, T2:T2 + 8], ALU.add), R=["der_t1", "der_t2", "der_t3"], W=["der_t3"])
    P.add("dve", lambda e: e.tensor_tensor(der[:, LB:LB + 8], der[:, T3:T3 + 8], der[:, T1:T1 + 8], ALU.subtract), R=["der_t1", "der_t3"], W=["der_lb"])
    P.add("act", lambda e: e.activation(der[:, LNOM:LNOM + 8], der[:, LB:LB + 8], AF.Ln, scale=-1.0, bias=1.0), R=["der_lb"], W=["der_lnom"])
    P.barrier()

    slot_rr = [0]

    def slot():
        i = slot_rr[0] % NSLOT
        slot_rr[0] += 1
        return i

    ps_rr = {}

    def psum(tag, banks):
        i = ps_rr.get(tag, 0)
        ps_rr[tag] = i + 1
        return banks[i % len(banks)]

    def act(out, in_, func, R, Wr, **kw):
        P.add("act", lambda e: e.activation(out, in_, func, **kw), R=R, W=Wr)

    NB = 2

    def norm_sq(c, N):
        sq = xsq[:, c % 2, :N]
        P.add("act", lambda e: e.activation(sq, xt[:, c, :N], AF.Square), R=["xt%d" % c], W=["xsq%d" % (c % 2)])

    def norm_mm(c, N):
        sq = xsq[:, c % 2, :N]
        P.add("pe", lambda e: e.matmul(ps[NB][:, :N], ones[:], sq, start=(c == 0), stop=(c == 7)), R=["ones", "xsq%d" % (c % 2)], W=[psb[NB]])

    def norm_step(c, N):
        norm_sq(c, N)
        norm_mm(c, N)

    def norm_finish(N, gcol, out_final, fs=None, after=None):
        P.add("dve", lambda e: e.tensor_scalar(stat[:, 0, :N], ps[NB][:, :N], 1.0 / D, EPS, ALU.mult, ALU.add), R=[psb[NB]], W=["stat0"])
        act(stat[:, 0, :N], stat[:, 0, :N], AF.Ln, ["stat0"], ["stat0"])
        act(stat[:, 1, :N], stat[:, 0, :N], AF.Exp, ["stat0"], ["stat1"], scale=-0.5)
        for c in range(8):
            if out_final:
                P.add("dve", lambda e: e.scalar_tensor_tensor(slots[:, fs[c], :N], xt[:, c, :N], pvec[:, gcol + c:gcol + c + 1], stat[:, 1, :N], ALU.mult, ALU.mult),
                      R=["xt%d" % c, "stat1", "pvec"], W=["slot%d" % fs[c]])
            else:
                P.add("dve", lambda e: e.scalar_tensor_tensor(xn[:, c, :N], xt[:, c, :N], pvec[:, gcol + c:gcol + c + 1], stat[:, 1, :N], ALU.mult, ALU.mult),
                      R=["xt%d" % c, "stat1", "pvec"], W=["xn%d" % c])
            if after is not None:
                after(c)

    def norm(N, gcol, out_final, fs=None):
        for c in range(8):
            norm_step(c, N)
        norm_finish(N, gcol, out_final, fs)

    XN = ["xn%d" % c for c in range(8)]

    class WStream:
        def __init__(self):
            self.q = []
            self.idx = {}
            self.nxt = 0
            self.rings = {}

        def ring(self, name, bufnames):
            self.rings[name] = dict(names=bufnames, owner=[None] * len(bufnames), rr=0)

        def plan(self, key, ring, fn, ndma):
            it = dict(key=key, ring=ring, fn=fn, ndma=ndma, slot=None, issued=False)
            self.q.append(it)
            self.idx[key] = it

        def pump(self):
            while self.nxt < len(self.q):
                it = self.q[self.nxt]
                r = self.rings[it["ring"]]
                i = r["rr"] % len(r["names"])
                if r["owner"][i] is not None:
                    break
                r["owner"][i] = it["key"]
                r["rr"] += 1
                it["slot"] = i
                bname = r["names"][i]
                fn = it["fn"]
                P.add("pool", lambda e: fn(e, i), W=[bname], dma=B(bname), ndma=it["ndma"], bar=False)
                it["issued"] = True
                self.nxt += 1

        def get(self, key):
            self.pump()
            it = self.idx[key]
            assert it["issued"], ("weight piece not loadable yet", key)
            return it["slot"], self.rings[it["ring"]]["names"][it["slot"]]

        def release(self, key):
            it = self.idx[key]
            r = self.rings[it["ring"]]
            assert r["owner"][it["slot"]] == key
            r["owner"][it["slot"]] = None
            self.pump()

    WS = WStream()
    WS.ring("A", ["WA%d" % i for i in range(4)])
    WS.ring("O", ["WO%d" % i for i in range(5)])
    WS.ring("D", ["WD%d" % i for i in range(2)])
    WS.ring("B", ["WB%d" % i for i in range(2)])
    WS.ring("V", ["WV0", "WV1"])

    def layer0(ti, N, nseq, T, sample, last_prompt):
        save_ptr = A.ptr
        xbb = A([128, 12, nseq, 3 + T], BF16)
        xcb = A([128, 2, 3, N], BF16)
        xcf = A([128, 2, 3, N], F32)
        if sample:
            h0s = A([128, 12, NSEQ], F32)
            cv0s = A([128, 12, NSEQ, 3], F32)
            tmp0 = A([128, NSEQ], F32)
            ld(h0s[:], h0_d, "h0s")
            ld(cv0s[:], cv0_d, "cv0s")
        wa = {}
        st = {}

        def p1(g, j, split=False):
            wi, wname = WS.get(("X", ti, g))
            ch = 3 * g + j
            xs = g % 2
            bk = psum("xb", [0, 1])
            P.add("pe", lambda e: [e.matmul(ps[bk][:, :N], WA[:, wi, c, j * 128:(j + 1) * 128], xn[:, c, :N], start=(c == 0), stop=(c == 7)) for c in range(8)],
                  R=[wname] + XN, W=[psb[bk]])
            xbname = "xbb%d" % ch
            src3 = ps[bk][:, :N].rearrange("p (s t) -> p s t", s=nseq)
            if sample:
                P.add("dve", lambda e: e.tensor_copy(xbb[:, ch, :, 0:3], cv0s[:, ch, :, :]), R=["cv0s"], W=[xbname])
            else:
                P.add("dve", lambda e: e.tensor_copy(xbb[:, ch, 0, 0:3], halo[:, ch, 0:3]), R=["halo%d" % ch], W=[xbname])
            P.add("dve", lambda e: e.tensor_copy(xbb[:, ch, :, 3:3 + T], src3), R=[psb[bk]], W=[xbname])
            if sample:
                P.add("dve", lambda e: e.tensor_copy(cstage[:, ch, 1:17, :], src3[:, :, 1:4]), R=[psb[bk]], W=["cstage"])
            else:
                P.add("pool", lambda e: e.tensor_copy(halo[:, ch, 0:3], xbb[:, ch, 0, T:T + 3]), R=[xbname], W=["halo%d" % ch])
                if last_prompt:
                    P.add("dve", lambda e: e.tensor_copy(cstage[:, ch, 0, :], ps[bk][:, N - 3:N]), R=[psb[bk]], W=["cstage"])
            if j == 2:
                WS.release(("X", ti, g))
            if not split:
                p1b(g, j)

        def p1b(g, j):
            di, dname = WS.get(("D", ti, g))
            ch = 3 * g + j
            xs = g % 2
            xbname = "xbb%d" % ch
            cb = psum("xc", [2, 3, 4])
            dst3 = ps[cb][:, :N].rearrange("p (s t) -> p s t", s=nseq)
            P.add("pe", lambda e: [e.matmul(dst3, WD[:, di, j * 4 + k, :], xbb[:, ch, :, k:k + T], start=(k == 0), stop=(k == 3)) for k in range(4)],
                  R=[xbname, dname], W=[psb[cb]])
            P.add("dve", lambda e: e.tensor_scalar(xcf[:, xs, j, :], ps[cb][:, :N], pvec[:, CB + ch:CB + ch + 1], None, ALU.add),
                  R=[psb[cb], "pvec"], W=["xcf%d_%d" % (xs, j)])
            P.add("dve", lambda e: e.tensor_copy(xcb[:, xs, j, :], xcf[:, xs, j, :]), R=["xcf%d_%d" % (xs, j)], W=["xcb%d_%d" % (xs, j)])
            if j == 2:
                WS.release(("D", ti, g))

        def gm(n):
            g, j = divmod(n, 3)
            wi, wname = WS.get(("G", ti, g))
            bi, bdname = WS.get(("B", ti, g))
            xs = g % 2
            XC = ["xcb%d_%d" % (xs, jj) for jj in range(3)]
            rb, ib, gb = 5, 6, 7
            P.add("pe", lambda e: [e.matmul(ps[rb][:, :N], WBD[:, bi, 0, jp, j * 128:(j + 1) * 128], xcb[:, xs, jp, :], start=(jp == 0), stop=(jp == 2)) for jp in range(3)],
                  R=XC + [bdname], W=[psb[rb]])
            P.add("pe", lambda e: [e.matmul(ps[ib][:, :N], WBD[:, bi, 1, jp, j * 128:(j + 1) * 128], xcb[:, xs, jp, :], start=(jp == 0), stop=(jp == 2)) for jp in range(3)],
                  R=XC + [bdname], W=[psb[ib]])
            P.add("pe", lambda e: [e.matmul(ps[gb][:, :N], WA[:, wi, c, j * 128:(j + 1) * 128], xn[:, c, :N], start=(c == 0), stop=(c == 7)) for c in range(8)],
                  R=[wname] + XN, W=[psb[gb]])
            if j == 2:
                WS.release(("G", ti, g))
                WS.release(("B", ti, g))

        def stage_s(n):
            ch = n
            rb, ib, gb = 5, 6, 7
            s0 = 5 * (n % 3)
            sl = [s0, s0 + 1, s0 + 2, s0 + 3, s0 + 4]
            nm = ["slot%d" % s_ for s_ in sl]
            vv = [slots[:, s_, :N] for s_ in sl]
            st[n] = (nm, vv)
            (n1, n2, n3, n4, n5), (v1, v2, v3, v4, v5) = nm, vv
            act(v1, ps[rb][:, :N], AF.Sigmoid, [psb[rb], "pvec"], [n1], bias=pvec[:, BR + ch:BR + ch + 1])
            act(v4, ps[ib][:, :N], AF.Sigmoid, [psb[ib], "pvec"], [n4], bias=pvec[:, BI + ch:BI + ch + 1])
            act(v5, ps[gb][:, :N], AF.Sigmoid, [psb[gb]], [n5])
            P.add("dve", lambda e: e.tensor_tensor(v5, v5, ps[gb][:, :N], ALU.mult), R=[n5, psb[gb]], W=[n5])

        def stage_e(n):
            ch = n
            (n1, n2, n3, n4, n5), (v1, v2, v3, v4, v5) = st[n]
            act(v2, v1, AF.Exp, [n1, "der_c"], [n2], scale=der[:, CC + ch:CC + ch + 1])
            act(v3, v1, AF.Exp, [n1, "der_c2"], [n3], scale=der[:, C2 + ch:C2 + ch + 1])
            act(v3, v3, AF.Ln, [n3], [n3], scale=-1.0, bias=1.0)
            act(v3, v3, AF.Exp, [n3], [n3], scale=0.5)
            P.add("dve", lambda e: e.tensor_tensor(v3, v3, v4, ALU.mult), R=[n3, n4], W=[n3])

        def stage_b(n):
            ch = n
            g, j = divmod(n, 3)
            xs = g % 2
            (n1, n2, n3, n4, n5), (v1, v2, v3, v4, v5) = st[n]
            P.add("dve", lambda e: e.tensor_tensor(v3, v3, xcf[:, xs, j, :], ALU.mult), R=[n3, "xcf%d_%d" % (xs, j)], W=[n3])
            if sample:
                a3 = v2.rearrange("p (s t) -> p s t", s=nseq)
                b3 = v3.rearrange("p (s t) -> p s t", s=nseq)
                P.add("dve", lambda e: e.tensor_tensor(tmp0[:, :], a3[:, :, 0], h0s[:, ch, :], ALU.mult), R=[n2, "h0s"], W=["tmp0"])
                P.add("dve", lambda e: e.tensor_tensor(b3[:, :, 0], b3[:, :, 0], tmp0[:, :], ALU.add), R=[n3, "tmp0"], W=[n3])
                P.add("dve", lambda e: e.memset(a3[:, :, 0], 0.0), R=["tmp0"], W=[n2])
                P.add("dve", lambda e: e.tensor_tensor_scan(v1, v2, v3, 0.0, ALU.mult, ALU.add), R=[n2, n3], W=[n1])
                h3 = v1.rearrange("p (s t) -> p s t", s=nseq)
                P.add("dve", lambda e: e.tensor_copy(hstage[:, ch, 1:17], h3[:, :, T - 1]), R=[n1], W=["hstage"])
            else:
                P.add("dve", lambda e: e.tensor_tensor_scan(v1, v2, v3, hcar[:, ch:ch + 1], ALU.mult, ALU.add), R=[n2, n3, "hcar%d" % ch], W=[n1])
                P.add("pool", lambda e: e.tensor_copy(hcar[:, ch:ch + 1], v1[:, N - 1:N]), R=[n1], W=["hcar%d" % ch])
                if last_prompt:
                    P.add("pool", lambda e: e.tensor_copy(hstage[:, ch, 0:1], v1[:, N - 1:N]), R=[n1], W=["hstage"])
            P.add("pool" if n < 10 else "dve", lambda e: e.tensor_tensor(ybuf[:, ch, :N], v1, v5, ALU.mult), R=[n1, n5], W=["y%d" % ch])

        for j in range(3):
            p1(0, j, split=True)
        for j in range(3):
            p1b(0, j)
        for g in range(4):
            for j in range(3):
                n = 3 * g + j
                gm(n)
                if g + 1 < 4:
                    p1(g + 1, j)
                stage_s(n)
            for j in range(3):
                stage_e(3 * g + j)
                stage_b(3 * g + j)
        YB = ["y%d" % ch for ch in range(12)]
        for c in range(8):
            wo, woname = WS.get(("O0", ti, c))
            bk = psum("xb", [0, 1])
            P.add("pe", lambda e: [e.matmul(ps[bk][:, :N], WO[:, wo, ch, :], ybuf[:, ch, :N], start=(ch == 0), stop=(ch == 11)) for ch in range(12)],
                  R=[woname] + YB, W=[psb[bk]])
            WS.release(("O0", ti, c))
            if c > 0:
                norm_mm(c - 1, N)
            P.add("dve", lambda e: e.tensor_tensor(xt[:, c, :N], xt[:, c, :N], ps[bk][:, :N], ALU.add), R=["xt%d" % c, psb[bk]], W=["xt%d" % c])
            norm_sq(c, N)
        norm_mm(7, N)
        A.ptr = save_ptr

    def layer1(ti, N, nseq, T, sample):
        save_ptr = A.ptr
        nst = max(1, N // 128)
        SW = min(N, 128)
        C = 64 if not sample else T
        nch = N // C
        cps = SW // C
        Qd = A([128, 8, N], BF16)
        Kt = A([128, 8, N], BF16)
        Kl = A([128, 8, N], BF16)
        KlT = A([128, nst, 1024], BF16)
        V = A([128, nst, 1024], BF16)
        sgate = A([128, 8, N], BF16)
        ATb = A([128, 8, 128], BF16)
        osq = A([128, 2, 512], BF16)
        elast = A([128, 8, nch], F32)
        St = A([128, 8, 128], F32)
        if sample:
            NSB = 4
            S0f = A([128, NSB, 8, 128], F32)
            S0b = A([128, NSB, 8, 128], BF16)
            KlTm = A([128, 2, 1024], BF16)
            Sout = A([128, 2, 8, 128], F32)

            def load_s0(j):
                sb = j % NSB
                P.add("sp", lambda e: [e.dma_start(out=S0f[:, sb], in_=s0_d[:, j])], W=["S0f%d" % sb], dma=B("S0f%d" % sb))
                P.add("pool", lambda e: [e.dma_start(out=S0b[:, sb], in_=s0_d[:, j])], W=["S0b%d" % sb], dma=B("S0b%d" % sb))
            for j in range(NSB):
                load_s0(j)
        norm_finish(N, G1, False)
        cm = cmS if sample else cmP
        maskb = maskS if sample else maskP
        st = {}
        QB, FB, GB = [0, 3, 6, 7], [1, 4], [2, 5]

        def pm(h):
            wi, wname = WS.get(("H", ti, h))
            for (bk, off) in ((QB[h % 4], 0), (FB[h % 2], 128), (GB[h % 2], 256)):
                P.add("pe", lambda e: [e.matmul(ps[bk][:, :N], WA[:, wi, c, off:off + 128], xn[:, c, :N], start=(c == 0), stop=(c == 7)) for c in range(8)],
                      R=[wname] + XN, W=[psb[bk]])
            WS.release(("H", ti, h))

        def stage_a(h):
            qb, fb, gb = QB[h % 4], FB[h % 2], GB[h % 2]
            sl = [slot() for _ in range(5)]
            nm = ["slot%d" % s_ for s_ in sl]
            vv = [slots[:, s_, :N] for s_ in sl]
            st[h] = (nm, vv)
            (n1, n2, n3, n4, n5), (v1, v2, v3, v4, v5) = nm, vv
            act(v1, ps[qb][:, :N], AF.Exp, [psb[qb]], [n1], scale=-1.0)
            act(v2, ps[fb][:, :N], AF.Exp, [psb[fb]], [n2], scale=-1.0)
            act(v5, ps[gb][:, :N], AF.Exp, [psb[gb]], [n5], scale=-1.0)
            act(v1, v1, AF.Ln, [n1], [n1], bias=1.0)
            act(v3, v2, AF.Ln, [n2, "der_lb"], [n3], scale=der[:, LB + h:LB + h + 1], bias=1.0)
            act(v2, v2, AF.Ln, [n2], [n2], bias=1.0)
            act(v5, v5, AF.Ln, [n5], [n5], bias=1.0)
            act(v5, v5, AF.Exp, [n5], [n5], scale=-1.0)
            P.add("dve", lambda e: e.tensor_tensor(v3, v3, v2, ALU.subtract), R=[n2, n3], W=[n3])
            P.add("dve", lambda e: e.tensor_tensor(v2, v2, ps[fb][:, :N], ALU.add), R=[n2, psb[fb]], W=[n2])
            P.add("dve", lambda e: e.tensor_tensor_scan(v4, cm[:, :N], v3, 0.0, ALU.mult, ALU.add), R=[n3, "cst"], W=[n4])
            P.add("dve", lambda e: e.tensor_tensor(v2, v2, v4, ALU.add), R=[n2, n4], W=[n2])
            P.add("dve", lambda e: e.tensor_tensor(v1, v4, v1, ALU.subtract), R=[n1, n4], W=[n1])
            P.add("dve", lambda e: e.scalar_tensor_tensor(sgate[:, h, :], ps[gb][:, :N], pvec[:, OG + h:OG + h + 1], v5, ALU.mult, ALU.mult),
                  R=[n5, psb[gb], "pvec"], W=["sgate%d" % h])

        def stage_b(h):
            qb = QB[h % 4]
            (n1, n2, n3, n4, n5), (v1, v2, v3, v4, v5) = st[h]
            c3 = v4.rearrange("p (c t) -> p c t", t=C)
            act(elast[:, h, :], c3[:, :, C - 1], AF.Exp, [n4], ["elast%d" % h])
            act(Kt[:, h, :], v2, AF.Exp, [n2, "der_lnom"], ["Kt%d" % h], scale=-1.0, bias=der[:, LNOM + h:LNOM + h + 1])
            act(v1, v1, AF.Exp, [n1], [n1])
            P.add("dve", lambda e: e.tensor_tensor(Qd[:, h, :], v1, ps[qb][:, :N], ALU.mult), R=[n1, psb[qb]], W=["Qd%d" % h])
            if sample:
                P.add("dve", lambda e: [e.tensor_scalar(Kl[:, h, c * C:(c + 1) * C], Kt[:, h, c * C:(c + 1) * C], elast[:, h, c:c + 1], None, ALU.mult) for c in range(nch)],
                      R=["Kt%d" % h, "elast%d" % h], W=["Kl%d" % h])
            else:
                P.add("dve", lambda e: e.tensor_tensor(Kl[:, h, :].rearrange("p (c t) -> p c t", t=C), Kt[:, h, :].rearrange("p (c t) -> p c t", t=C),
                                                       elast[:, h, :].unsqueeze(2).broadcast_to([128, nch, C]), ALU.mult),
                      R=["Kt%d" % h, "elast%d" % h], W=["Kl%d" % h])

        for h in range(8):
            pm(h)
            stage_a(h)
            if h >= 1:
                stage_b(h - 1)
        stage_b(7)
        vi = 0
        for q in range(4):
            wv, wvname = WS.get(("V", ti, q))
            for s_i in range(nst):
                bk = [4, 5][vi % 2]
                P.add("pe", lambda e: [e.matmul(ps[bk][:SW, 0:256], xn[:, c, s_i * SW:(s_i + 1) * SW], WV[:, wv, c, :], start=(c == 0), stop=(c == 7)) for c in range(8)],
                      R=[wvname] + XN, W=[psb[bk]])
                if vi % 2 == 0:
                    P.add("act", lambda e: e.activation(V[:SW, s_i, q * 256:(q + 1) * 256], ps[bk][:SW, 0:256], AF.Copy), R=[psb[bk]], W=["V%d" % s_i])
                else:
                    P.add("dve", lambda e: e.tensor_copy(V[:SW, s_i, q * 256:(q + 1) * 256], ps[bk][:SW, 0:256]), R=[psb[bk]], W=["V%d" % s_i])
                vi += 1
            WS.release(("V", ti, q))
        KLN = ["Kl%d" % h for h in range(8)]
        for s_i in range(nst):
            for half in range(2):
                bk = [6, 7][vi % 2]
                pst = pstv[bk]
                P.add("pe", lambda e: [e.transpose(pst[:SW, hh * 128:(hh + 1) * 128], Kl[:, half * 4 + hh, s_i * SW:(s_i + 1) * SW], ident[:]) for hh in range(4)],
                      R=KLN + ["ident"], W=[psb[bk]])
                if vi % 2 == 0:
                    P.add("act", lambda e: e.activation(KlT[:SW, s_i, half * 512:(half + 1) * 512], pst[:SW, 0:512], AF.Copy), R=[psb[bk]], W=["KlT%d" % s_i])
                else:
                    P.add("dve", lambda e: e.tensor_copy(KlT[:SW, s_i, half * 512:(half + 1) * 512], pst[:SW, 0:512]), R=[psb[bk]], W=["KlT%d" % s_i])
                vi += 1
        ost = {}

        def opath_a(s_i):
            OB = [6, 7] if s_i % 2 == 0 else [0, 1]
            for half in range(2):
                ob = OB[half]
                nb = 4 + half
                sr, s1_ = slot(), slot()
                ost[(s_i, half)] = (sr, s1_)
                ow = ps[ob][:, :].rearrange("p (h t) -> p h t", h=4)[:, :, :SW]
                osq3 = osq[:, half, :].rearrange("p (h t) -> p h t", h=4)[:, :, :SW]
                nw = ps[nb][:, :].rearrange("p (h t) -> p h t", h=4)[:, :, :SW]
                P.add("act", lambda e: e.activation(osq3, ow, AF.Square), R=[psb[ob]], W=["osq%d" % half])
                if SW == 128:
                    P.add("pe", lambda e: e.matmul(nw, ones[:], osq3, start=True, stop=True, skip_group_check=True), R=["osq%d" % half, "ones"], W=[psb[nb]])
                else:
                    P.add("pe", lambda e: [e.matmul(ps[nb][:, hh * 128:hh * 128 + SW], ones[:], osq[:, half, hh * 128:hh * 128 + SW], start=True, stop=True, skip_group_check=True) for hh in range(4)],
                          R=["osq%d" % half, "ones"], W=[psb[nb]])

        def opath_b1(s_i):
            for half in range(2):
                nb = 4 + half
                sr, s1_ = ost[(s_i, half)]
                nr = "slot%d" % sr
                nw = ps[nb][:, :].rearrange("p (h t) -> p h t", h=4)[:, :, :SW]
                rs3 = slots[:, sr, :].rearrange("p (h t) -> p h t", h=4)[:, :, :SW]
                P.add("dve", lambda e: e.tensor_scalar(rs3, nw, 1.0 / 128.0, EPS, ALU.mult, ALU.add), R=[psb[nb]], W=[nr])
                act(rs3, rs3, AF.Ln, [nr], [nr])
                act(rs3, rs3, AF.Exp, [nr], [nr], scale=-0.5)

        def opath_b2(s_i):
            OB = [6, 7] if s_i % 2 == 0 else [0, 1]
            t0_ = s_i * SW
            for half in range(2):
                ob = OB[half]
                sr, s1_ = ost[(s_i, half)]
                nr, n1_ = "slot%d" % sr, "slot%d" % s1_
                ow = ps[ob][:, :].rearrange("p (h t) -> p h t", h=4)[:, :, :SW]
                rs3 = slots[:, sr, :].rearrange("p (h t) -> p h t", h=4)[:, :, :SW]
                t13 = slots[:, s1_, :].rearrange("p (h t) -> p h t", h=4)[:, :, :SW]
                P.add("dve", lambda e: e.tensor_tensor(t13, ow, rs3, ALU.mult), R=[psb[ob], nr], W=[n1_])
                P.add("dve", lambda e: e.tensor_tensor(ybuf[:, half * 4:half * 4 + 4, t0_:t0_ + SW], t13, sgate[:, half * 4:half * 4 + 4, t0_:t0_ + SW], ALU.mult),
                      R=[n1_] + ["sgate%d" % (half * 4 + hh) for hh in range(4)], W=["y1_%d_%d" % (half, s_i)])

        for s_i in range(nst):
            t0 = s_i * SW
            OB = [6, 7] if s_i % 2 == 0 else [0, 1]
            for half in range(2):
                atb = 4 + half
                for hh in range(4):
                    h = half * 4 + hh
                    P.add("pe", lambda e: e.matmul(ps[atb][:SW, hh * 128:hh * 128 + SW], Kt[:, h, t0:t0 + SW], Qd[:, h, t0:t0 + SW], start=True, stop=True, skip_group_check=True),
                          R=["Kt%d" % h, "Qd%d" % h], W=[psb[atb]])
                for hh in range(4):
                    h = half * 4 + hh
                    P.add("dve", lambda e: e.tensor_tensor(ATb[:SW, h, :SW], ps[atb][:SW, hh * 128:hh * 128 + SW], maskb[:SW, :SW], ALU.mult),
                          R=[psb[atb], "maskP", "maskS"], W=["AT%d" % h])
            for half in range(2):
                ob = OB[half]
                for hh in range(4):
                    h = half * 4 + hh
                    P.add("pe", lambda e: e.matmul(ps[ob][:, hh * 128:hh * 128 + SW], V[:SW, s_i, h * 128:(h + 1) * 128], ATb[:SW, h, :SW], start=(hh == 0), stop=False, skip_group_check=True),
                          R=["V%d" % s_i, "AT%d" % h], W=[psb[ob]])
            if s_i > 0:
                opath_a(s_i - 1)
            if not sample:
                for cc in range(cps):
                    ci = s_i * cps + cc
                    p0 = cc * C
                    UB = [2, 3]
                    for half in range(2):
                        hs = slice(half * 4, half * 4 + 4)
                        HN = ["%d" % (half * 4 + hh) for hh in range(4)]
                        ebc = elast[:, hs, ci:ci + 1].broadcast_to([128, 4, 128])
                        P.add("dve", lambda e: e.tensor_tensor(St[:, hs, :], S[:, hs, :], ebc, ALU.mult),
                              R=["S" + x for x in HN] + ["elast" + x for x in HN], W=["St%d" % half])
                    for half in range(2):
                        ob = OB[half]
                        for hh in range(4):
                            h = half * 4 + hh
                            P.add("pe", lambda e: e.matmul(ps[ob][:, hh * 128 + p0:hh * 128 + p0 + C], Sb[:, h, :], Qd[:, h, t0 + p0:t0 + p0 + C], start=False, stop=(cc == cps - 1), skip_group_check=True),
                                  R=["Sb%d" % h, "Qd%d" % h], W=[psb[ob]])
                    for half in range(2):
                        ub = UB[half]
                        for hh in range(4):
                            h = half * 4 + hh
                            P.add("pe", lambda e: e.matmul(ps[ub][:, hh * 128:(hh + 1) * 128], KlT[p0:p0 + C, s_i, h * 128:(h + 1) * 128], V[p0:p0 + C, s_i, h * 128:(h + 1) * 128], start=True, stop=True, skip_group_check=True),
                                  R=["KlT%d" % s_i, "V%d" % s_i], W=[psb[ub]])
                    for half in range(2):
                        ub = UB[half]
                        hs = slice(half * 4, half * 4 + 4)
                        HN = ["%d" % (half * 4 + hh) for hh in range(4)]
                        u3 = ps[ub][:, :].rearrange("p (h v) -> p h v", h=4)
                        P.add("dve", lambda e: e.tensor_tensor(S[:, hs, :], St[:, hs, :], u3, ALU.add),
                              R=[psb[ub], "St%d" % half], W=["S" + x for x in HN])
                        P.add("act", lambda e: e.activation(Sb[:, hs, :], S[:, hs, :], AF.Copy), R=["S" + x for x in HN], W=["Sb" + x for x in HN])
                    if s_i > 0:
                        if cc == 0:
                            opath_b1(s_i - 1)
                        elif cc == cps - 1:
                            opath_b2(s_i - 1)
            else:
                for j in range(NSEQ):
                    sb = j % NSB
                    so = j % 2
                    UB = [2, 3]
                    P.add("dve", lambda e: e.tensor_scalar(KlTm[:SW, so, :], KlT[:SW, 0, :], rowmask[:SW, j:j + 1], None, ALU.mult),
                          R=["KlT0", "cst"], W=["KlTm%d" % so])
                    for half in range(2):
                        hs = slice(half * 4, half * 4 + 4)
                        ebc = elast[:, hs, j:j + 1].broadcast_to([128, 4, 128])
                        P.add("dve", lambda e: e.tensor_tensor(St[:, hs, :], S0f[:, sb, hs, :], ebc, ALU.mult),
                              R=["S0f%d" % sb] + ["elast%d" % (half * 4 + hh) for hh in range(4)], W=["St%d" % half])
                    for half in range(2):
                        ob = OB[half]
                        for hh in range(4):
                            h = half * 4 + hh
                            P.add("pe", lambda e: e.matmul(ps[ob][:, hh * 128 + j * T:hh * 128 + (j + 1) * T], S0b[:, sb, h, :], Qd[:, h, j * T:(j + 1) * T], start=False, stop=(j == NSEQ - 1), skip_group_check=True),
                                  R=["S0b%d" % sb, "Qd%d" % h], W=[psb[ob]])
                    for half in range(2):
                        ub = UB[half]
                        for hh in range(4):
                            h = half * 4 + hh
                            P.add("pe", lambda e: e.matmul(ps[ub][:, hh * 128:(hh + 1) * 128], KlTm[:SW, so, h * 128:(h + 1) * 128], V[:SW, 0, h * 128:(h + 1) * 128], start=True, stop=True, skip_group_check=True),
                                  R=["KlTm%d" % so, "V0"], W=[psb[ub]])
                    for half in range(2):
                        ub = UB[half]
                        hs = slice(half * 4, half * 4 + 4)
                        u3 = ps[ub][:, :].rearrange("p (h v) -> p h v", h=4)
                        P.add("dve", lambda e: e.tensor_tensor(Sout[:, so, hs, :], St[:, hs, :], u3, ALU.add),
                              R=[psb[ub], "St%d" % half], W=["Sout%d_%d" % (so, half)])
                    P.add("sp", lambda e: [e.dma_start(out=sout_d[:, 1 + j], in_=Sout[:, so])], R=["Sout%d_0" % so, "Sout%d_1" % so], dma=B("Sout%d" % so))
                    if j + NSB < NSEQ:
                        load_s0(j + NSB)
        opath_a(nst - 1)
        opath_b1(nst - 1)
        opath_b2(nst - 1)
        YB = ["y1_%d_%d" % (half, s_i) for half in range(2) for s_i in range(nst)]
        for c in range(8):
            wo, woname = WS.get(("O1", ti, c))
            bk = psum("xb", [0, 1])
            P.add("pe", lambda e: [e.matmul(ps[bk][:, :N], WO[:, wo, hc, :], ybuf[:, hc, :N], start=(hc == 0), stop=(hc == 7)) for hc in range(8)],
                  R=[woname] + YB, W=[psb[bk]])
            WS.release(("O1", ti, c))
            if c > 0:
                norm_mm(c - 1, N)
            P.add("dve", lambda e: e.tensor_tensor(xt[:, c, :N], xt[:, c, :N], ps[bk][:, :N], ALU.add), R=["xt%d" % c, psb[bk]], W=["xt%d" % c])
            norm_sq(c, N)
        norm_mm(7, N)
        A.ptr = save_ptr

    pstv = {6: ps[6].bitcast(BF16), 7: ps[7].bitcast(BF16)}

    tiles = [(i * NT, NT, 1, NT, False) for i in range(SEQ // NT)] + [(SEQ, NS, NSEQ, 4, True)]
    XT = ["xt%d" % c for c in range(8)]
    for ti in range(len(tiles)):
        for g in range(4):
            WS.plan(("X", ti, g), "A", (lambda g: lambda e, i: [e.dma_start(out=WA[:, i, :, :], in_=w_in0_v[:, :, 384 * g:384 * g + 384])])(g), 1)
            WS.plan(("D", ti, g), "D", (lambda g: lambda e, i: [e.dma_start(out=WD[:, i], in_=diagw_d[:, 12 * g:12 * g + 12, :])])(g), 1)
            WS.plan(("G", ti, g), "A", (lambda g: lambda e, i: [e.dma_start(out=WA[:, i, :, :], in_=w_in0_v[:, :, W + 384 * g:W + 384 * g + 384])])(g), 1)
            WS.plan(("B", ti, g), "B", (lambda g: lambda e, i: [e.dma_start(out=WBD[:, i], in_=wbd_d[:, g])])(g), 1)
        for c in range(8):
            WS.plan(("O0", ti, c), "O", (lambda c: lambda e, i: [e.dma_start(out=WO[:, i, 0:12, :], in_=w_out0_v[:, :, c * 128:(c + 1) * 128])])(c), 1)
        for h in range(8):
            WS.plan(("H", ti, h), "A", (lambda h: lambda e, i: [e.dma_start(out=WA[:, i, :, k * 128:(k + 1) * 128], in_=w_in1_v[:, :, cb_ + h * 128:cb_ + (h + 1) * 128]) for k, cb_ in enumerate((0, 1024, 3072))])(h), 3)
        for q in range(4):
            WS.plan(("V", ti, q), "V", (lambda q: lambda e, i: [e.dma_start(out=WV[:, i], in_=w_in1_v[:, :, 2048 + q * 256:2048 + (q + 1) * 256])])(q), 1)
        for c in range(8):
            WS.plan(("O1", ti, c), "O", (lambda c: lambda e, i: [e.dma_start(out=WO[:, i, 0:8, :], in_=w_out1_v[:, :, c * 128:(c + 1) * 128])])(c), 1)

    def load_x_chunk(c, n0, N):
        P.add("sp", lambda e: [e.dma_start(out=xt[:, c, :N], in_=xin_v[:, c, n0:n0 + N])], W=["xt%d" % c], dma=B("xtld%d" % c), bar=False)

    for c in range(8):
        load_x_chunk(c, tiles[0][0], tiles[0][1])
    norm(tiles[0][1], G0, False)
    for ti, (n0, N, nseq, T, sample) in enumerate(tiles):
        layer0(ti, N, nseq, T, sample, (not sample) and n0 + N == SEQ)
        P.barrier()
        layer1(ti, N, nseq, T, sample)
        fs = [slot() for _ in range(8)]
        nxt = tiles[ti + 1] if ti + 1 < len(tiles) else None
        norm_finish(N, GF, True, fs, after=(lambda c: load_x_chunk(c, nxt[0], nxt[1])) if nxt else None)
        P.add("sp", lambda e: [e.dma_start(out=yout_v[:, c, n0:n0 + N], in_=slots[:, fs[c], :N]) for c in range(8)], R=["slot%d" % s_ for s_ in fs], dma=B("xtst"), ndma=8, bar=False)
        if n0 + N == SEQ:
            P.add("sp", lambda e: [e.dma_start(out=sout_d[:, 0], in_=S[:])], R=["S%d" % h for h in range(8)], dma=B("Sst"))
        if nxt:
            norm(nxt[1], G0, False)
        P.barrier()
    P.add("sp", lambda e: [e.dma_start(out=hout_d, in_=hstage[:])], R=["hstage"], dma=B("hst"))
    P.add("sp", lambda e: [e.dma_start(out=cvout_d, in_=cstage[:])], R=["cstage"], dma=B("cst_out"))
    P.barrier(all_dma=True)
    return nc, P


_CACHE = {}


def _consts():
    c = np.zeros((128, 976), np.float32)
    c[:, 0:128] = np.eye(128, dtype=np.float32)
    s = np.arange(128)[:, None]
    t = np.arange(128)[None, :]
    c[:, 128:256] = ((s // 64 == t // 64) & (s <= t)).astype(np.float32)
    c[:, 256:384] = ((s // 4 == t // 4) & (s <= t)).astype(np.float32)
    cm = np.ones(512, np.float32)
    cm[::64] = 0.0
    c[:, 384:896] = cm[None, :]
    cs = np.ones(64, np.float32)
    cs[::4] = 0.0
    c[:, 896:960] = cs[None, :]
    p = np.arange(128)[:, None]
    j = np.arange(16)[None, :]
    c[:, 960:976] = ((p // 4 == j) & (p < 64)).astype(np.float32)
    return c


def _pc(v, n):
    return np.ascontiguousarray(np.asarray(v, np.float32).reshape(n, 128).T)


def kernel(x_prompt, x_sample, state_lru_h, state_lru_conv, state_hgrn, norm_gain, a_w_in, a_conv_w,
           a_conv_b, a_w_r, a_b_r, a_w_i, a_b_i, a_lambda, a_w_out, b_w_in, b_lb_logits, b_o_gain,
           b_w_out, final_gain):
    f = np.float32
    if "nc" not in _CACHE:
        nc, P = build_nc()
        _assign_and_emit(P)
        _CACHE["nc"] = nc
    nc = _CACHE["nc"]
    pvec = np.zeros((128, 96), f)
    pvec[:, 0:8] = _pc(norm_gain[0], 8)
    pvec[:, 8:16] = _pc(norm_gain[1], 8)
    pvec[:, 16:24] = _pc(final_gain, 8)
    pvec[:, 24:36] = _pc(a_conv_b[0], 12)
    pvec[:, 36:48] = _pc(a_b_r[0], 12)
    pvec[:, 48:60] = _pc(a_b_i[0], 12)
    pvec[:, 60:72] = _pc(a_lambda[0], 12)
    pvec[:, 72:80] = _pc(b_lb_logits[0], 8)
    pvec[:, 80:88] = _pc(b_lb_logits[1], 8)
    pvec[:, 88:96] = _pc(b_o_gain[0], 8)
    diagw = np.zeros((128, 48, 128), f)
    cw = np.asarray(a_conv_w[0], f)
    idx = np.arange(128)
    for ch in range(12):
        for k in range(4):
            diagw[idx, ch * 4 + k, idx] = cw[k, ch * 128:(ch + 1) * 128]
    wbd = np.zeros((128, 4, 2, 3, 384), f)
    for gi, wg in enumerate((np.asarray(a_w_r[0], f), np.asarray(a_w_i[0], f))):
        for g in range(4):
            full = np.zeros((384, 384), f)
            for b in range(4):
                full[96 * b:96 * b + 96, 96 * b:96 * b + 96] = wg[4 * g + b]
            wbd[:, g, gi] = full.reshape(3, 128, 384).transpose(1, 0, 2)
    cst = _consts()
    w_in0 = np.ascontiguousarray(a_w_in[0], f)
    w_out0 = np.ascontiguousarray(a_w_out[0], f)
    w_in1 = np.ascontiguousarray(b_w_in[0], f)
    w_out1 = np.ascontiguousarray(b_w_out[0], f)
    in_maps = []
    for b in range(NCORES):
        sl = slice(16 * b, 16 * b + 16)
        xs = np.asarray(x_sample[sl], f).reshape(64, D)
        xin = np.ascontiguousarray(np.concatenate([np.asarray(x_prompt[b], f), xs], axis=0).T)
        h0 = np.ascontiguousarray(np.asarray(state_lru_h[0, sl], f).reshape(16, 12, 128).transpose(2, 1, 0))
        cv0 = np.ascontiguousarray(np.asarray(state_lru_conv[0, sl], f).reshape(16, 3, 12, 128).transpose(3, 2, 0, 1))
        s0 = np.ascontiguousarray(np.asarray(state_hgrn[0, sl], f).transpose(2, 0, 1, 3))
        in_maps.append({"xin": xin, "w_in0": w_in0, "w_out0": w_out0, "w_in1": w_in1, "w_out1": w_out1,
                        "diagw": diagw, "wbd": wbd, "pvec": pvec, "cst": cst, "h0": h0, "cv0": cv0, "s0": s0})
    res = run_bass_kernel_spmd(nc, in_maps, core_ids=list(range(NCORES)))
    y_prompt = np.zeros((8, SEQ, D), f)
    y_sample = np.zeros((128, 4, D), f)
    hp = np.zeros((1, 8, W), f)
    bp = np.zeros((1, 8, 3, W), f)
    Sp = np.zeros((1, 8, 8, 128, 128), f)
    hs = np.zeros((1, 128, W), f)
    bs = np.zeros((1, 128, 3, W), f)
    Ss = np.zeros((1, 128, 8, 128, 128), f)
    for b in range(NCORES):
        r = res.results[b]
        sl = slice(16 * b, 16 * b + 16)
        yT = r["yout"]
        y_prompt[b] = yT[:, :SEQ].T
        y_sample[sl] = yT[:, SEQ:].T.reshape(16, 4, D)
        ho = r["hout"]
        hp[0, b] = ho[:, :, 0].T.reshape(W)
        hs[0, sl] = ho[:, :, 1:].transpose(2, 1, 0).reshape(16, W)
        co = r["cvout"]
        bp[0, b] = co[:, :, 0, :].transpose(2, 1, 0).reshape(3, W)
        bs[0, sl] = co[:, :, 1:, :].transpose(2, 3, 1, 0).reshape(16, 3, W)
        so = r["sout"]
        Sp[0, b] = so[:, 0].transpose(1, 0, 2)
        Ss[0, sl] = so[:, 1:].transpose(1, 2, 0, 3)
    return (y_prompt, y_sample, hp, bp, Sp, hs, bs, Ss)


def _assign_and_emit(P):
    P.emit()
```

```python
import numpy as np
import concourse.bass as bass
import concourse.mybir as mybir
from concourse.bass_utils import run_bass_kernel_spmd

F32 = mybir.dt.float32
BF16 = mybir.dt.bfloat16
AF = mybir.ActivationFunctionType
ALU = mybir.AluOpType

D = 1024
W = 1536
SEQ = 2048
NT = 512
NS = 64
NSEQ = 16
EPS = 1e-6
NCORES = 8
SB_BASE = 16512
SB_LIMIT = 16512 + 212863


class Buf:
    __slots__ = ("name", "lw", "rd", "sem", "dcount", "excl")

    def __init__(self, name):
        self.name = name
        self.excl = False
        self.lw = None
        self.rd = []
        self.sem = None
        self.dcount = 0


class Op:
    __slots__ = ("eng", "fn", "deps", "idx", "signal", "count", "dma", "buf", "reads", "writes", "ndma", "calls", "nobar")


class Rec:
    def __init__(self):
        self.calls = []

    def __getattr__(self, name):
        def f(*a, **k):
            self.calls.append((name, a, k))
            return len(self.calls) - 1
        return f


COMPUTE = ("pe", "act", "dve", "pool")
ENGS = ("pe", "act", "dve", "pool", "sp")


class Prog:
    def __init__(self, nc):
        self.nc = nc
        self.ops = {e: [] for e in ENGS}
        self.dma_since_barrier = []
        self.dma_nobar = []
        self.eng_obj = {"pe": nc.tensor, "act": nc.scalar, "dve": nc.vector, "pool": nc.gpsimd, "sp": nc.sync}
        self.bufs = {}

    def buf(self, name):
        b = self.bufs.get(name)
        if b is None:
            b = Buf(name)
            self.bufs[name] = b
        return b

    def add(self, eng, fn, R=(), W=(), dma=None, ndma=1, bar=True):
        op = Op()
        op.nobar = False
        op.eng = eng
        op.fn = fn
        rec = Rec()
        fn(rec)
        op.calls = rec.calls
        op.signal = False
        op.count = 0
        op.dma = dma is not None
        op.buf = dma
        if dma is not None:
            if dma.sem is None:
                dma.sem = self.nc.alloc_semaphore("dsem_" + dma.name)
            dma.dcount += 16 * ndma
            op.count = dma.dcount
            op.ndma = ndma
        R = [self.buf(b) if isinstance(b, str) else b for b in R]
        W = [self.buf(b) if isinstance(b, str) else b for b in W]
        op.reads = R
        op.writes = W
        deps = {}
        for b in R:
            if b.lw is not None:
                deps[id(b.lw)] = (b.lw, True)
            if b.excl:
                for r in b.rd:
                    if r.eng != eng and id(r) not in deps:
                        deps[id(r)] = (r, False)
        for b in W:
            if b.lw is not None and id(b.lw) not in deps:
                deps[id(b.lw)] = (b.lw, False)
            for r in b.rd:
                if id(r) not in deps:
                    deps[id(r)] = (r, False)
        for b in W:
            b.lw = op
            b.rd = []
        for b in R:
            if b.lw is not op:
                b.rd.append(op)
        op.deps = list(deps.values())
        op.idx = len(self.ops[eng])
        self.ops[eng].append(op)
        if op.dma:
            op.nobar = not bar
            if bar:
                self.dma_since_barrier.append(op)
            else:
                self.dma_nobar.append(op)
        return op

    def barrier(self, all_dma=False):
        if all_dma:
            self.dma_since_barrier += self.dma_nobar
            self.dma_nobar = []
        lasts = []
        for e in ENGS:
            for o in reversed(self.ops[e]):
                if o.dma and getattr(o, "nobar", False) and not all_dma:
                    continue
                lasts.append(o)
                break
        last_by_buf = {}
        for o in self.dma_since_barrier:
            last_by_buf[id(o.buf)] = o
        dmas = list(last_by_buf.values())
        self.dma_since_barrier = []
        for e in ENGS:
            op = Op()
            op.nobar = False
            op.eng = e
            op.fn = None
            op.signal = False
            op.count = 0
            op.dma = False
            op.buf = None
            op.reads = []
            op.writes = []
            op.deps = [(o, True) for o in lasts if o.eng != e or o.dma] + [(o, True) for o in dmas]
            op.idx = len(self.ops[e])
            self.ops[e].append(op)

    def emit(self):
        nc = self.nc
        def needs(op, dep, raw):
            if dep.dma:
                return True
            if dep.fn is None:
                return False
            if dep.eng != op.eng:
                return True
            if op.eng == "pe":
                return False
            return True

        for e in ENGS:
            for op in self.ops[e]:
                op.deps = [(d, r) for (d, r) in op.deps if needs(op, d, r)]
                for d, r in op.deps:
                    d.signal = True
        sems = {}
        for e in COMPUTE:
            sems[e] = nc.alloc_semaphore("sem_" + e)
            c = 0
            for op in self.ops[e]:
                if op.dma:
                    continue
                if op.signal and op.fn is not None:
                    c += 1
                    op.count = c
        for e in ENGS:
            eng = self.eng_obj[e]
            waited = {}
            for op in self.ops[e]:
                for d, r in op.deps:
                    if d.dma:
                        sem = d.buf.sem
                        val = d.count
                        if val == 0:
                            raise RuntimeError("dma dep emitted before producer: %s" % d.buf.name)
                    else:
                        sem = sems[d.eng]
                        val = d.count
                    key = id(sem)
                    if waited.get(key, 0) >= val:
                        continue
                    waited[key] = val
                    eng.wait_ge(sem, val)
                if op.fn is None:
                    continue
                res = [getattr(eng, name)(*a, **k) for (name, a, k) in op.calls]
                if op.dma:
                    assert len(res) == op.ndma, (op.buf.name, len(res), op.ndma)
                    for ins in res:
                        ins.then_inc(op.buf.sem, 16)
                elif op.signal:
                    res[-1].then_inc(sems[e], 1)


class Alloc:
    def __init__(self, nc):
        self.nc = nc
        self.ptr = SB_BASE
        self.n = 0

    def __call__(self, shape, dtype, at=None):
        esz = 2 if dtype == BF16 else 4
        nbytes = esz
        for s in shape[1:]:
            nbytes *= s
        nbytes = (nbytes + 31) // 32 * 32
        if at is None:
            at = self.ptr
            self.ptr += nbytes
        assert at + nbytes <= SB_LIMIT, ("sbuf overflow", at, nbytes)
        self.n += 1
        return self.nc.alloc_sbuf_tensor_at("t%d" % self.n, list(shape), dtype, offset=at)


def build_nc():
    nc = bass.Bass("TRN2", target_bir_lowering=False)
    NTOK = SEQ + NS
    xin = nc.dram_tensor("xin", [D, NTOK], F32, kind="ExternalInput").ap()
    w_in0 = nc.dram_tensor("w_in0", [D, 2 * W], F32, kind="ExternalInput").ap()
    w_out0 = nc.dram_tensor("w_out0", [W, D], F32, kind="ExternalInput").ap()
    w_in1 = nc.dram_tensor("w_in1", [D, 4 * D], F32, kind="ExternalInput").ap()
    w_out1 = nc.dram_tensor("w_out1", [D, D], F32, kind="ExternalInput").ap()
    diagw_d = nc.dram_tensor("diagw", [128, 48, 128], F32, kind="ExternalInput").ap()
    wbd_d = nc.dram_tensor("wbd", [128, 4, 2, 3, 384], F32, kind="ExternalInput").ap()
    pvec_d = nc.dram_tensor("pvec", [128, 96], F32, kind="ExternalInput").ap()
    cst_d = nc.dram_tensor("cst", [128, 976], F32, kind="ExternalInput").ap()
    h0_d = nc.dram_tensor("h0", [128, 12, NSEQ], F32, kind="ExternalInput").ap()
    cv0_d = nc.dram_tensor("cv0", [128, 12, NSEQ, 3], F32, kind="ExternalInput").ap()
    s0_d = nc.dram_tensor("s0", [128, NSEQ, 8, 128], F32, kind="ExternalInput").ap()
    yout = nc.dram_tensor("yout", [D, NTOK], F32, kind="ExternalOutput").ap()
    hout_d = nc.dram_tensor("hout", [128, 12, 17], F32, kind="ExternalOutput").ap()
    cvout_d = nc.dram_tensor("cvout", [128, 12, 17, 3], F32, kind="ExternalOutput").ap()
    sout_d = nc.dram_tensor("sout", [128, 17, 8, 128], F32, kind="ExternalOutput").ap()

    xin_v = xin.rearrange("(c p) n -> p c n", p=128)
    yout_v = yout.rearrange("(c p) n -> p c n", p=128)
    w_in0_v = w_in0.rearrange("(c p) n -> p c n", p=128)
    w_out0_v = w_out0.rearrange("(c p) n -> p c n", p=128)
    w_in1_v = w_in1.rearrange("(c p) n -> p c n", p=128)
    w_out1_v = w_out1.rearrange("(c p) n -> p c n", p=128)

    P = Prog(nc)
    A = Alloc(nc)
    B = P.buf

    xt = A([128, 8, NT], F32)
    xn = A([128, 8, NT], BF16)
    xsq = A([128, 2, NT], BF16)
    stat = A([128, 2, NT], F32)
    NSLOT = 15
    slots = A([128, NSLOT, NT], F32)
    ybuf = A([128, 12, NT], BF16)
    WA = A([128, 4, 8, 384], BF16)
    WV = A([128, 2, 8, 256], BF16)
    WO = A([128, 5, 12, 128], BF16)
    ident = A([128, 128], BF16)
    ones = A([128, 128], BF16)
    maskP = A([128, 128], BF16)
    maskS = A([128, 128], BF16)
    cst = A([128, 976], F32)
    pvec = A([128, 96], F32)
    der = A([128, 96], F32)
    halo = A([128, 12, 4], BF16)
    hcar = A([128, 12], F32)
    WD = A([128, 2, 12, 128], BF16)
    WBD = A([128, 2, 2, 3, 384], BF16)
    S = A([128, 8, 128], F32)
    Sb = A([128, 8, 128], BF16)
    hstage = A([128, 12, 17], F32)
    cstage = A([128, 12, 17, 3], F32)
    ARENA0 = A.ptr

    ps = [nc.alloc_psum_tensor("ps%d" % i, [128, 512], F32) for i in range(8)]
    psb = [B("ps%d" % i) for i in range(8)]
    for b_ in psb:
        b_.excl = True

    cmP = cst[:, 384:896]
    cmS = cst[:, 896:960]
    rowmask = cst[:, 960:976]
    G0, G1, GF, CB, BR, BI, LAM, L0C, L1C, OG = 0, 8, 16, 24, 36, 48, 60, 72, 80, 88
    NBR, NBI, CC, C2, LB, LNOM, T0, T1, T2, T3 = 0, 12, 24, 36, 48, 56, 64, 72, 80, 88

    def ld(dst, src, bname, eng="sp"):
        P.add(eng, lambda e: [e.dma_start(out=dst, in_=src)], W=[bname], dma=B(bname))

    ld(cst[:], cst_d, "cst")
    ld(pvec[:], pvec_d, "pvec")
    P.add("dve", lambda e: e.tensor_copy(ident[:], cst[:, 0:128]), R=["cst"], W=["ident"])
    P.add("dve", lambda e: e.tensor_copy(maskP[:], cst[:, 128:256]), R=["cst"], W=["maskP"])
    P.add("dve", lambda e: e.tensor_copy(maskS[:], cst[:, 256:384]), R=["cst"], W=["maskS"])
    P.add("dve", lambda e: e.memset(ones[:], 1.0), W=["ones"])
    P.add("dve", lambda e: e.memset(hcar[:], 0.0), W=["hcar"])
    P.add("dve", lambda e: e.memset(halo[:], 0.0), W=["halo"])
    P.add("dve", lambda e: e.memset(S[:], 0.0), W=["S"])
    P.add("dve", lambda e: e.memset(Sb[:], 0.0), W=["Sb"])
    P.add("dve", lambda e: e.tensor_scalar(der[:, NBR:NBR + 12], pvec[:, BR:BR + 12], -1.0, None, ALU.mult), R=["pvec"], W=["der_a"])
    P.add("dve", lambda e: e.tensor_scalar(der[:, NBI:NBI + 12], pvec[:, BI:BI + 12], -1.0, None, ALU.mult), R=["pvec"], W=["der_a"])
    P.add("act", lambda e: e.activation(der[:, CC:CC + 12], pvec[:, LAM:LAM + 12], AF.Exp, scale=-1.0), R=["pvec"], W=["der_c"])
    P.add("act", lambda e: e.activation(der[:, CC:CC + 12], der[:, CC:CC + 12], AF.Ln, bias=1.0), R=["der_c"], W=["der_c"])
    P.add("dve", lambda e: e.tensor_scalar(der[:, C2:C2 + 12], der[:, CC:CC + 12], -16.0, None, ALU.mult), R=["der_c"], W=["der_c2"])
    P.add("dve", lambda e: e.tensor_scalar(der[:, CC:CC + 12], der[:, CC:CC + 12], -8.0, None, ALU.mult), R=["der_c", "der_c2"], W=["der_c"])
    P.add("dve", lambda e: e.tensor_tensor(der[:, T0:T0 + 8], pvec[:, L0C:L0C + 8], pvec[:, L1C:L1C + 8], ALU.max), R=["pvec"], W=["der_t0"])
    P.add("dve", lambda e: e.tensor_tensor(der[:, T1:T1 + 8], pvec[:, L0C:L0C + 8], der[:, T0:T0 + 8], ALU.subtract), R=["pvec", "der_t0"], W=["der_t1"])
    P.add("dve", lambda e: e.tensor_tensor(der[:, T2:T2 + 8], pvec[:, L1C:L1C + 8], der[:, T0:T0 + 8], ALU.subtract), R=["pvec", "der_t0"], W=["der_t2"])
    P.add("act", lambda e: e.activation(der[:, T1:T1 + 8], der[:, T1:T1 + 8], AF.Exp), R=["der_t1"], W=["der_t1"])
    P.add("act", lambda e: e.activation(der[:, T2:T2 + 8], der[:, T2:T2 + 8], AF.Exp), R=["der_t2"], W=["der_t2"])
    P.add("dve", lambda e: e.tensor_tensor(der[:, T3:T3 + 8], der[:, T1:T1 + 8], der[:, T2:T2 + 8], ALU.add), R=["der_t1", "der_t2"], W=["der_t3"])
    P.add("dve", lambda e: e.reciprocal(der[:, T3:T3 + 8], der[:, T3:T3 + 8]), R=["der_t3"], W=["der_t3"])
    P.add("dve", lambda e: e.tensor_tensor(der[:, T1:T1 + 8], der[:, T1:T1 + 8], der[:, T3:T3 + 8], ALU.mult), R=["der_t1", "der_t3"], W=["der_t1"])
    P.add("dve", lambda e: e.tensor_tensor(der[:, T2:T2 + 8], der[:, T2:T2 + 8], der[:, T3:T3 + 8], ALU.mult), R=["der_t2", "der_t3"], W=["der_t2"])
    P.add("dve", lambda e: e.tensor_tensor(der[:, T3:T3 + 8], der[:, T1:T1 + 8], der[:, T2:T2 + 8], ALU.add), R=["der_t1", "der_t2", "der_t3"], W=["der_t3"])
    P.add("dve", lambda e: e.tensor_tensor(der[:, LB:LB + 8], der[:, T3:T3 + 8], der[:, T1:T1 + 8], ALU.subtract), R=["der_t1", "der_t3"], W=["der_lb"])
    P.add("act", lambda e: e.activation(der[:, LNOM:LNOM + 8], der[:, LB:LB + 8], AF.Ln, scale=-1.0, bias=1.0), R=["der_lb"], W=["der_lnom"])

    slot_rr = [0]

    def slot():
        i = slot_rr[0] % NSLOT
        slot_rr[0] += 1
        return i

    ps_rr = {}

    def psum(tag, banks):
        i = ps_rr.get(tag, 0)
        ps_rr[tag] = i + 1
        return banks[i % len(banks)]

    def act(out, in_, func, R, Wr, **kw):
        P.add("act", lambda e: e.activation(out, in_, func, **kw), R=R, W=Wr)

    NB = 2

    def norm_sq(c, N):
        sq = xsq[:, c % 2, :N]
        P.add("act", lambda e: e.activation(sq, xt[:, c, :N], AF.Square), R=["xt%d" % c], W=["xsq%d" % (c % 2)])

    def norm_mm(c, N):
        sq = xsq[:, c % 2, :N]
        P.add("pe", lambda e: e.matmul(ps[NB][:, :N], ones[:], sq, start=(c == 0), stop=(c == 7)), R=["ones", "xsq%d" % (c % 2)], W=[psb[NB]])

    def norm_step(c, N):
        norm_sq(c, N)
        norm_mm(c, N)

    def norm_finish(N, gcol, out_final, fs=None, after=None):
        P.add("dve", lambda e: e.tensor_scalar(stat[:, 0, :N], ps[NB][:, :N], 1.0 / D, EPS, ALU.mult, ALU.add), R=[psb[NB]], W=["stat0"])
        act(stat[:, 0, :N], stat[:, 0, :N], AF.Ln, ["stat0"], ["stat0"])
        act(stat[:, 1, :N], stat[:, 0, :N], AF.Exp, ["stat0"], ["stat1"], scale=-0.5)
        for c in range(8):
            if out_final:
                P.add("dve", lambda e: e.scalar_tensor_tensor(slots[:, fs[c], :N], xt[:, c, :N], pvec[:, gcol + c:gcol + c + 1], stat[:, 1, :N], ALU.mult, ALU.mult),
                      R=["xt%d" % c, "stat1", "pvec"], W=["slot%d" % fs[c]])
            else:
                P.add("dve", lambda e: e.scalar_tensor_tensor(xn[:, c, :N], xt[:, c, :N], pvec[:, gcol + c:gcol + c + 1], stat[:, 1, :N], ALU.mult, ALU.mult),
                      R=["xt%d" % c, "stat1", "pvec"], W=["xn%d" % c])
            if after is not None:
                after(c)

    def norm(N, gcol, out_final, fs=None):
        for c in range(8):
            norm_step(c, N)
        norm_finish(N, gcol, out_final, fs)

    XN = ["xn%d" % c for c in range(8)]

    class WStream:
        def __init__(self):
            self.q = []
            self.idx = {}
            self.nxt = 0
            self.rings = {}

        def ring(self, name, bufnames):
            self.rings[name] = dict(names=bufnames, owner=[None] * len(bufnames), rr=0)

        def plan(self, key, ring, fn, ndma):
            it = dict(key=key, ring=ring, fn=fn, ndma=ndma, slot=None, issued=False)
            self.q.append(it)
            self.idx[key] = it

        def pump(self):
            while self.nxt < len(self.q):
                it = self.q[self.nxt]
                r = self.rings[it["ring"]]
                i = r["rr"] % len(r["names"])
                if r["owner"][i] is not None:
                    break
                r["owner"][i] = it["key"]
                r["rr"] += 1
                it["slot"] = i
                bname = r["names"][i]
                fn = it["fn"]
                P.add("pool", lambda e: fn(e, i), W=[bname], dma=B(bname), ndma=it["ndma"], bar=False)
                it["issued"] = True
                self.nxt += 1

        def get(self, key):
            self.pump()
            it = self.idx[key]
            assert it["issued"], ("weight piece not loadable yet", key)
            return it["slot"], self.rings[it["ring"]]["names"][it["slot"]]

        def release(self, key):
            it = self.idx[key]
            r = self.rings[it["ring"]]
            assert r["owner"][it["slot"]] == key
            r["owner"][it["slot"]] = None
            self.pump()

    WS = WStream()
    WS.ring("A", ["WA%d" % i for i in range(4)])
    WS.ring("O", ["WO%d" % i for i in range(5)])
    WS.ring("D", ["WD%d" % i for i in range(2)])
    WS.ring("B", ["WB%d" % i for i in range(2)])
    WS.ring("V", ["WV0", "WV1"])

    def layer0(ti, N, nseq, T, sample, last_prompt):
        save_ptr = A.ptr
        xbb = A([128, 12, nseq, 3 + T], BF16)
        xcb = A([128, 2, 3, N], BF16)
        xcf = A([128, 2, 3, N], F32)
        if sample:
            h0s = A([128, 12, NSEQ], F32)
            cv0s = A([128, 12, NSEQ, 3], F32)
            tmp0 = A([128, NSEQ], F32)
            ld(h0s[:], h0_d, "h0s")
            ld(cv0s[:], cv0_d, "cv0s")
        wa = {}
        st = {}

        def p1(g, j, split=False):
            wi, wname = WS.get(("X", ti, g))
            ch = 3 * g + j
            xs = g % 2
            bk = psum("xb", [0, 1])
            P.add("pe", lambda e: [e.matmul(ps[bk][:, :N], WA[:, wi, c, j * 128:(j + 1) * 128], xn[:, c, :N], start=(c == 0), stop=(c == 7)) for c in range(8)],
                  R=[wname] + XN, W=[psb[bk]])
            xbname = "xbb%d" % ch
            src3 = ps[bk][:, :N].rearrange("p (s t) -> p s t", s=nseq)
            if sample:
                P.add("dve", lambda e: e.tensor_copy(xbb[:, ch, :, 0:3], cv0s[:, ch, :, :]), R=["cv0s"], W=[xbname])
            else:
                P.add("dve", lambda e: e.tensor_copy(xbb[:, ch, 0, 0:3], halo[:, ch, 0:3]), R=["halo%d" % ch], W=[xbname])
            P.add("dve", lambda e: e.tensor_copy(xbb[:, ch, :, 3:3 + T], src3), R=[psb[bk]], W=[xbname])
            if sample:
                P.add("dve", lambda e: e.tensor_copy(cstage[:, ch, 1:17, :], src3[:, :, 1:4]), R=[psb[bk]], W=["cstage"])
            else:
                P.add("pool", lambda e: e.tensor_copy(halo[:, ch, 0:3], xbb[:, ch, 0, T:T + 3]), R=[xbname], W=["halo%d" % ch])
                if last_prompt:
                    P.add("dve", lambda e: e.tensor_copy(cstage[:, ch, 0, :], ps[bk][:, N - 3:N]), R=[psb[bk]], W=["cstage"])
            if j == 2:
                WS.release(("X", ti, g))
            if not split:
                p1b(g, j)

        def p1b(g, j):
            di, dname = WS.get(("D", ti, g))
            ch = 3 * g + j
            xs = g % 2
            xbname = "xbb%d" % ch
            cb = psum("xc", [2, 3, 4])
            dst3 = ps[cb][:, :N].rearrange("p (s t) -> p s t", s=nseq)
            P.add("pe", lambda e: [e.matmul(dst3, WD[:, di, j * 4 + k, :], xbb[:, ch, :, k:k + T], start=(k == 0), stop=(k == 3)) for k in range(4)],
                  R=[xbname, dname], W=[psb[cb]])
            P.add("dve", lambda e: e.tensor_scalar(xcf[:, xs, j, :], ps[cb][:, :N], pvec[:, CB + ch:CB + ch + 1], None, ALU.add),
                  R=[psb[cb], "pvec"], W=["xcf%d_%d" % (xs, j)])
            P.add("dve", lambda e: e.tensor_copy(xcb[:, xs, j, :], xcf[:, xs, j, :]), R=["xcf%d_%d" % (xs, j)], W=["xcb%d_%d" % (xs, j)])
            if j == 2:
                WS.release(("D", ti, g))

        def gm(n):
            g, j = divmod(n, 3)
            wi, wname = WS.get(("G", ti, g))
            bi, bdname = WS.get(("B", ti, g))
            xs = g % 2
            XC = ["xcb%d_%d" % (xs, jj) for jj in range(3)]
            rb, ib, gb = 5, 6, 7
            P.add("pe", lambda e: [e.matmul(ps[rb][:, :N], WBD[:, bi, 0, jp, j * 128:(j + 1) * 128], xcb[:, xs, jp, :], start=(jp == 0), stop=(jp == 2)) for jp in range(3)],
                  R=XC + [bdname], W=[psb[rb]])
            P.add("pe", lambda e: [e.matmul(ps[ib][:, :N], WBD[:, bi, 1, jp, j * 128:(j + 1) * 128], xcb[:, xs, jp, :], start=(jp == 0), stop=(jp == 2)) for jp in range(3)],
                  R=XC + [bdname], W=[psb[ib]])
            P.add("pe", lambda e: [e.matmul(ps[gb][:, :N], WA[:, wi, c, j * 128:(j + 1) * 128], xn[:, c, :N], start=(c == 0), stop=(c == 7)) for c in range(8)],
                  R=[wname] + XN, W=[psb[gb]])
            if j == 2:
                WS.release(("G", ti, g))
                WS.release(("B", ti, g))

        def stage_a(n):
            ch = n
            rb, ib, gb = 5, 6, 7
            sl = [slot() for _ in range(5)]
            nm = ["slot%d" % s_ for s_ in sl]
            vv = [slots[:, s_, :N] for s_ in sl]
            st[n] = (nm, vv)
            (n1, n2, n3, n4, n5), (v1, v2, v3, v4, v5) = nm, vv
            act(v1, ps[rb][:, :N], AF.Exp, [psb[rb], "der_a"], [n1], scale=-1.0, bias=der[:, NBR + ch:NBR + ch + 1])
            act(v4, ps[ib][:, :N], AF.Exp, [psb[ib], "der_a"], [n4], scale=-1.0, bias=der[:, NBI + ch:NBI + ch + 1])
            act(v5, ps[gb][:, :N], AF.Exp, [psb[gb]], [n5], scale=-1.0)
            act(v1, v1, AF.Ln, [n1], [n1], bias=1.0)
            act(v4, v4, AF.Ln, [n4], [n4], bias=1.0)
            act(v5, v5, AF.Ln, [n5], [n5], bias=1.0)
            act(v1, v1, AF.Exp, [n1], [n1], scale=-1.0)
            act(v5, v5, AF.Exp, [n5], [n5], scale=-1.0)
            act(v2, v1, AF.Exp, [n1, "der_c"], [n2], scale=der[:, CC + ch:CC + ch + 1])
            P.add("dve", lambda e: e.tensor_tensor(v5, v5, ps[gb][:, :N], ALU.mult), R=[n5, psb[gb]], W=[n5])
            act(v3, v1, AF.Exp, [n1, "der_c2"], [n3], scale=der[:, C2 + ch:C2 + ch + 1])
            act(v3, v3, AF.Ln, [n3], [n3], scale=-1.0, bias=1.0)
            P.add("dve", lambda e: e.scalar_tensor_tensor(v3, v3, 0.5, v4, ALU.mult, ALU.subtract), R=[n3, n4], W=[n3])

        def stage_b(n):
            ch = n
            g, j = divmod(n, 3)
            xs = g % 2
            (n1, n2, n3, n4, n5), (v1, v2, v3, v4, v5) = st[n]
            act(v3, v3, AF.Exp, [n3], [n3])
            P.add("dve", lambda e: e.tensor_tensor(v3, v3, xcf[:, xs, j, :], ALU.mult), R=[n3, "xcf%d_%d" % (xs, j)], W=[n3])
            if sample:
                a3 = v2.rearrange("p (s t) -> p s t", s=nseq)
                b3 = v3.rearrange("p (s t) -> p s t", s=nseq)
                P.add("dve", lambda e: e.tensor_tensor(tmp0[:, :], a3[:, :, 0], h0s[:, ch, :], ALU.mult), R=[n2, "h0s"], W=["tmp0"])
                P.add("dve", lambda e: e.tensor_tensor(b3[:, :, 0], b3[:, :, 0], tmp0[:, :], ALU.add), R=[n3, "tmp0"], W=[n3])
                P.add("dve", lambda e: e.memset(a3[:, :, 0], 0.0), R=["tmp0"], W=[n2])
                P.add("dve", lambda e: e.tensor_tensor_scan(v1, v2, v3, 0.0, ALU.mult, ALU.add), R=[n2, n3], W=[n1])
                h3 = v1.rearrange("p (s t) -> p s t", s=nseq)
                P.add("dve", lambda e: e.tensor_copy(hstage[:, ch, 1:17], h3[:, :, T - 1]), R=[n1], W=["hstage"])
            else:
                P.add("dve", lambda e: e.tensor_tensor_scan(v1, v2, v3, hcar[:, ch:ch + 1], ALU.mult, ALU.add), R=[n2, n3, "hcar%d" % ch], W=[n1])
                P.add("pool", lambda e: e.tensor_copy(hcar[:, ch:ch + 1], v1[:, N - 1:N]), R=[n1], W=["hcar%d" % ch])
                if last_prompt:
                    P.add("pool", lambda e: e.tensor_copy(hstage[:, ch, 0:1], v1[:, N - 1:N]), R=[n1], W=["hstage"])
            P.add("pool" if n < 10 else "dve", lambda e: e.tensor_tensor(ybuf[:, ch, :N], v1, v5, ALU.mult), R=[n1, n5], W=["y%d" % ch])

        for j in range(3):
            p1(0, j, split=True)
        for j in range(3):
            p1b(0, j)
        for n in range(12):
            g, j = divmod(n, 3)
            gm(n)
            if g + 1 < 4:
                p1(g + 1, j)
            stage_a(n)
            if n >= 1:
                stage_b(n - 1)
        stage_b(11)
        YB = ["y%d" % ch for ch in range(12)]
        for c in range(8):
            wo, woname = WS.get(("O0", ti, c))
            bk = psum("xb", [0, 1])
            P.add("pe", lambda e: [e.matmul(ps[bk][:, :N], WO[:, wo, ch, :], ybuf[:, ch, :N], start=(ch == 0), stop=(ch == 11)) for ch in range(12)],
                  R=[woname] + YB, W=[psb[bk]])
            WS.release(("O0", ti, c))
            if c > 0:
                norm_mm(c - 1, N)
            P.add("dve", lambda e: e.tensor_tensor(xt[:, c, :N], xt[:, c, :N], ps[bk][:, :N], ALU.add), R=["xt%d" % c, psb[bk]], W=["xt%d" % c])
            norm_sq(c, N)
        norm_mm(7, N)
        A.ptr = save_ptr

    def layer1(ti, N, nseq, T, sample):
        save_ptr = A.ptr
        nst = max(1, N // 128)
        SW = min(N, 128)
        C = 64 if not sample else T
        nch = N // C
        cps = SW // C
        Qd = A([128, 8, N], BF16)
        Kt = A([128, 8, N], BF16)
        Kl = A([128, 8, N], BF16)
        KlT = A([128, nst, 1024], BF16)
        V = A([128, nst, 1024], BF16)
        sgate = A([128, 8, N], BF16)
        ATb = A([128, 8, 128], BF16)
        osq = A([128, 2, 512], BF16)
        elast = A([128, 8, nch], F32)
        St = A([128, 8, 128], F32)
        if sample:
            NSB = 4
            S0f = A([128, NSB, 8, 128], F32)
            S0b = A([128, NSB, 8, 128], BF16)
            KlTm = A([128, 2, 1024], BF16)
            Sout = A([128, 2, 8, 128], F32)

            def load_s0(j):
                sb = j % NSB
                P.add("sp", lambda e: [e.dma_start(out=S0f[:, sb], in_=s0_d[:, j])], W=["S0f%d" % sb], dma=B("S0f%d" % sb))
                P.add("pool", lambda e: [e.dma_start(out=S0b[:, sb], in_=s0_d[:, j])], W=["S0b%d" % sb], dma=B("S0b%d" % sb))
            for j in range(NSB):
                load_s0(j)
        norm_finish(N, G1, False)
        cm = cmS if sample else cmP
        maskb = maskS if sample else maskP
        st = {}
        QB, FB, GB = [0, 3, 6, 7], [1, 4], [2, 5]

        def pm(h):
            wi, wname = WS.get(("H", ti, h))
            for (bk, off) in ((QB[h % 4], 0), (FB[h % 2], 128), (GB[h % 2], 256)):
                P.add("pe", lambda e: [e.matmul(ps[bk][:, :N], WA[:, wi, c, off:off + 128], xn[:, c, :N], start=(c == 0), stop=(c == 7)) for c in range(8)],
                      R=[wname] + XN, W=[psb[bk]])
            WS.release(("H", ti, h))

        def stage_a(h):
            qb, fb, gb = QB[h % 4], FB[h % 2], GB[h % 2]
            sl = [slot() for _ in range(5)]
            nm = ["slot%d" % s_ for s_ in sl]
            vv = [slots[:, s_, :N] for s_ in sl]
            st[h] = (nm, vv)
            (n1, n2, n3, n4, n5), (v1, v2, v3, v4, v5) = nm, vv
            act(v1, ps[qb][:, :N], AF.Exp, [psb[qb]], [n1], scale=-1.0)
            act(v2, ps[fb][:, :N], AF.Exp, [psb[fb]], [n2], scale=-1.0)
            act(v5, ps[gb][:, :N], AF.Exp, [psb[gb]], [n5], scale=-1.0)
            act(v1, v1, AF.Ln, [n1], [n1], bias=1.0)
            act(v3, v2, AF.Ln, [n2, "der_lb"], [n3], scale=der[:, LB + h:LB + h + 1], bias=1.0)
            act(v2, v2, AF.Ln, [n2], [n2], bias=1.0)
            act(v5, v5, AF.Ln, [n5], [n5], bias=1.0)
            act(v5, v5, AF.Exp, [n5], [n5], scale=-1.0)
            P.add("dve", lambda e: e.tensor_tensor(v3, v3, v2, ALU.subtract), R=[n2, n3], W=[n3])
            P.add("dve", lambda e: e.tensor_tensor(v2, v2, ps[fb][:, :N], ALU.add), R=[n2, psb[fb]], W=[n2])
            P.add("dve", lambda e: e.tensor_tensor_scan(v4, cm[:, :N], v3, 0.0, ALU.mult, ALU.add), R=[n3, "cst"], W=[n4])
            P.add("dve", lambda e: e.tensor_tensor(v2, v2, v4, ALU.add), R=[n2, n4], W=[n2])
            P.add("dve", lambda e: e.tensor_tensor(v1, v4, v1, ALU.subtract), R=[n1, n4], W=[n1])
            P.add("dve", lambda e: e.scalar_tensor_tensor(sgate[:, h, :], ps[gb][:, :N], pvec[:, OG + h:OG + h + 1], v5, ALU.mult, ALU.mult),
                  R=[n5, psb[gb], "pvec"], W=["sgate%d" % h])

        def stage_b(h):
            qb = QB[h % 4]
            (n1, n2, n3, n4, n5), (v1, v2, v3, v4, v5) = st[h]
            c3 = v4.rearrange("p (c t) -> p c t", t=C)
            act(elast[:, h, :], c3[:, :, C - 1], AF.Exp, [n4], ["elast%d" % h])
            act(Kt[:, h, :], v2, AF.Exp, [n2, "der_lnom"], ["Kt%d" % h], scale=-1.0, bias=der[:, LNOM + h:LNOM + h + 1])
            act(v1, v1, AF.Exp, [n1], [n1])
            P.add("dve", lambda e: e.tensor_tensor(Qd[:, h, :], v1, ps[qb][:, :N], ALU.mult), R=[n1, psb[qb]], W=["Qd%d" % h])
            if sample:
                P.add("dve", lambda e: [e.tensor_scalar(Kl[:, h, c * C:(c + 1) * C], Kt[:, h, c * C:(c + 1) * C], elast[:, h, c:c + 1], None, ALU.mult) for c in range(nch)],
                      R=["Kt%d" % h, "elast%d" % h], W=["Kl%d" % h])
            else:
                P.add("dve", lambda e: e.tensor_tensor(Kl[:, h, :].rearrange("p (c t) -> p c t", t=C), Kt[:, h, :].rearrange("p (c t) -> p c t", t=C),
                                                       elast[:, h, :].unsqueeze(2).broadcast_to([128, nch, C]), ALU.mult),
                      R=["Kt%d" % h, "elast%d" % h], W=["Kl%d" % h])

        for h in range(8):
            pm(h)
            stage_a(h)
            if h >= 1:
                stage_b(h - 1)
        stage_b(7)
        vi = 0
        for q in range(4):
            wv, wvname = WS.get(("V", ti, q))
            for s_i in range(nst):
                bk = [4, 5][vi % 2]
                P.add("pe", lambda e: [e.matmul(ps[bk][:SW, 0:256], xn[:, c, s_i * SW:(s_i + 1) * SW], WV[:, wv, c, :], start=(c == 0), stop=(c == 7)) for c in range(8)],
                      R=[wvname] + XN, W=[psb[bk]])
                if vi % 2 == 0:
                    P.add("act", lambda e: e.activation(V[:SW, s_i, q * 256:(q + 1) * 256], ps[bk][:SW, 0:256], AF.Copy), R=[psb[bk]], W=["V%d" % s_i])
                else:
                    P.add("dve", lambda e: e.tensor_copy(V[:SW, s_i, q * 256:(q + 1) * 256], ps[bk][:SW, 0:256]), R=[psb[bk]], W=["V%d" % s_i])
                vi += 1
            WS.release(("V", ti, q))
        KLN = ["Kl%d" % h for h in range(8)]
        for s_i in range(nst):
            for half in range(2):
                bk = [6, 7][vi % 2]
                pst = pstv[bk]
                P.add("pe", lambda e: [e.transpose(pst[:SW, hh * 128:(hh + 1) * 128], Kl[:, half * 4 + hh, s_i * SW:(s_i + 1) * SW], ident[:]) for hh in range(4)],
                      R=KLN + ["ident"], W=[psb[bk]])
                if vi % 2 == 0:
                    P.add("act", lambda e: e.activation(KlT[:SW, s_i, half * 512:(half + 1) * 512], pst[:SW, 0:512], AF.Copy), R=[psb[bk]], W=["KlT%d" % s_i])
                else:
                    P.add("dve", lambda e: e.tensor_copy(KlT[:SW, s_i, half * 512:(half + 1) * 512], pst[:SW, 0:512]), R=[psb[bk]], W=["KlT%d" % s_i])
                vi += 1
        ost = {}

        def opath_a(s_i):
            OB = [6, 7] if s_i % 2 == 0 else [0, 1]
            for half in range(2):
                ob = OB[half]
                nb = 4 + half
                sr, s1_ = slot(), slot()
                ost[(s_i, half)] = (sr, s1_)
                ow = ps[ob][:, :].rearrange("p (h t) -> p h t", h=4)[:, :, :SW]
                osq3 = osq[:, half, :].rearrange("p (h t) -> p h t", h=4)[:, :, :SW]
                nw = ps[nb][:, :].rearrange("p (h t) -> p h t", h=4)[:, :, :SW]
                P.add("act", lambda e: e.activation(osq3, ow, AF.Square), R=[psb[ob]], W=["osq%d" % half])
                if SW == 128:
                    P.add("pe", lambda e: e.matmul(nw, ones[:], osq3, start=True, stop=True, skip_group_check=True), R=["osq%d" % half, "ones"], W=[psb[nb]])
                else:
                    P.add("pe", lambda e: [e.matmul(ps[nb][:, hh * 128:hh * 128 + SW], ones[:], osq[:, half, hh * 128:hh * 128 + SW], start=True, stop=True, skip_group_check=True) for hh in range(4)],
                          R=["osq%d" % half, "ones"], W=[psb[nb]])

        def opath_b1(s_i):
            for half in range(2):
                nb = 4 + half
                sr, s1_ = ost[(s_i, half)]
                nr = "slot%d" % sr
                nw = ps[nb][:, :].rearrange("p (h t) -> p h t", h=4)[:, :, :SW]
                rs3 = slots[:, sr, :].rearrange("p (h t) -> p h t", h=4)[:, :, :SW]
                P.add("dve", lambda e: e.tensor_scalar(rs3, nw, 1.0 / 128.0, EPS, ALU.mult, ALU.add), R=[psb[nb]], W=[nr])
                act(rs3, rs3, AF.Ln, [nr], [nr])
                act(rs3, rs3, AF.Exp, [nr], [nr], scale=-0.5)

        def opath_b2(s_i):
            OB = [6, 7] if s_i % 2 == 0 else [0, 1]
            t0_ = s_i * SW
            for half in range(2):
                ob = OB[half]
                sr, s1_ = ost[(s_i, half)]
                nr, n1_ = "slot%d" % sr, "slot%d" % s1_
                ow = ps[ob][:, :].rearrange("p (h t) -> p h t", h=4)[:, :, :SW]
                rs3 = slots[:, sr, :].rearrange("p (h t) -> p h t", h=4)[:, :, :SW]
                t13 = slots[:, s1_, :].rearrange("p (h t) -> p h t", h=4)[:, :, :SW]
                P.add("dve", lambda e: e.tensor_tensor(t13, ow, rs3, ALU.mult), R=[psb[ob], nr], W=[n1_])
                P.add("dve", lambda e: e.tensor_tensor(ybuf[:, half * 4:half * 4 + 4, t0_:t0_ + SW], t13, sgate[:, half * 4:half * 4 + 4, t0_:t0_ + SW], ALU.mult),
                      R=[n1_] + ["sgate%d" % (half * 4 + hh) for hh in range(4)], W=["y1_%d_%d" % (half, s_i)])

        for s_i in range(nst):
            t0 = s_i * SW
            OB = [6, 7] if s_i % 2 == 0 else [0, 1]
            for half in range(2):
                atb = 4 + half
                for hh in range(4):
                    h = half * 4 + hh
                    P.add("pe", lambda e: e.matmul(ps[atb][:SW, hh * 128:hh * 128 + SW], Kt[:, h, t0:t0 + SW], Qd[:, h, t0:t0 + SW], start=True, stop=True, skip_group_check=True),
                          R=["Kt%d" % h, "Qd%d" % h], W=[psb[atb]])
                for hh in range(4):
                    h = half * 4 + hh
                    P.add("dve", lambda e: e.tensor_tensor(ATb[:SW, h, :SW], ps[atb][:SW, hh * 128:hh * 128 + SW], maskb[:SW, :SW], ALU.mult),
                          R=[psb[atb], "maskP", "maskS"], W=["AT%d" % h])
            for half in range(2):
                ob = OB[half]
                for hh in range(4):
                    h = half * 4 + hh
                    P.add("pe", lambda e: e.matmul(ps[ob][:, hh * 128:hh * 128 + SW], V[:SW, s_i, h * 128:(h + 1) * 128], ATb[:SW, h, :SW], start=(hh == 0), stop=False, skip_group_check=True),
                          R=["V%d" % s_i, "AT%d" % h], W=[psb[ob]])
            if s_i > 0:
                opath_a(s_i - 1)
            if not sample:
                for cc in range(cps):
                    ci = s_i * cps + cc
                    p0 = cc * C
                    UB = [2, 3]
                    for half in range(2):
                        hs = slice(half * 4, half * 4 + 4)
                        HN = ["%d" % (half * 4 + hh) for hh in range(4)]
                        ebc = elast[:, hs, ci:ci + 1].broadcast_to([128, 4, 128])
                        P.add("dve", lambda e: e.tensor_tensor(St[:, hs, :], S[:, hs, :], ebc, ALU.mult),
                              R=["S" + x for x in HN] + ["elast" + x for x in HN], W=["St%d" % half])
                    for half in range(2):
                        ob = OB[half]
                        for hh in range(4):
                            h = half * 4 + hh
                            P.add("pe", lambda e: e.matmul(ps[ob][:, hh * 128 + p0:hh * 128 + p0 + C], Sb[:, h, :], Qd[:, h, t0 + p0:t0 + p0 + C], start=False, stop=(cc == cps - 1), skip_group_check=True),
                                  R=["Sb%d" % h, "Qd%d" % h], W=[psb[ob]])
                    for half in range(2):
                        ub = UB[half]
                        for hh in range(4):
                            h = half * 4 + hh
                            P.add("pe", lambda e: e.matmul(ps[ub][:, hh * 128:(hh + 1) * 128], KlT[p0:p0 + C, s_i, h * 128:(h + 1) * 128], V[p0:p0 + C, s_i, h * 128:(h + 1) * 128], start=True, stop=True, skip_group_check=True),
                                  R=["KlT%d" % s_i, "V%d" % s_i], W=[psb[ub]])
                    for half in range(2):
                        ub = UB[half]
                        hs = slice(half * 4, half * 4 + 4)
                        HN = ["%d" % (half * 4 + hh) for hh in range(4)]
                        u3 = ps[ub][:, :].rearrange("p (h v) -> p h v", h=4)
                        P.add("dve", lambda e: e.tensor_tensor(S[:, hs, :], St[:, hs, :], u3, ALU.add),
                              R=[psb[ub], "St%d" % half], W=["S" + x for x in HN])
                        P.add("act", lambda e: e.activation(Sb[:, hs, :], S[:, hs, :], AF.Copy), R=["S" + x for x in HN], W=["Sb" + x for x in HN])
                    if s_i > 0:
                        if cc == 0:
                            opath_b1(s_i - 1)
                        elif cc == cps - 1:
                            opath_b2(s_i - 1)
            else:
                for j in range(NSEQ):
                    sb = j % NSB
                    so = j % 2
                    UB = [2, 3] if j % 2 == 0 else [4, 5]
                    P.add("dve", lambda e: e.tensor_scalar(KlTm[:SW, so, :], KlT[:SW, 0, :], rowmask[:SW, j:j + 1], None, ALU.mult),
                          R=["KlT0", "cst"], W=["KlTm%d" % so])
                    for half in range(2):
                        hs = slice(half * 4, half * 4 + 4)
                        ebc = elast[:, hs, j:j + 1].broadcast_to([128, 4, 128])
                        P.add("dve", lambda e: e.tensor_tensor(St[:, hs, :], S0f[:, sb, hs, :], ebc, ALU.mult),
                              R=["S0f%d" % sb] + ["elast%d" % (half * 4 + hh) for hh in range(4)], W=["St%d" % half])
                    for half in range(2):
                        ob = OB[half]
                        for hh in range(4):
                            h = half * 4 + hh
                            P.add("pe", lambda e: e.matmul(ps[ob][:, hh * 128 + j * T:hh * 128 + (j + 1) * T], S0b[:, sb, h, :], Qd[:, h, j * T:(j + 1) * T], start=False, stop=(j == NSEQ - 1), skip_group_check=True),
                                  R=["S0b%d" % sb, "Qd%d" % h], W=[psb[ob]])
                    for half in range(2):
                        ub = UB[half]
                        for hh in range(4):
                            h = half * 4 + hh
                            P.add("pe", lambda e: e.matmul(ps[ub][:, hh * 128:(hh + 1) * 128], KlTm[:SW, so, h * 128:(h + 1) * 128], V[:SW, 0, h * 128:(h + 1) * 128], start=True, stop=True, skip_group_check=True),
                                  R=["KlTm%d" % so, "V0"], W=[psb[ub]])
                    for half in range(2):
                        ub = UB[half]
                        hs = slice(half * 4, half * 4 + 4)
                        u3 = ps[ub][:, :].rearrange("p (h v) -> p h v", h=4)
                        P.add("dve", lambda e: e.tensor_tensor(Sout[:, so, hs, :], St[:, hs, :], u3, ALU.add),
                              R=[psb[ub], "St%d" % half], W=["Sout%d_%d" % (so, half)])
                    P.add("sp", lambda e: [e.dma_start(out=sout_d[:, 1 + j], in_=Sout[:, so])], R=["Sout%d_0" % so, "Sout%d_1" % so], dma=B("Sout%d" % so))
                    if j + NSB < NSEQ:
                        load_s0(j + NSB)
        opath_a(nst - 1)
        opath_b1(nst - 1)
        opath_b2(nst - 1)
        YB = ["y1_%d_%d" % (half, s_i) for half in range(2) for s_i in range(nst)]
        for c in range(8):
            wo, woname = WS.get(("O1", ti, c))
            bk = psum("xb", [0, 1])
            P.add("pe", lambda e: [e.matmul(ps[bk][:, :N], WO[:, wo, hc, :], ybuf[:, hc, :N], start=(hc == 0), stop=(hc == 7)) for hc in range(8)],
                  R=[woname] + YB, W=[psb[bk]])
            WS.release(("O1", ti, c))
            if c > 0:
                norm_mm(c - 1, N)
            P.add("dve", lambda e: e.tensor_tensor(xt[:, c, :N], xt[:, c, :N], ps[bk][:, :N], ALU.add), R=["xt%d" % c, psb[bk]], W=["xt%d" % c])
            norm_sq(c, N)
        norm_mm(7, N)
        A.ptr = save_ptr

    pstv = {6: ps[6].bitcast(BF16), 7: ps[7].bitcast(BF16)}

    tiles = [(i * NT, NT, 1, NT, False) for i in range(SEQ // NT)] + [(SEQ, NS, NSEQ, 4, True)]
    XT = ["xt%d" % c for c in range(8)]
    for ti in range(len(tiles)):
        for g in range(4):
            WS.plan(("X", ti, g), "A", (lambda g: lambda e, i: [e.dma_start(out=WA[:, i, :, :], in_=w_in0_v[:, :, 384 * g:384 * g + 384])])(g), 1)
            WS.plan(("D", ti, g), "D", (lambda g: lambda e, i: [e.dma_start(out=WD[:, i], in_=diagw_d[:, 12 * g:12 * g + 12, :])])(g), 1)
            WS.plan(("G", ti, g), "A", (lambda g: lambda e, i: [e.dma_start(out=WA[:, i, :, :], in_=w_in0_v[:, :, W + 384 * g:W + 384 * g + 384])])(g), 1)
            WS.plan(("B", ti, g), "B", (lambda g: lambda e, i: [e.dma_start(out=WBD[:, i], in_=wbd_d[:, g])])(g), 1)
        for c in range(8):
            WS.plan(("O0", ti, c), "O", (lambda c: lambda e, i: [e.dma_start(out=WO[:, i, 0:12, :], in_=w_out0_v[:, :, c * 128:(c + 1) * 128])])(c), 1)
        for h in range(8):
            WS.plan(("H", ti, h), "A", (lambda h: lambda e, i: [e.dma_start(out=WA[:, i, :, k * 128:(k + 1) * 128], in_=w_in1_v[:, :, cb_ + h * 128:cb_ + (h + 1) * 128]) for k, cb_ in enumerate((0, 1024, 3072))])(h), 3)
        for q in range(4):
            WS.plan(("V", ti, q), "V", (lambda q: lambda e, i: [e.dma_start(out=WV[:, i], in_=w_in1_v[:, :, 2048 + q * 256:2048 + (q + 1) * 256])])(q), 1)
        for c in range(8):
            WS.plan(("O1", ti, c), "O", (lambda c: lambda e, i: [e.dma_start(out=WO[:, i, 0:8, :], in_=w_out1_v[:, :, c * 128:(c + 1) * 128])])(c), 1)

    def load_x_chunk(c, n0, N):
        P.add("sp", lambda e: [e.dma_start(out=xt[:, c, :N], in_=xin_v[:, c, n0:n0 + N])], W=["xt%d" % c], dma=B("xtld%d" % c), bar=False)

    for c in range(8):
        load_x_chunk(c, tiles[0][0], tiles[0][1])
    WS.pump()
    P.barrier()
    norm(tiles[0][1], G0, False)
    for ti, (n0, N, nseq, T, sample) in enumerate(tiles):
        layer0(ti, N, nseq, T, sample, (not sample) and n0 + N == SEQ)
        P.barrier()
        layer1(ti, N, nseq, T, sample)
        fs = [slot() for _ in range(8)]
        nxt = tiles[ti + 1] if ti + 1 < len(tiles) else None
        norm_finish(N, GF, True, fs, after=(lambda c: load_x_chunk(c, nxt[0], nxt[1])) if nxt else None)
        P.add("sp", lambda e: [e.dma_start(out=yout_v[:, c, n0:n0 + N], in_=slots[:, fs[c], :N]) for c in range(8)], R=["slot%d" % s_ for s_ in fs], dma=B("xtst"), ndma=8, bar=False)
        if n0 + N == SEQ:
            P.add("sp", lambda e: [e.dma_start(out=sout_d[:, 0], in_=S[:])], R=["S%d" % h for h in range(8)], dma=B("Sst"))
        if nxt:
            norm(nxt[1], G0, False)
        P.barrier()
    P.add("sp", lambda e: [e.dma_start(out=hout_d, in_=hstage[:])], R=["hstage"], dma=B("hst"))
    P.add("sp", lambda e: [e.dma_start(out=cvout_d, in_=cstage[:])], R=["cstage"], dma=B("cst_out"))
    P.barrier(all_dma=True)
    return nc, P


_CACHE = {}


def _consts():
    c = np.zeros((128, 976), np.float32)
    c[:, 0:128] = np.eye(128, dtype=np.float32)
    s = np.arange(128)[:, None]
    t = np.arange(128)[None, :]
    c[:, 128:256] = ((s // 64 == t // 64) & (s <= t)).astype(np.float32)
    c[:, 256:384] = ((s // 4 == t // 4) & (s <= t)).astype(np.float32)
    cm = np.ones(512, np.float32)
    cm[::64] = 0.0
    c[:, 384:896] = cm[None, :]
    cs = np.ones(64, np.float32)
    cs[::4] = 0.0
    c[:, 896:960] = cs[None, :]
    p = np.arange(128)[:, None]
    j = np.arange(16)[None, :]
    c[:, 960:976] = ((p // 4 == j) & (p < 64)).astype(np.float32)
    return c


def _pc(v, n):
    return np.ascontiguousarray(np.asarray(v, np.float32).reshape(n, 128).T)


def kernel(x_prompt, x_sample, state_lru_h, state_lru_conv, state_hgrn, norm_gain, a_w_in, a_conv_w,
           a_conv_b, a_w_r, a_b_r, a_w_i, a_b_i, a_lambda, a_w_out, b_w_in, b_lb_logits, b_o_gain,
           b_w_out, final_gain):
    f = np.float32
    if "nc" not in _CACHE:
        nc, P = build_nc()
        _assign_and_emit(P)
        _CACHE["nc"] = nc
    nc = _CACHE["nc"]
    pvec = np.zeros((128, 96), f)
    pvec[:, 0:8] = _pc(norm_gain[0], 8)
    pvec[:, 8:16] = _pc(norm_gain[1], 8)
    pvec[:, 16:24] = _pc(final_gain, 8)
    pvec[:, 24:36] = _pc(a_conv_b[0], 12)
    pvec[:, 36:48] = _pc(a_b_r[0], 12)
    pvec[:, 48:60] = _pc(a_b_i[0], 12)
    pvec[:, 60:72] = _pc(a_lambda[0], 12)
    pvec[:, 72:80] = _pc(b_lb_logits[0], 8)
    pvec[:, 80:88] = _pc(b_lb_logits[1], 8)
    pvec[:, 88:96] = _pc(b_o_gain[0], 8)
    diagw = np.zeros((128, 48, 128), f)
    cw = np.asarray(a_conv_w[0], f)
    idx = np.arange(128)
    for ch in range(12):
        for k in range(4):
            diagw[idx, ch * 4 + k, idx] = cw[k, ch * 128:(ch + 1) * 128]
    wbd = np.zeros((128, 4, 2, 3, 384), f)
    for gi, wg in enumerate((np.asarray(a_w_r[0], f), np.asarray(a_w_i[0], f))):
        for g in range(4):
            full = np.zeros((384, 384), f)
            for b in range(4):
                full[96 * b:96 * b + 96, 96 * b:96 * b + 96] = wg[4 * g + b]
            wbd[:, g, gi] = full.reshape(3, 128, 384).transpose(1, 0, 2)
    cst = _consts()
    w_in0 = np.ascontiguousarray(a_w_in[0], f)
    w_out0 = np.ascontiguousarray(a_w_out[0], f)
    w_in1 = np.ascontiguousarray(b_w_in[0], f)
    w_out1 = np.ascontiguousarray(b_w_out[0], f)
    in_maps = []
    for b in range(NCORES):
        sl = slice(16 * b, 16 * b + 16)
        xs = np.asarray(x_sample[sl], f).reshape(64, D)
        xin = np.ascontiguousarray(np.concatenate([np.asarray(x_prompt[b], f), xs], axis=0).T)
        h0 = np.ascontiguousarray(np.asarray(state_lru_h[0, sl], f).reshape(16, 12, 128).transpose(2, 1, 0))
        cv0 = np.ascontiguousarray(np.asarray(state_lru_conv[0, sl], f).reshape(16, 3, 12, 128).transpose(3, 2, 0, 1))
        s0 = np.ascontiguousarray(np.asarray(state_hgrn[0, sl], f).transpose(2, 0, 1, 3))
        in_maps.append({"xin": xin, "w_in0": w_in0, "w_out0": w_out0, "w_in1": w_in1, "w_out1": w_out1,
                        "diagw": diagw, "wbd": wbd, "pvec": pvec, "cst": cst, "h0": h0, "cv0": cv0, "s0": s0})
    res = run_bass_kernel_spmd(nc, in_maps, core_ids=list(range(NCORES)))
    y_prompt = np.zeros((8, SEQ, D), f)
    y_sample = np.zeros((128, 4, D), f)
    hp = np.zeros((1, 8, W), f)
    bp = np.zeros((1, 8, 3, W), f)
    Sp = np.zeros((1, 8, 8, 128, 128), f)
    hs = np.zeros((1, 128, W), f)
    bs = np.zeros((1, 128, 3, W), f)
    Ss = np.zeros((1, 128, 8, 128, 128), f)
    for b in range(NCORES):
        r = res.results[b]
        sl = slice(16 * b, 16 * b + 16)
        yT = r["yout"]
        y_prompt[b] = yT[:, :SEQ].T
        y_sample[sl] = yT[:, SEQ:].T.reshape(16, 4, D)
        ho = r["hout"]
        hp[0, b] = ho[:, :, 0].T.reshape(W)
        hs[0, sl] = ho[:, :, 1:].transpose(2, 1, 0).reshape(16, W)
        co = r["cvout"]
        bp[0, b] = co[:, :, 0, :].transpose(2, 1, 0).reshape(3, W)
        bs[0, sl] = co[:, :, 1:, :].transpose(2, 3, 1, 0).reshape(16, 3, W)
        so = r["sout"]
        Sp[0, b] = so[:, 0].transpose(1, 0, 2)
        Ss[0, sl] = so[:, 1:].transpose(1, 2, 0, 3)
    return (y_prompt, y_sample, hp, bp, Sp, hs, bs, Ss)


def _assign_and_emit(P):
    P.emit()
```

```python
import numpy as np
import concourse.bass as bass
import concourse.mybir as mybir
from concourse.bass_utils import run_bass_kernel_spmd

F32 = mybir.dt.float32
BF16 = mybir.dt.bfloat16
AF = mybir.ActivationFunctionType
ALU = mybir.AluOpType

D = 1024
W = 1536
SEQ = 2048
NT = 512
NS = 64
NSEQ = 16
EPS = 1e-6
NCORES = 8
SB_BASE = 16512
SB_LIMIT = 16512 + 212863


class Buf:
    __slots__ = ("name", "lw", "rd", "sem", "dcount", "excl")

    def __init__(self, name):
        self.name = name
        self.excl = False
        self.lw = None
        self.rd = []
        self.sem = None
        self.dcount = 0


class Op:
    __slots__ = ("eng", "fn", "deps", "idx", "signal", "count", "dma", "buf", "reads", "writes", "ndma", "calls", "nobar")


class Rec:
    def __init__(self):
        self.calls = []

    def __getattr__(self, name):
        def f(*a, **k):
            self.calls.append((name, a, k))
            return len(self.calls) - 1
        return f


COMPUTE = ("pe", "act", "dve", "pool")
ENGS = ("pe", "act", "dve", "pool", "sp")


class Prog:
    def __init__(self, nc):
        self.nc = nc
        self.ops = {e: [] for e in ENGS}
        self.dma_since_barrier = []
        self.dma_nobar = []
        self.eng_obj = {"pe": nc.tensor, "act": nc.scalar, "dve": nc.vector, "pool": nc.gpsimd, "sp": nc.sync}
        self.bufs = {}

    def buf(self, name):
        b = self.bufs.get(name)
        if b is None:
            b = Buf(name)
            self.bufs[name] = b
        return b

    def add(self, eng, fn, R=(), W=(), dma=None, ndma=1, bar=True):
        op = Op()
        op.nobar = False
        op.eng = eng
        op.fn = fn
        rec = Rec()
        fn(rec)
        op.calls = rec.calls
        op.signal = False
        op.count = 0
        op.dma = dma is not None
        op.buf = dma
        if dma is not None:
            if dma.sem is None:
                dma.sem = self.nc.alloc_semaphore("dsem_" + dma.name)
            dma.dcount += 16 * ndma
            op.count = dma.dcount
            op.ndma = ndma
        R = [self.buf(b) if isinstance(b, str) else b for b in R]
        W = [self.buf(b) if isinstance(b, str) else b for b in W]
        op.reads = R
        op.writes = W
        deps = {}
        for b in R:
            if b.lw is not None:
                deps[id(b.lw)] = (b.lw, True)
            if b.excl:
                for r in b.rd:
                    if r.eng != eng and id(r) not in deps:
                        deps[id(r)] = (r, False)
        for b in W:
            if b.lw is not None and id(b.lw) not in deps:
                deps[id(b.lw)] = (b.lw, False)
            for r in b.rd:
                if id(r) not in deps:
                    deps[id(r)] = (r, False)
        for b in W:
            b.lw = op
            b.rd = []
        for b in R:
            if b.lw is not op:
                b.rd.append(op)
        op.deps = list(deps.values())
        op.idx = len(self.ops[eng])
        self.ops[eng].append(op)
        if op.dma:
            op.nobar = not bar
            if bar:
                self.dma_since_barrier.append(op)
            else:
                self.dma_nobar.append(op)
        return op

    def barrier(self, all_dma=False):
        if all_dma:
            self.dma_since_barrier += self.dma_nobar
            self.dma_nobar = []
        lasts = []
        for e in ENGS:
            for o in reversed(self.ops[e]):
                if o.dma and getattr(o, "nobar", False) and not all_dma:
                    continue
                lasts.append(o)
                break
        last_by_buf = {}
        for o in self.dma_since_barrier:
            last_by_buf[id(o.buf)] = o
        dmas = list(last_by_buf.values())
        self.dma_since_barrier = []
        for e in ENGS:
            op = Op()
            op.nobar = False
            op.eng = e
            op.fn = None
            op.signal = False
            op.count = 0
            op.dma = False
            op.buf = None
            op.reads = []
            op.writes = []
            op.deps = [(o, True) for o in lasts if o.eng != e or o.dma] + [(o, True) for o in dmas]
            op.idx = len(self.ops[e])
            self.ops[e].append(op)

    def emit(self):
        nc = self.nc
        def needs(op, dep, raw):
            if dep.dma:
                return True
            if dep.fn is None:
                return False
            if dep.eng != op.eng:
                return True
            if op.eng == "pe":
                return False
            return True

        for e in ENGS:
            for op in self.ops[e]:
                op.deps = [(d, r) for (d, r) in op.deps if needs(op, d, r)]
                for d, r in op.deps:
                    d.signal = True
        sems = {}
        for e in COMPUTE:
            sems[e] = nc.alloc_semaphore("sem_" + e)
            c = 0
            for op in self.ops[e]:
                if op.dma:
                    continue
                if op.signal and op.fn is not None:
                    c += 1
                    op.count = c
        for e in ENGS:
            eng = self.eng_obj[e]
            waited = {}
            for op in self.ops[e]:
                for d, r in op.deps:
                    if d.dma:
                        sem = d.buf.sem
                        val = d.count
                        if val == 0:
                            raise RuntimeError("dma dep emitted before producer: %s" % d.buf.name)
                    else:
                        sem = sems[d.eng]
                        val = d.count
                    key = id(sem)
                    if waited.get(key, 0) >= val:
                        continue
                    waited[key] = val
                    eng.wait_ge(sem, val)
                if op.fn is None:
                    continue
                res = [getattr(eng, name)(*a, **k) for (name, a, k) in op.calls]
                if op.dma:
                    assert len(res) == op.ndma, (op.buf.name, len(res), op.ndma)
                    for ins in res:
                        ins.then_inc(op.buf.sem, 16)
                elif op.signal:
                    res[-1].then_inc(sems[e], 1)


class Alloc:
    def __init__(self, nc):
        self.nc = nc
        self.ptr = SB_BASE
        self.n = 0

    def __call__(self, shape, dtype, at=None):
        esz = 2 if dtype == BF16 else 4
        nbytes = esz
        for s in shape[1:]:
            nbytes *= s
        nbytes = (nbytes + 31) // 32 * 32
        if at is None:
            at = self.ptr
            self.ptr += nbytes
        assert at + nbytes <= SB_LIMIT, ("sbuf overflow", at, nbytes)
        self.n += 1
        return self.nc.alloc_sbuf_tensor_at("t%d" % self.n, list(shape), dtype, offset=at)


def build_nc():
    nc = bass.Bass("TRN2", target_bir_lowering=False)
    NTOK = SEQ + NS
    xin = nc.dram_tensor("xin", [D, NTOK], F32, kind="ExternalInput").ap()
    w_in0 = nc.dram_tensor("w_in0", [D, 2 * W], F32, kind="ExternalInput").ap()
    w_out0 = nc.dram_tensor("w_out0", [W, D], F32, kind="ExternalInput").ap()
    w_in1 = nc.dram_tensor("w_in1", [D, 4 * D], F32, kind="ExternalInput").ap()
    w_out1 = nc.dram_tensor("w_out1", [D, D], F32, kind="ExternalInput").ap()
    diagw_d = nc.dram_tensor("diagw", [128, 48, 128], F32, kind="ExternalInput").ap()
    wbd_d = nc.dram_tensor("wbd", [128, 4, 2, 3, 384], F32, kind="ExternalInput").ap()
    pvec_d = nc.dram_tensor("pvec", [128, 96], F32, kind="ExternalInput").ap()
    cst_d = nc.dram_tensor("cst", [128, 976], F32, kind="ExternalInput").ap()
    h0_d = nc.dram_tensor("h0", [128, 12, NSEQ], F32, kind="ExternalInput").ap()
    cv0_d = nc.dram_tensor("cv0", [128, 12, NSEQ, 3], F32, kind="ExternalInput").ap()
    s0_d = nc.dram_tensor("s0", [128, NSEQ, 8, 128], F32, kind="ExternalInput").ap()
    yout = nc.dram_tensor("yout", [D, NTOK], F32, kind="ExternalOutput").ap()
    hout_d = nc.dram_tensor("hout", [128, 12, 17], F32, kind="ExternalOutput").ap()
    cvout_d = nc.dram_tensor("cvout", [128, 12, 17, 3], F32, kind="ExternalOutput").ap()
    sout_d = nc.dram_tensor("sout", [128, 17, 8, 128], F32, kind="ExternalOutput").ap()

    xin_v = xin.rearrange("(c p) n -> p c n", p=128)
    yout_v = yout.rearrange("(c p) n -> p c n", p=128)
    w_in0_v = w_in0.rearrange("(c p) n -> p c n", p=128)
    w_out0_v = w_out0.rearrange("(c p) n -> p c n", p=128)
    w_in1_v = w_in1.rearrange("(c p) n -> p c n", p=128)
    w_out1_v = w_out1.rearrange("(c p) n -> p c n", p=128)

    P = Prog(nc)
    A = Alloc(nc)
    B = P.buf

    xt = A([128, 8, NT], F32)
    xn = A([128, 8, NT], BF16)
    xsq = A([128, 2, NT], BF16)
    stat = A([128, 2, NT], F32)
    NSLOT = 15
    slots = A([128, NSLOT, NT], F32)
    ybuf = A([128, 12, NT], BF16)
    WA = A([128, 4, 8, 384], BF16)
    WV = A([128, 2, 8, 256], BF16)
    WO = A([128, 5, 12, 128], BF16)
    ident = A([128, 128], BF16)
    ones = A([128, 128], BF16)
    maskP = A([128, 128], BF16)
    maskS = A([128, 128], BF16)
    cst = A([128, 976], F32)
    pvec = A([128, 96], F32)
    der = A([128, 96], F32)
    halo = A([128, 12, 4], BF16)
    hcar = A([128, 12], F32)
    WD = A([128, 2, 12, 128], BF16)
    WBD = A([128, 2, 2, 3, 384], BF16)
    S = A([128, 8, 128], F32)
    Sb = A([128, 8, 128], BF16)
    hstage = A([128, 12, 17], F32)
    cstage = A([128, 12, 17, 3], F32)
    cv0s = A([128, 12, NSEQ, 3], F32)
    ARENA0 = A.ptr

    ps = [nc.alloc_psum_tensor("ps%d" % i, [128, 512], F32) for i in range(8)]
    psb = [B("ps%d" % i) for i in range(8)]
    for b_ in psb:
        b_.excl = True

    cmP = cst[:, 384:896]
    cmS = cst[:, 896:960]
    rowmask = cst[:, 960:976]
    G0, G1, GF, CB, BR, BI, LAM, L0C, L1C, OG = 0, 8, 16, 24, 36, 48, 60, 72, 80, 88
    NBR, NBI, CC, C2, LB, LNOM, T0, T1, T2, T3 = 0, 12, 24, 36, 48, 56, 64, 72, 80, 88

    def ld(dst, src, bname, eng="sp"):
        P.add(eng, lambda e: [e.dma_start(out=dst, in_=src)], W=[bname], dma=B(bname))

    ld(cst[:], cst_d, "cst")
    ld(pvec[:], pvec_d, "pvec")
    ld(cv0s[:], cv0_d, "cv0s")
    P.add("dve", lambda e: e.tensor_copy(ident[:], cst[:, 0:128]), R=["cst"], W=["ident"])
    P.add("dve", lambda e: e.tensor_copy(maskP[:], cst[:, 128:256]), R=["cst"], W=["maskP"])
    P.add("dve", lambda e: e.tensor_copy(maskS[:], cst[:, 256:384]), R=["cst"], W=["maskS"])
    P.add("dve", lambda e: e.memset(ones[:], 1.0), W=["ones"])
    P.add("dve", lambda e: e.memset(hcar[:], 0.0), W=["hcar"])
    P.add("dve", lambda e: e.memset(halo[:], 0.0), W=["halo"])
    P.add("dve", lambda e: e.memset(S[:], 0.0), W=["S"])
    P.add("dve", lambda e: e.memset(Sb[:], 0.0), W=["Sb"])
    P.add("dve", lambda e: e.tensor_scalar(der[:, NBR:NBR + 12], pvec[:, BR:BR + 12], -1.0, None, ALU.mult), R=["pvec"], W=["der_a"])
    P.add("dve", lambda e: e.tensor_scalar(der[:, NBI:NBI + 12], pvec[:, BI:BI + 12], -1.0, None, ALU.mult), R=["pvec"], W=["der_a"])
    P.add("act", lambda e: e.activation(der[:, CC:CC + 12], pvec[:, LAM:LAM + 12], AF.Exp, scale=-1.0), R=["pvec"], W=["der_c"])
    P.add("act", lambda e: e.activation(der[:, CC:CC + 12], der[:, CC:CC + 12], AF.Ln, bias=1.0), R=["der_c"], W=["der_c"])
    P.add("dve", lambda e: e.tensor_scalar(der[:, C2:C2 + 12], der[:, CC:CC + 12], -16.0, None, ALU.mult), R=["der_c"], W=["der_c2"])
    P.add("dve", lambda e: e.tensor_scalar(der[:, CC:CC + 12], der[:, CC:CC + 12], -8.0, None, ALU.mult), R=["der_c", "der_c2"], W=["der_c"])
    P.add("dve", lambda e: e.tensor_tensor(der[:, T0:T0 + 8], pvec[:, L0C:L0C + 8], pvec[:, L1C:L1C + 8], ALU.max), R=["pvec"], W=["der_t0"])
    P.add("dve", lambda e: e.tensor_tensor(der[:, T1:T1 + 8], pvec[:, L0C:L0C + 8], der[:, T0:T0 + 8], ALU.subtract), R=["pvec", "der_t0"], W=["der_t1"])
    P.add("dve", lambda e: e.tensor_tensor(der[:, T2:T2 + 8], pvec[:, L1C:L1C + 8], der[:, T0:T0 + 8], ALU.subtract), R=["pvec", "der_t0"], W=["der_t2"])
    P.add("act", lambda e: e.activation(der[:, T1:T1 + 8], der[:, T1:T1 + 8], AF.Exp), R=["der_t1"], W=["der_t1"])
    P.add("act", lambda e: e.activation(der[:, T2:T2 + 8], der[:, T2:T2 + 8], AF.Exp), R=["der_t2"], W=["der_t2"])
    P.add("dve", lambda e: e.tensor_tensor(der[:, T3:T3 + 8], der[:, T1:T1 + 8], der[:, T2:T2 + 8], ALU.add), R=["der_t1", "der_t2"], W=["der_t3"])
    P.add("dve", lambda e: e.reciprocal(der[:, T3:T3 + 8], der[:, T3:T3 + 8]), R=["der_t3"], W=["der_t3"])
    P.add("dve", lambda e: e.tensor_tensor(der[:, T1:T1 + 8], der[:, T1:T1 + 8], der[:, T3:T3 + 8], ALU.mult), R=["der_t1", "der_t3"], W=["der_t1"])
    P.add("dve", lambda e: e.tensor_tensor(der[:, T2:T2 + 8], der[:, T2:T2 + 8], der[:, T3:T3 + 8], ALU.mult), R=["der_t2", "der_t3"], W=["der_t2"])
    P.add("dve", lambda e: e.tensor_tensor(der[:, T3:T3 + 8], der[:, T1:T1 + 8], der[:, T2:T2 + 8], ALU.add), R=["der_t1", "der_t2", "der_t3"], W=["der_t3"])
    P.add("dve", lambda e: e.tensor_tensor(der[:, LB:LB + 8], der[:, T3:T3 + 8], der[:, T1:T1 + 8], ALU.subtract), R=["der_t1", "der_t3"], W=["der_lb"])
    P.add("act", lambda e: e.activation(der[:, LNOM:LNOM + 8], der[:, LB:LB + 8], AF.Ln, scale=-1.0, bias=1.0), R=["der_lb"], W=["der_lnom"])

    slot_rr = [0]

    def slot():
        i = slot_rr[0] % NSLOT
        slot_rr[0] += 1
        return i

    ps_rr = {}

    def psum(tag, banks):
        i = ps_rr.get(tag, 0)
        ps_rr[tag] = i + 1
        return banks[i % len(banks)]

    def act(out, in_, func, R, Wr, **kw):
        P.add("act", lambda e: e.activation(out, in_, func, **kw), R=R, W=Wr)

    NB = 2

    def norm_sq(c, N):
        sq = xsq[:, c % 2, :N]
        P.add("act", lambda e: e.activation(sq, xt[:, c, :N], AF.Square), R=["xt%d" % c], W=["xsq%d" % (c % 2)])

    def norm_mm(c, N):
        sq = xsq[:, c % 2, :N]
        P.add("pe", lambda e: e.matmul(ps[NB][:, :N], ones[:], sq, start=(c == 0), stop=(c == 7)), R=["ones", "xsq%d" % (c % 2)], W=[psb[NB]])

    def norm_step(c, N):
        norm_sq(c, N)
        norm_mm(c, N)

    def norm_finish(N, gcol, out_final, fs=None, after=None):
        P.add("dve", lambda e: e.tensor_scalar(stat[:, 0, :N], ps[NB][:, :N], 1.0 / D, EPS, ALU.mult, ALU.add), R=[psb[NB]], W=["stat0"])
        act(stat[:, 0, :N], stat[:, 0, :N], AF.Ln, ["stat0"], ["stat0"])
        act(stat[:, 1, :N], stat[:, 0, :N], AF.Exp, ["stat0"], ["stat1"], scale=-0.5)
        for c in range(8):
            if out_final:
                P.add("dve", lambda e: e.scalar_tensor_tensor(slots[:, fs[c], :N], xt[:, c, :N], pvec[:, gcol + c:gcol + c + 1], stat[:, 1, :N], ALU.mult, ALU.mult),
                      R=["xt%d" % c, "stat1", "pvec"], W=["slot%d" % fs[c]])
            else:
                P.add("dve", lambda e: e.scalar_tensor_tensor(xn[:, c, :N], xt[:, c, :N], pvec[:, gcol + c:gcol + c + 1], stat[:, 1, :N], ALU.mult, ALU.mult),
                      R=["xt%d" % c, "stat1", "pvec"], W=["xn%d" % c])
            if after is not None:
                after(c)

    def norm(N, gcol, out_final, fs=None):
        for c in range(8):
            norm_step(c, N)
        norm_finish(N, gcol, out_final, fs)

    XN = ["xn%d" % c for c in range(8)]

    class WStream:
        def __init__(self):
            self.q = []
            self.idx = {}
            self.nxt = 0
            self.rings = {}

        def ring(self, name, bufnames):
            self.rings[name] = dict(names=bufnames, owner=[None] * len(bufnames), rr=0)

        def plan(self, key, ring, fn, ndma):
            it = dict(key=key, ring=ring, fn=fn, ndma=ndma, slot=None, issued=False)
            self.q.append(it)
            self.idx[key] = it

        def pump(self):
            while self.nxt < len(self.q):
                it = self.q[self.nxt]
                r = self.rings[it["ring"]]
                i = r["rr"] % len(r["names"])
                if r["owner"][i] is not None:
                    break
                r["owner"][i] = it["key"]
                r["rr"] += 1
                it["slot"] = i
                bname = r["names"][i]
                fn = it["fn"]
                P.add("pool", lambda e: fn(e, i), W=[bname], dma=B(bname), ndma=it["ndma"], bar=False)
                it["issued"] = True
                self.nxt += 1

        def get(self, key):
            self.pump()
            it = self.idx[key]
            assert it["issued"], ("weight piece not loadable yet", key)
            return it["slot"], self.rings[it["ring"]]["names"][it["slot"]]

        def release(self, key):
            it = self.idx[key]
            r = self.rings[it["ring"]]
            assert r["owner"][it["slot"]] == key
            r["owner"][it["slot"]] = None
            self.pump()

    WS = WStream()
    WS.ring("A", ["WA%d" % i for i in range(4)])
    WS.ring("O", ["WO%d" % i for i in range(5)])
    WS.ring("D", ["WD%d" % i for i in range(2)])
    WS.ring("B", ["WB%d" % i for i in range(2)])
    WS.ring("V", ["WV0", "WV1"])

    def layer0(ti, N, nseq, T, sample, last_prompt):
        save_ptr = A.ptr
        xbb = A([128, 12, nseq, 3 + T], BF16)
        xcb = A([128, 2, 3, N], BF16)
        xcf = A([128, 2, 3, N], F32)
        if sample:
            h0s = A([128, 12, NSEQ], F32)
            tmp0 = A([128, NSEQ], F32)
            ld(h0s[:], h0_d, "h0s")
        wa = {}
        st = {}

        def p1(g, j, split=False):
            wi, wname = WS.get(("X", ti, g))
            ch = 3 * g + j
            xs = g % 2
            bk = psum("xb", [0, 1])
            P.add("pe", lambda e: [e.matmul(ps[bk][:, :N], WA[:, wi, c, j * 128:(j + 1) * 128], xn[:, c, :N], start=(c == 0), stop=(c == 7)) for c in range(8)],
                  R=[wname] + XN, W=[psb[bk]])
            xbname = "xbb%d" % ch
            src3 = ps[bk][:, :N].rearrange("p (s t) -> p s t", s=nseq)
            if sample:
                P.add("dve", lambda e: e.tensor_copy(xbb[:, ch, :, 0:3], cv0s[:, ch, :, :]), R=["cv0s"], W=[xbname])
            else:
                P.add("dve", lambda e: e.tensor_copy(xbb[:, ch, 0, 0:3], halo[:, ch, 0:3]), R=["halo%d" % ch], W=[xbname])
            P.add("dve", lambda e: e.tensor_copy(xbb[:, ch, :, 3:3 + T], src3), R=[psb[bk]], W=[xbname])
            if sample:
                P.add("dve", lambda e: e.tensor_copy(cstage[:, ch, 1:17, :], src3[:, :, 1:4]), R=[psb[bk]], W=["cstage"])
            else:
                P.add("pool", lambda e: e.tensor_copy(halo[:, ch, 0:3], xbb[:, ch, 0, T:T + 3]), R=[xbname], W=["halo%d" % ch])
                if last_prompt:
                    P.add("dve", lambda e: e.tensor_copy(cstage[:, ch, 0, :], ps[bk][:, N - 3:N]), R=[psb[bk]], W=["cstage"])
            if j == 2:
                WS.release(("X", ti, g))
            if not split:
                p1b(g, j)

        def p1b(g, j):
            di, dname = WS.get(("D", ti, g))
            ch = 3 * g + j
            xs = g % 2
            xbname = "xbb%d" % ch
            cb = psum("xc", [2, 3, 4])
            dst3 = ps[cb][:, :N].rearrange("p (s t) -> p s t", s=nseq)
            P.add("pe", lambda e: [e.matmul(dst3, WD[:, di, j * 4 + k, :], xbb[:, ch, :, k:k + T], start=(k == 0), stop=(k == 3)) for k in range(4)],
                  R=[xbname, dname], W=[psb[cb]])
            P.add("dve", lambda e: e.tensor_scalar(xcf[:, xs, j, :], ps[cb][:, :N], pvec[:, CB + ch:CB + ch + 1], None, ALU.add),
                  R=[psb[cb], "pvec"], W=["xcf%d_%d" % (xs, j)])
            P.add("dve", lambda e: e.tensor_copy(xcb[:, xs, j, :], xcf[:, xs, j, :]), R=["xcf%d_%d" % (xs, j)], W=["xcb%d_%d" % (xs, j)])
            if j == 2:
                WS.release(("D", ti, g))

        def gm(n):
            g, j = divmod(n, 3)
            wi, wname = WS.get(("G", ti, g))
            bi, bdname = WS.get(("B", ti, g))
            xs = g % 2
            XC = ["xcb%d_%d" % (xs, jj) for jj in range(3)]
            rb, ib, gb = 5, 6, 7
            P.add("pe", lambda e: [e.matmul(ps[rb][:, :N], WBD[:, bi, 0, jp, j * 128:(j + 1) * 128], xcb[:, xs, jp, :], start=(jp == 0), stop=(jp == 2)) for jp in range(3)],
                  R=XC + [bdname], W=[psb[rb]])
            P.add("pe", lambda e: [e.matmul(ps[ib][:, :N], WBD[:, bi, 1, jp, j * 128:(j + 1) * 128], xcb[:, xs, jp, :], start=(jp == 0), stop=(jp == 2)) for jp in range(3)],
                  R=XC + [bdname], W=[psb[ib]])
            P.add("pe", lambda e: [e.matmul(ps[gb][:, :N], WA[:, wi, c, j * 128:(j + 1) * 128], xn[:, c, :N], start=(c == 0), stop=(c == 7)) for c in range(8)],
                  R=[wname] + XN, W=[psb[gb]])
            if j == 2:
                WS.release(("G", ti, g))
                WS.release(("B", ti, g))

        def stage_a(n):
            ch = n
            rb, ib, gb = 5, 6, 7
            sl = [slot() for _ in range(5)]
            nm = ["slot%d" % s_ for s_ in sl]
            vv = [slots[:, s_, :N] for s_ in sl]
            st[n] = (nm, vv)
            (n1, n2, n3, n4, n5), (v1, v2, v3, v4, v5) = nm, vv
            act(v1, ps[rb][:, :N], AF.Exp, [psb[rb], "der_a"], [n1], scale=-1.0, bias=der[:, NBR + ch:NBR + ch + 1])
            act(v4, ps[ib][:, :N], AF.Exp, [psb[ib], "der_a"], [n4], scale=-1.0, bias=der[:, NBI + ch:NBI + ch + 1])
            act(v5, ps[gb][:, :N], AF.Exp, [psb[gb]], [n5], scale=-1.0)
            act(v1, v1, AF.Ln, [n1], [n1], bias=1.0)
            act(v4, v4, AF.Ln, [n4], [n4], bias=1.0)
            act(v5, v5, AF.Ln, [n5], [n5], bias=1.0)
            act(v1, v1, AF.Exp, [n1], [n1], scale=-1.0)
            act(v5, v5, AF.Exp, [n5], [n5], scale=-1.0)
            act(v2, v1, AF.Exp, [n1, "der_c"], [n2], scale=der[:, CC + ch:CC + ch + 1])
            P.add("dve", lambda e: e.tensor_tensor(v5, v5, ps[gb][:, :N], ALU.mult), R=[n5, psb[gb]], W=[n5])
            act(v3, v1, AF.Exp, [n1, "der_c2"], [n3], scale=der[:, C2 + ch:C2 + ch + 1])
            act(v3, v3, AF.Ln, [n3], [n3], scale=-1.0, bias=1.0)
            P.add("dve", lambda e: e.scalar_tensor_tensor(v3, v3, 0.5, v4, ALU.mult, ALU.subtract), R=[n3, n4], W=[n3])

        def stage_b(n):
            ch = n
            g, j = divmod(n, 3)
            xs = g % 2
            (n1, n2, n3, n4, n5), (v1, v2, v3, v4, v5) = st[n]
            act(v3, v3, AF.Exp, [n3], [n3])
            P.add("dve", lambda e: e.tensor_tensor(v3, v3, xcf[:, xs, j, :], ALU.mult), R=[n3, "xcf%d_%d" % (xs, j)], W=[n3])
            if sample:
                a3 = v2.rearrange("p (s t) -> p s t", s=nseq)
                b3 = v3.rearrange("p (s t) -> p s t", s=nseq)
                P.add("dve", lambda e: e.tensor_tensor(tmp0[:, :], a3[:, :, 0], h0s[:, ch, :], ALU.mult), R=[n2, "h0s"], W=["tmp0"])
                P.add("dve", lambda e: e.tensor_tensor(b3[:, :, 0], b3[:, :, 0], tmp0[:, :], ALU.add), R=[n3, "tmp0"], W=[n3])
                P.add("dve", lambda e: e.memset(a3[:, :, 0], 0.0), R=["tmp0"], W=[n2])
                P.add("dve", lambda e: e.tensor_tensor_scan(v1, v2, v3, 0.0, ALU.mult, ALU.add), R=[n2, n3], W=[n1])
                h3 = v1.rearrange("p (s t) -> p s t", s=nseq)
                P.add("dve", lambda e: e.tensor_copy(hstage[:, ch, 1:17], h3[:, :, T - 1]), R=[n1], W=["hstage"])
            else:
                P.add("dve", lambda e: e.tensor_tensor_scan(v1, v2, v3, hcar[:, ch:ch + 1], ALU.mult, ALU.add), R=[n2, n3, "hcar%d" % ch], W=[n1])
                P.add("pool", lambda e: e.tensor_copy(hcar[:, ch:ch + 1], v1[:, N - 1:N]), R=[n1], W=["hcar%d" % ch])
                if last_prompt:
                    P.add("pool", lambda e: e.tensor_copy(hstage[:, ch, 0:1], v1[:, N - 1:N]), R=[n1], W=["hstage"])
            P.add("pool" if n < 10 else "dve", lambda e: e.tensor_tensor(ybuf[:, ch, :N], v1, v5, ALU.mult), R=[n1, n5], W=["y%d" % ch])

        for j in range(3):
            p1(0, j, split=True)
        for j in range(3):
            p1b(0, j)
        for n in range(12):
            g, j = divmod(n, 3)
            gm(n)
            if g + 1 < 4:
                p1(g + 1, j)
            stage_a(n)
            if n >= 1:
                stage_b(n - 1)
        stage_b(11)
        YB = ["y%d" % ch for ch in range(12)]
        for c in range(8):
            wo, woname = WS.get(("O0", ti, c))
            bk = psum("xb", [0, 1])
            P.add("pe", lambda e: [e.matmul(ps[bk][:, :N], WO[:, wo, ch, :], ybuf[:, ch, :N], start=(ch == 0), stop=(ch == 11)) for ch in range(12)],
                  R=[woname] + YB, W=[psb[bk]])
            WS.release(("O0", ti, c))
            if c > 0:
                norm_mm(c - 1, N)
            P.add("dve", lambda e: e.tensor_tensor(xt[:, c, :N], xt[:, c, :N], ps[bk][:, :N], ALU.add), R=["xt%d" % c, psb[bk]], W=["xt%d" % c])
            norm_sq(c, N)
        norm_mm(7, N)
        A.ptr = save_ptr

    def layer1(ti, N, nseq, T, sample):
        save_ptr = A.ptr
        nst = max(1, N // 128)
        SW = min(N, 128)
        C = 64 if not sample else T
        nch = N // C
        cps = SW // C
        Qd = A([128, 8, N], BF16)
        Kt = A([128, 8, N], BF16)
        Kl = A([128, 8, N], BF16)
        KlT = A([128, nst, 1024], BF16)
        V = A([128, nst, 1024], BF16)
        sgate = A([128, 8, N], BF16)
        ATb = A([128, 8, 128], BF16)
        osq = A([128, 2, 512], BF16)
        elast = A([128, 8, nch], F32)
        St = A([128, 8, 128], F32)
        if sample:
            NSB = 4
            S0f = A([128, NSB, 8, 128], F32)
            S0b = A([128, NSB, 8, 128], BF16)
            KlTm = A([128, 2, 1024], BF16)
            Sout = A([128, 2, 8, 128], F32)

            def load_s0(j):
                sb = j % NSB
                P.add("sp", lambda e: [e.dma_start(out=S0f[:, sb], in_=s0_d[:, j])], W=["S0f%d" % sb], dma=B("S0f%d" % sb))
                P.add("pool", lambda e: [e.dma_start(out=S0b[:, sb], in_=s0_d[:, j])], W=["S0b%d" % sb], dma=B("S0b%d" % sb))
            for j in range(NSB):
                load_s0(j)
        norm_finish(N, G1, False)
        cm = cmS if sample else cmP
        maskb = maskS if sample else maskP
        st = {}
        QB, FB, GB = [0, 3, 6, 7], [1, 4], [2, 5]

        def pm(h):
            wi, wname = WS.get(("H", ti, h))
            for (bk, off) in ((QB[h % 4], 0), (FB[h % 2], 128), (GB[h % 2], 256)):
                P.add("pe", lambda e: [e.matmul(ps[bk][:, :N], WA[:, wi, c, off:off + 128], xn[:, c, :N], start=(c == 0), stop=(c == 7)) for c in range(8)],
                      R=[wname] + XN, W=[psb[bk]])
            WS.release(("H", ti, h))

        def stage_a(h):
            qb, fb, gb = QB[h % 4], FB[h % 2], GB[h % 2]
            sl = [slot() for _ in range(5)]
            nm = ["slot%d" % s_ for s_ in sl]
            vv = [slots[:, s_, :N] for s_ in sl]
            st[h] = (nm, vv)
            (n1, n2, n3, n4, n5), (v1, v2, v3, v4, v5) = nm, vv
            act(v1, ps[qb][:, :N], AF.Exp, [psb[qb]], [n1], scale=-1.0)
            act(v2, ps[fb][:, :N], AF.Exp, [psb[fb]], [n2], scale=-1.0)
            act(v5, ps[gb][:, :N], AF.Exp, [psb[gb]], [n5], scale=-1.0)
            act(v1, v1, AF.Ln, [n1], [n1], bias=1.0)
            act(v3, v2, AF.Ln, [n2, "der_lb"], [n3], scale=der[:, LB + h:LB + h + 1], bias=1.0)
            act(v2, v2, AF.Ln, [n2], [n2], bias=1.0)
            act(v5, v5, AF.Ln, [n5], [n5], bias=1.0)
            act(v5, v5, AF.Exp, [n5], [n5], scale=-1.0)
            P.add("dve", lambda e: e.tensor_tensor(v3, v3, v2, ALU.subtract), R=[n2, n3], W=[n3])
            P.add("dve", lambda e: e.tensor_tensor(v2, v2, ps[fb][:, :N], ALU.add), R=[n2, psb[fb]], W=[n2])
            P.add("dve", lambda e: e.tensor_tensor_scan(v4, cm[:, :N], v3, 0.0, ALU.mult, ALU.add), R=[n3, "cst"], W=[n4])
            P.add("dve", lambda e: e.tensor_tensor(v2, v2, v4, ALU.add), R=[n2, n4], W=[n2])
            P.add("dve", lambda e: e.tensor_tensor(v1, v4, v1, ALU.subtract), R=[n1, n4], W=[n1])
            P.add("dve", lambda e: e.scalar_tensor_tensor(sgate[:, h, :], ps[gb][:, :N], pvec[:, OG + h:OG + h + 1], v5, ALU.mult, ALU.mult),
                  R=[n5, psb[gb], "pvec"], W=["sgate%d" % h])

        def stage_b(h):
            qb = QB[h % 4]
            (n1, n2, n3, n4, n5), (v1, v2, v3, v4, v5) = st[h]
            c3 = v4.rearrange("p (c t) -> p c t", t=C)
            act(elast[:, h, :], c3[:, :, C - 1], AF.Exp, [n4], ["elast%d" % h])
            act(Kt[:, h, :], v2, AF.Exp, [n2, "der_lnom"], ["Kt%d" % h], scale=-1.0, bias=der[:, LNOM + h:LNOM + h + 1])
            act(v1, v1, AF.Exp, [n1], [n1])
            P.add("dve", lambda e: e.tensor_tensor(Qd[:, h, :], v1, ps[qb][:, :N], ALU.mult), R=[n1, psb[qb]], W=["Qd%d" % h])
            if sample:
                P.add("dve", lambda e: [e.tensor_scalar(Kl[:, h, c * C:(c + 1) * C], Kt[:, h, c * C:(c + 1) * C], elast[:, h, c:c + 1], None, ALU.mult) for c in range(nch)],
                      R=["Kt%d" % h, "elast%d" % h], W=["Kl%d" % h])
            else:
                P.add("dve", lambda e: e.tensor_tensor(Kl[:, h, :].rearrange("p (c t) -> p c t", t=C), Kt[:, h, :].rearrange("p (c t) -> p c t", t=C),
                                                       elast[:, h, :].unsqueeze(2).broadcast_to([128, nch, C]), ALU.mult),
                      R=["Kt%d" % h, "elast%d" % h], W=["Kl%d" % h])

        for h in range(8):
            pm(h)
            stage_a(h)
            if h >= 1:
                stage_b(h - 1)
        stage_b(7)
        vi = 0
        for q in range(4):
            wv, wvname = WS.get(("V", ti, q))
            for s_i in range(nst):
                bk = [4, 5][vi % 2]
                P.add("pe", lambda e: [e.matmul(ps[bk][:SW, 0:256], xn[:, c, s_i * SW:(s_i + 1) * SW], (WV[:, wv, c, :] if q < 2 else WA[:, wv, c, 0:256]), start=(c == 0), stop=(c == 7)) for c in range(8)],
                      R=[wvname] + XN, W=[psb[bk]])
                if vi % 2 == 0:
                    P.add("act", lambda e: e.activation(V[:SW, s_i, q * 256:(q + 1) * 256], ps[bk][:SW, 0:256], AF.Copy), R=[psb[bk]], W=["V%d" % s_i])
                else:
                    P.add("dve", lambda e: e.tensor_copy(V[:SW, s_i, q * 256:(q + 1) * 256], ps[bk][:SW, 0:256]), R=[psb[bk]], W=["V%d" % s_i])
                vi += 1
            WS.release(("V", ti, q))
        KLN = ["Kl%d" % h for h in range(8)]
        for s_i in range(nst):
            for half in range(2):
                bk = [6, 7][vi % 2]
                pst = pstv[bk]
                P.add("pe", lambda e: [e.transpose(pst[:SW, hh * 128:(hh + 1) * 128], Kl[:, half * 4 + hh, s_i * SW:(s_i + 1) * SW], ident[:]) for hh in range(4)],
                      R=KLN + ["ident"], W=[psb[bk]])
                if vi % 2 == 0:
                    P.add("act", lambda e: e.activation(KlT[:SW, s_i, half * 512:(half + 1) * 512], pst[:SW, 0:512], AF.Copy), R=[psb[bk]], W=["KlT%d" % s_i])
                else:
                    P.add("dve", lambda e: e.tensor_copy(KlT[:SW, s_i, half * 512:(half + 1) * 512], pst[:SW, 0:512]), R=[psb[bk]], W=["KlT%d" % s_i])
                vi += 1
        ost = {}

        def opath_a(s_i):
            OB = [6, 7] if s_i % 2 == 0 else [0, 1]
            for half in range(2):
                ob = OB[half]
                nb = 4 + half
                sr, s1_ = slot(), slot()
                ost[(s_i, half)] = (sr, s1_)
                ow = ps[ob][:, :].rearrange("p (h t) -> p h t", h=4)[:, :, :SW]
                osq3 = osq[:, half, :].rearrange("p (h t) -> p h t", h=4)[:, :, :SW]
                nw = ps[nb][:, :].rearrange("p (h t) -> p h t", h=4)[:, :, :SW]
                P.add("act", lambda e: e.activation(osq3, ow, AF.Square), R=[psb[ob]], W=["osq%d" % half])
                if SW == 128:
                    P.add("pe", lambda e: e.matmul(nw, ones[:], osq3, start=True, stop=True, skip_group_check=True), R=["osq%d" % half, "ones"], W=[psb[nb]])
                else:
                    P.add("pe", lambda e: [e.matmul(ps[nb][:, hh * 128:hh * 128 + SW], ones[:], osq[:, half, hh * 128:hh * 128 + SW], start=True, stop=True, skip_group_check=True) for hh in range(4)],
                          R=["osq%d" % half, "ones"], W=[psb[nb]])

        def opath_b1(s_i):
            for half in range(2):
                nb = 4 + half
                sr, s1_ = ost[(s_i, half)]
                nr = "slot%d" % sr
                nw = ps[nb][:, :].rearrange("p (h t) -> p h t", h=4)[:, :, :SW]
                rs3 = slots[:, sr, :].rearrange("p (h t) -> p h t", h=4)[:, :, :SW]
                P.add("dve", lambda e: e.tensor_scalar(rs3, nw, 1.0 / 128.0, EPS, ALU.mult, ALU.add), R=[psb[nb]], W=[nr])
                act(rs3, rs3, AF.Ln, [nr], [nr])
                act(rs3, rs3, AF.Exp, [nr], [nr], scale=-0.5)

        def opath_b2(s_i):
            OB = [6, 7] if s_i % 2 == 0 else [0, 1]
            t0_ = s_i * SW
            for half in range(2):
                ob = OB[half]
                sr, s1_ = ost[(s_i, half)]
                nr, n1_ = "slot%d" % sr, "slot%d" % s1_
                ow = ps[ob][:, :].rearrange("p (h t) -> p h t", h=4)[:, :, :SW]
                rs3 = slots[:, sr, :].rearrange("p (h t) -> p h t", h=4)[:, :, :SW]
                t13 = slots[:, s1_, :].rearrange("p (h t) -> p h t", h=4)[:, :, :SW]
                P.add("dve", lambda e: e.tensor_tensor(t13, ow, rs3, ALU.mult), R=[psb[ob], nr], W=[n1_])
                P.add("dve", lambda e: e.tensor_tensor(ybuf[:, half * 4:half * 4 + 4, t0_:t0_ + SW], t13, sgate[:, half * 4:half * 4 + 4, t0_:t0_ + SW], ALU.mult),
                      R=[n1_] + ["sgate%d" % (half * 4 + hh) for hh in range(4)], W=["y1_%d_%d" % (half, s_i)])

        for s_i in range(nst):
            t0 = s_i * SW
            OB = [6, 7] if s_i % 2 == 0 else [0, 1]
            for half in range(2):
                atb = 4 + half
                for hh in range(4):
                    h = half * 4 + hh
                    P.add("pe", lambda e: e.matmul(ps[atb][:SW, hh * 128:hh * 128 + SW], Kt[:, h, t0:t0 + SW], Qd[:, h, t0:t0 + SW], start=True, stop=True, skip_group_check=True),
                          R=["Kt%d" % h, "Qd%d" % h], W=[psb[atb]])
                for hh in range(4):
                    h = half * 4 + hh
                    P.add("dve", lambda e: e.tensor_tensor(ATb[:SW, h, :SW], ps[atb][:SW, hh * 128:hh * 128 + SW], maskb[:SW, :SW], ALU.mult),
                          R=[psb[atb], "maskP", "maskS"], W=["AT%d" % h])
            for half in range(2):
                ob = OB[half]
                for hh in range(4):
                    h = half * 4 + hh
                    P.add("pe", lambda e: e.matmul(ps[ob][:, hh * 128:hh * 128 + SW], V[:SW, s_i, h * 128:(h + 1) * 128], ATb[:SW, h, :SW], start=(hh == 0), stop=False, skip_group_check=True),
                          R=["V%d" % s_i, "AT%d" % h], W=[psb[ob]])
            if s_i > 0:
                opath_a(s_i - 1)
            if not sample:
                for cc in range(cps):
                    ci = s_i * cps + cc
                    p0 = cc * C
                    UB = [2, 3]
                    for half in range(2):
                        hs = slice(half * 4, half * 4 + 4)
                        HN = ["%d" % (half * 4 + hh) for hh in range(4)]
                        ebc = elast[:, hs, ci:ci + 1].broadcast_to([128, 4, 128])
                        P.add("dve", lambda e: e.tensor_tensor(St[:, hs, :], S[:, hs, :], ebc, ALU.mult),
                              R=["S" + x for x in HN] + ["elast" + x for x in HN], W=["St%d" % half])
                    for half in range(2):
                        ob = OB[half]
                        for hh in range(4):
                            h = half * 4 + hh
                            P.add("pe", lambda e: e.matmul(ps[ob][:, hh * 128 + p0:hh * 128 + p0 + C], Sb[:, h, :], Qd[:, h, t0 + p0:t0 + p0 + C], start=False, stop=(cc == cps - 1), skip_group_check=True),
                                  R=["Sb%d" % h, "Qd%d" % h], W=[psb[ob]])
                    for half in range(2):
                        ub = UB[half]
                        for hh in range(4):
                            h = half * 4 + hh
                            P.add("pe", lambda e: e.matmul(ps[ub][:, hh * 128:(hh + 1) * 128], KlT[p0:p0 + C, s_i, h * 128:(h + 1) * 128], V[p0:p0 + C, s_i, h * 128:(h + 1) * 128], start=True, stop=True, skip_group_check=True),
                                  R=["KlT%d" % s_i, "V%d" % s_i], W=[psb[ub]])
                    for half in range(2):
                        ub = UB[half]
                        hs = slice(half * 4, half * 4 + 4)
                        HN = ["%d" % (half * 4 + hh) for hh in range(4)]
                        u3 = ps[ub][:, :].rearrange("p (h v) -> p h v", h=4)
                        P.add("dve", lambda e: e.tensor_tensor(S[:, hs, :], St[:, hs, :], u3, ALU.add),
                              R=[psb[ub], "St%d" % half], W=["S" + x for x in HN])
                        P.add("act", lambda e: e.activation(Sb[:, hs, :], S[:, hs, :], AF.Copy), R=["S" + x for x in HN], W=["Sb" + x for x in HN])
                    if s_i > 0:
                        if cc == 0:
                            opath_b1(s_i - 1)
                        elif cc == cps - 1:
                            opath_b2(s_i - 1)
            else:
                for j in range(NSEQ):
                    sb = j % NSB
                    so = j % 2
                    UB = [2, 3] if j % 2 == 0 else [4, 5]
                    P.add("dve", lambda e: e.tensor_scalar(KlTm[:SW, so, :], KlT[:SW, 0, :], rowmask[:SW, j:j + 1], None, ALU.mult),
                          R=["KlT0", "cst"], W=["KlTm%d" % so])
                    for half in range(2):
                        hs = slice(half * 4, half * 4 + 4)
                        ebc = elast[:, hs, j:j + 1].broadcast_to([128, 4, 128])
                        P.add("dve", lambda e: e.tensor_tensor(St[:, hs, :], S0f[:, sb, hs, :], ebc, ALU.mult),
                              R=["S0f%d" % sb] + ["elast%d" % (half * 4 + hh) for hh in range(4)], W=["St%d" % half])
                    for half in range(2):
                        ob = OB[half]
                        for hh in range(4):
                            h = half * 4 + hh
                            P.add("pe", lambda e: e.matmul(ps[ob][:, hh * 128 + j * T:hh * 128 + (j + 1) * T], S0b[:, sb, h, :], Qd[:, h, j * T:(j + 1) * T], start=False, stop=(j == NSEQ - 1), skip_group_check=True),
                                  R=["S0b%d" % sb, "Qd%d" % h], W=[psb[ob]])
                    for half in range(2):
                        ub = UB[half]
                        for hh in range(4):
                            h = half * 4 + hh
                            P.add("pe", lambda e: e.matmul(ps[ub][:, hh * 128:(hh + 1) * 128], KlTm[:SW, so, h * 128:(h + 1) * 128], V[:SW, 0, h * 128:(h + 1) * 128], start=True, stop=True, skip_group_check=True),
                                  R=["KlTm%d" % so, "V0"], W=[psb[ub]])
                    for half in range(2):
                        ub = UB[half]
                        hs = slice(half * 4, half * 4 + 4)
                        u3 = ps[ub][:, :].rearrange("p (h v) -> p h v", h=4)
                        P.add("dve", lambda e: e.tensor_tensor(Sout[:, so, hs, :], St[:, hs, :], u3, ALU.add),
                              R=[psb[ub], "St%d" % half], W=["Sout%d_%d" % (so, half)])
                    P.add("sp", lambda e: [e.dma_start(out=sout_d[:, 1 + j], in_=Sout[:, so])], R=["Sout%d_0" % so, "Sout%d_1" % so], dma=B("Sout%d" % so))
                    if j + NSB < NSEQ:
                        load_s0(j + NSB)
        opath_a(nst - 1)
        opath_b1(nst - 1)
        opath_b2(nst - 1)
        YB = ["y1_%d_%d" % (half, s_i) for half in range(2) for s_i in range(nst)]
        for c in range(8):
            wo, woname = WS.get(("O1", ti, c))
            bk = psum("xb", [0, 1])
            P.add("pe", lambda e: [e.matmul(ps[bk][:, :N], WO[:, wo, hc, :], ybuf[:, hc, :N], start=(hc == 0), stop=(hc == 7)) for hc in range(8)],
                  R=[woname] + YB, W=[psb[bk]])
            WS.release(("O1", ti, c))
            if c > 0:
                norm_mm(c - 1, N)
            P.add("dve", lambda e: e.tensor_tensor(xt[:, c, :N], xt[:, c, :N], ps[bk][:, :N], ALU.add), R=["xt%d" % c, psb[bk]], W=["xt%d" % c])
            norm_sq(c, N)
        norm_mm(7, N)
        A.ptr = save_ptr

    pstv = {6: ps[6].bitcast(BF16), 7: ps[7].bitcast(BF16)}

    tiles = [(i * NT, NT, 1, NT, False) for i in range(SEQ // NT)] + [(SEQ, NS, NSEQ, 4, True)]
    XT = ["xt%d" % c for c in range(8)]
    for ti in range(len(tiles)):
        for g in range(4):
            WS.plan(("X", ti, g), "A", (lambda g: lambda e, i: [e.dma_start(out=WA[:, i, :, :], in_=w_in0_v[:, :, 384 * g:384 * g + 384])])(g), 1)
            WS.plan(("D", ti, g), "D", (lambda g: lambda e, i: [e.dma_start(out=WD[:, i], in_=diagw_d[:, 12 * g:12 * g + 12, :])])(g), 1)
            WS.plan(("G", ti, g), "A", (lambda g: lambda e, i: [e.dma_start(out=WA[:, i, :, :], in_=w_in0_v[:, :, W + 384 * g:W + 384 * g + 384])])(g), 1)
            WS.plan(("B", ti, g), "B", (lambda g: lambda e, i: [e.dma_start(out=WBD[:, i], in_=wbd_d[:, g])])(g), 1)
        for c in range(8):
            WS.plan(("O0", ti, c), "O", (lambda c: lambda e, i: [e.dma_start(out=WO[:, i, 0:12, :], in_=w_out0_v[:, :, c * 128:(c + 1) * 128])])(c), 1)
        for h in range(8):
            WS.plan(("H", ti, h), "A", (lambda h: lambda e, i: [e.dma_start(out=WA[:, i, :, k * 128:(k + 1) * 128], in_=w_in1_v[:, :, cb_ + h * 128:cb_ + (h + 1) * 128]) for k, cb_ in enumerate((0, 1024, 3072))])(h), 3)
        for q in range(4):
            if q < 2:
                WS.plan(("V", ti, q), "V", (lambda q: lambda e, i: [e.dma_start(out=WV[:, i], in_=w_in1_v[:, :, 2048 + q * 256:2048 + (q + 1) * 256])])(q), 1)
            else:
                WS.plan(("V", ti, q), "A", (lambda q: lambda e, i: [e.dma_start(out=WA[:, i, :, 0:256], in_=w_in1_v[:, :, 2048 + q * 256:2048 + (q + 1) * 256])])(q), 1)
        for c in range(8):
            WS.plan(("O1", ti, c), "O", (lambda c: lambda e, i: [e.dma_start(out=WO[:, i, 0:8, :], in_=w_out1_v[:, :, c * 128:(c + 1) * 128])])(c), 1)

    def load_x_chunk(c, n0, N):
        P.add("sp", lambda e: [e.dma_start(out=xt[:, c, :N], in_=xin_v[:, c, n0:n0 + N])], W=["xt%d" % c], dma=B("xtld%d" % c), bar=False)

    for c in range(8):
        load_x_chunk(c, tiles[0][0], tiles[0][1])
    WS.pump()
    P.barrier()
    norm(tiles[0][1], G0, False)
    for ti, (n0, N, nseq, T, sample) in enumerate(tiles):
        layer0(ti, N, nseq, T, sample, (not sample) and n0 + N == SEQ)
        P.barrier()
        layer1(ti, N, nseq, T, sample)
        fs = [slot() for _ in range(8)]
        nxt = tiles[ti + 1] if ti + 1 < len(tiles) else None
        norm_finish(N, GF, True, fs, after=(lambda c: load_x_chunk(c, nxt[0], nxt[1])) if nxt else None)
        P.add("sp", lambda e: [e.dma_start(out=yout_v[:, c, n0:n0 + N], in_=slots[:, fs[c], :N]) for c in range(8)], R=["slot%d" % s_ for s_ in fs], dma=B("xtst"), ndma=8, bar=False)
        if n0 + N == SEQ:
            P.add("sp", lambda e: [e.dma_start(out=sout_d[:, 0], in_=S[:])], R=["S%d" % h for h in range(8)], dma=B("Sst"))
        if nxt:
            norm(nxt[1], G0, False)
        P.barrier()
    P.add("sp", lambda e: [e.dma_start(out=hout_d, in_=hstage[:])], R=["hstage"], dma=B("hst"))
    P.add("sp", lambda e: [e.dma_start(out=cvout_d, in_=cstage[:])], R=["cstage"], dma=B("cst_out"))
    P.barrier(all_dma=True)
    return nc, P


_CACHE = {}


def _consts():
    c = np.zeros((128, 976), np.float32)
    c[:, 0:128] = np.eye(128, dtype=np.float32)
    s = np.arange(128)[:, None]
    t = np.arange(128)[None, :]
    c[:, 128:256] = ((s // 64 == t // 64) & (s <= t)).astype(np.float32)
    c[:, 256:384] = ((s // 4 == t // 4) & (s <= t)).astype(np.float32)
    cm = np.ones(512, np.float32)
    cm[::64] = 0.0
    c[:, 384:896] = cm[None, :]
    cs = np.ones(64, np.float32)
    cs[::4] = 0.0
    c[:, 896:960] = cs[None, :]
    p = np.arange(128)[:, None]
    j = np.arange(16)[None, :]
    c[:, 960:976] = ((p // 4 == j) & (p < 64)).astype(np.float32)
    return c


def _pc(v, n):
    return np.ascontiguousarray(np.asarray(v, np.float32).reshape(n, 128).T)


def kernel(x_prompt, x_sample, state_lru_h, state_lru_conv, state_hgrn, norm_gain, a_w_in, a_conv_w,
           a_conv_b, a_w_r, a_b_r, a_w_i, a_b_i, a_lambda, a_w_out, b_w_in, b_lb_logits, b_o_gain,
           b_w_out, final_gain):
    f = np.float32
    if "nc" not in _CACHE:
        nc, P = build_nc()
        _assign_and_emit(P)
        _CACHE["nc"] = nc
    nc = _CACHE["nc"]
    pvec = np.zeros((128, 96), f)
    pvec[:, 0:8] = _pc(norm_gain[0], 8)
    pvec[:, 8:16] = _pc(norm_gain[1], 8)
    pvec[:, 16:24] = _pc(final_gain, 8)
    pvec[:, 24:36] = _pc(a_conv_b[0], 12)
    pvec[:, 36:48] = _pc(a_b_r[0], 12)
    pvec[:, 48:60] = _pc(a_b_i[0], 12)
    pvec[:, 60:72] = _pc(a_lambda[0], 12)
    pvec[:, 72:80] = _pc(b_lb_logits[0], 8)
    pvec[:, 80:88] = _pc(b_lb_logits[1], 8)
    pvec[:, 88:96] = _pc(b_o_gain[0], 8)
    diagw = np.zeros((128, 48, 128), f)
    cw = np.asarray(a_conv_w[0], f)
    idx = np.arange(128)
    for ch in range(12):
        for k in range(4):
            diagw[idx, ch * 4 + k, idx] = cw[k, ch * 128:(ch + 1) * 128]
    wbd = np.zeros((128, 4, 2, 3, 384), f)
    for gi, wg in enumerate((np.asarray(a_w_r[0], f), np.asarray(a_w_i[0], f))):
        for g in range(4):
            full = np.zeros((384, 384), f)
            for b in range(4):
                full[96 * b:96 * b + 96, 96 * b:96 * b + 96] = wg[4 * g + b]
            wbd[:, g, gi] = full.reshape(3, 128, 384).transpose(1, 0, 2)
    cst = _consts()
    w_in0 = np.ascontiguousarray(a_w_in[0], f)
    w_out0 = np.ascontiguousarray(a_w_out[0], f)
    w_in1 = np.ascontiguousarray(b_w_in[0], f)
    w_out1 = np.ascontiguousarray(b_w_out[0], f)
    in_maps = []
    for b in range(NCORES):
        sl = slice(16 * b, 16 * b + 16)
        xs = np.asarray(x_sample[sl], f).reshape(64, D)
        xin = np.ascontiguousarray(np.concatenate([np.asarray(x_prompt[b], f), xs], axis=0).T)
        h0 = np.ascontiguousarray(np.asarray(state_lru_h[0, sl], f).reshape(16, 12, 128).transpose(2, 1, 0))
        cv0 = np.ascontiguousarray(np.asarray(state_lru_conv[0, sl], f).reshape(16, 3, 12, 128).transpose(3, 2, 0, 1))
        s0 = np.ascontiguousarray(np.asarray(state_hgrn[0, sl], f).transpose(2, 0, 1, 3))
        in_maps.append({"xin": xin, "w_in0": w_in0, "w_out0": w_out0, "w_in1": w_in1, "w_out1": w_out1,
                        "diagw": diagw, "wbd": wbd, "pvec": pvec, "cst": cst, "h0": h0, "cv0": cv0, "s0": s0})
    res = run_bass_kernel_spmd(nc, in_maps, core_ids=list(range(NCORES)))
    y_prompt = np.zeros((8, SEQ, D), f)
    y_sample = np.zeros((128, 4, D), f)
    hp = np.zeros((1, 8, W), f)
    bp = np.zeros((1, 8, 3, W), f)
    Sp = np.zeros((1, 8, 8, 128, 128), f)
    hs = np.zeros((1, 128, W), f)
    bs = np.zeros((1, 128, 3, W), f)
    Ss = np.zeros((1, 128, 8, 128, 128), f)
    for b in range(NCORES):
        r = res.results[b]
        sl = slice(16 * b, 16 * b + 16)
        yT = r["yout"]
        y_prompt[b] = yT[:, :SEQ].T
        y_sample[sl] = yT[:, SEQ:].T.reshape(16, 4, D)
        ho = r["hout"]
        hp[0, b] = ho[:, :, 0].T.reshape(W)
        hs[0, sl] = ho[:, :, 1:].transpose(2, 1, 0).reshape(16, W)
        co = r["cvout"]
        bp[0, b] = co[:, :, 0, :].transpose(2, 1, 0).reshape(3, W)
        bs[0, sl] = co[:, :, 1:, :].transpose(2, 3, 1, 0).reshape(16, 3, W)
        so = r["sout"]
        Sp[0, b] = so[:, 0].transpose(1, 0, 2)
        Ss[0, sl] = so[:, 1:].transpose(1, 2, 0, 3)
    return (y_prompt, y_sample, hp, bp, Sp, hs, bs, Ss)


def _assign_and_emit(P):
    P.emit()
```

```python
import numpy as np
import concourse.bass as bass
import concourse.mybir as mybir
from concourse.bass_utils import run_bass_kernel_spmd

F32 = mybir.dt.float32
BF16 = mybir.dt.bfloat16
AF = mybir.ActivationFunctionType
ALU = mybir.AluOpType

D = 1024
W = 1536
SEQ = 2048
NT = 512
NS = 64
NSEQ = 16
EPS = 1e-6
NCORES = 8
SB_BASE = 16512
SB_LIMIT = 16512 + 212863


class Buf:
    __slots__ = ("name", "lw", "rd", "sem", "dcount", "excl")

    def __init__(self, name):
        self.name = name
        self.excl = False
        self.lw = None
        self.rd = []
        self.sem = None
        self.dcount = 0


class Op:
    __slots__ = ("eng", "fn", "deps", "idx", "signal", "count", "dma", "buf", "reads", "writes", "ndma", "calls", "nobar")


class Rec:
    def __init__(self):
        self.calls = []

    def __getattr__(self, name):
        def f(*a, **k):
            self.calls.append((name, a, k))
            return len(self.calls) - 1
        return f


COMPUTE = ("pe", "act", "dve", "pool")
ENGS = ("pe", "act", "dve", "pool", "sp")


class Prog:
    def __init__(self, nc):
        self.nc = nc
        self.ops = {e: [] for e in ENGS}
        self.dma_since_barrier = []
        self.dma_nobar = []
        self.eng_obj = {"pe": nc.tensor, "act": nc.scalar, "dve": nc.vector, "pool": nc.gpsimd, "sp": nc.sync}
        self.bufs = {}

    def buf(self, name):
        b = self.bufs.get(name)
        if b is None:
            b = Buf(name)
            self.bufs[name] = b
        return b

    def add(self, eng, fn, R=(), W=(), dma=None, ndma=1, bar=True):
        op = Op()
        op.nobar = False
        op.eng = eng
        op.fn = fn
        rec = Rec()
        fn(rec)
        op.calls = rec.calls
        op.signal = False
        op.count = 0
        op.dma = dma is not None
        op.buf = dma
        if dma is not None:
            if dma.sem is None:
                dma.sem = self.nc.alloc_semaphore("dsem_" + dma.name)
            dma.dcount += 16 * ndma
            op.count = dma.dcount
            op.ndma = ndma
        R = [self.buf(b) if isinstance(b, str) else b for b in R]
        W = [self.buf(b) if isinstance(b, str) else b for b in W]
        op.reads = R
        op.writes = W
        deps = {}
        for b in R:
            if b.lw is not None:
                deps[id(b.lw)] = (b.lw, True)
            if b.excl:
                for r in b.rd:
                    if r.eng != eng and id(r) not in deps:
                        deps[id(r)] = (r, False)
        for b in W:
            if b.lw is not None and id(b.lw) not in deps:
                deps[id(b.lw)] = (b.lw, False)
            for r in b.rd:
                if id(r) not in deps:
                    deps[id(r)] = (r, False)
        for b in W:
            b.lw = op
            b.rd = []
        for b in R:
            if b.lw is not op:
                b.rd.append(op)
        op.deps = list(deps.values())
        op.idx = len(self.ops[eng])
        self.ops[eng].append(op)
        if op.dma:
            op.nobar = not bar
            if bar:
                self.dma_since_barrier.append(op)
            else:
                self.dma_nobar.append(op)
        return op

    def barrier(self, all_dma=False):
        if all_dma:
            self.dma_since_barrier += self.dma_nobar
            self.dma_nobar = []
        lasts = []
        for e in ENGS:
            for o in reversed(self.ops[e]):
                if o.dma and getattr(o, "nobar", False) and not all_dma:
                    continue
                lasts.append(o)
                break
        last_by_buf = {}
        for o in self.dma_since_barrier:
            last_by_buf[id(o.buf)] = o
        dmas = list(last_by_buf.values())
        self.dma_since_barrier = []
        for e in ENGS:
            op = Op()
            op.nobar = False
            op.eng = e
            op.fn = None
            op.signal = False
            op.count = 0
            op.dma = False
            op.buf = None
            op.reads = []
            op.writes = []
            op.deps = [(o, True) for o in lasts if o.eng != e or o.dma] + [(o, True) for o in dmas]
            op.idx = len(self.ops[e])
            self.ops[e].append(op)

    def emit(self):
        nc = self.nc
        def needs(op, dep, raw):
            if dep.dma:
                return True
            if dep.fn is None:
                return False
            if dep.eng != op.eng:
                return True
            if op.eng == "pe":
                return False
            return True

        for e in ENGS:
            for op in self.ops[e]:
                op.deps = [(d, r) for (d, r) in op.deps if needs(op, d, r)]
                for d, r in op.deps:
                    d.signal = True
        sems = {}
        for e in COMPUTE:
            sems[e] = nc.alloc_semaphore("sem_" + e)
            c = 0
            for op in self.ops[e]:
                if op.dma:
                    continue
                if op.signal and op.fn is not None:
                    c += 1
                    op.count = c
        for e in ENGS:
            eng = self.eng_obj[e]
            waited = {}
            for op in self.ops[e]:
                for d, r in op.deps:
                    if d.dma:
                        sem = d.buf.sem
                        val = d.count
                        if val == 0:
                            raise RuntimeError("dma dep emitted before producer: %s" % d.buf.name)
                    else:
                        sem = sems[d.eng]
                        val = d.count
                    key = id(sem)
                    if waited.get(key, 0) >= val:
                        continue
                    waited[key] = val
                    eng.wait_ge(sem, val)
                if op.fn is None:
                    continue
                res = [getattr(eng, name)(*a, **k) for (name, a, k) in op.calls]
                if op.dma:
                    assert len(res) == op.ndma, (op.buf.name, len(res), op.ndma)
                    for ins in res:
                        ins.then_inc(op.buf.sem, 16)
                elif op.signal:
                    res[-1].then_inc(sems[e], 1)


class Alloc:
    def __init__(self, nc):
        self.nc = nc
        self.ptr = SB_BASE
        self.n = 0

    def __call__(self, shape, dtype, at=None):
        esz = 2 if dtype == BF16 else 4
        nbytes = esz
        for s in shape[1:]:
            nbytes *= s
        nbytes = (nbytes + 31) // 32 * 32
        if at is None:
            at = self.ptr
            self.ptr += nbytes
        assert at + nbytes <= SB_LIMIT, ("sbuf overflow", at, nbytes)
        self.n += 1
        return self.nc.alloc_sbuf_tensor_at("t%d" % self.n, list(shape), dtype, offset=at)


def build_nc():
    nc = bass.Bass("TRN2", target_bir_lowering=False)
    NTOK = SEQ + NS
    xin = nc.dram_tensor("xin", [D, NTOK], F32, kind="ExternalInput").ap()
    w_in0 = nc.dram_tensor("w_in0", [D, 2 * W], F32, kind="ExternalInput").ap()
    w_out0 = nc.dram_tensor("w_out0", [W, D], F32, kind="ExternalInput").ap()
    w_in1 = nc.dram_tensor("w_in1", [D, 4 * D], F32, kind="ExternalInput").ap()
    w_out1 = nc.dram_tensor("w_out1", [D, D], F32, kind="ExternalInput").ap()
    diagw_d = nc.dram_tensor("diagw", [128, 48, 128], F32, kind="ExternalInput").ap()
    wbd_d = nc.dram_tensor("wbd", [128, 4, 2, 3, 384], F32, kind="ExternalInput").ap()
    pvec_d = nc.dram_tensor("pvec", [128, 96], F32, kind="ExternalInput").ap()
    cst_d = nc.dram_tensor("cst", [128, 976], F32, kind="ExternalInput").ap()
    h0_d = nc.dram_tensor("h0", [128, 12, NSEQ], F32, kind="ExternalInput").ap()
    cv0_d = nc.dram_tensor("cv0", [128, 12, NSEQ, 3], F32, kind="ExternalInput").ap()
    s0_d = nc.dram_tensor("s0", [128, NSEQ, 8, 128], F32, kind="ExternalInput").ap()
    yout = nc.dram_tensor("yout", [D, NTOK], F32, kind="ExternalOutput").ap()
    hout_d = nc.dram_tensor("hout", [128, 12, 17], F32, kind="ExternalOutput").ap()
    cvout_d = nc.dram_tensor("cvout", [128, 12, 17, 3], F32, kind="ExternalOutput").ap()
    sout_d = nc.dram_tensor("sout", [128, 17, 8, 128], F32, kind="ExternalOutput").ap()

    xin_v = xin.rearrange("(c p) n -> p c n", p=128)
    yout_v = yout.rearrange("(c p) n -> p c n", p=128)
    w_in0_v = w_in0.rearrange("(c p) n -> p c n", p=128)
    w_out0_v = w_out0.rearrange("(c p) n -> p c n", p=128)
    w_in1_v = w_in1.rearrange("(c p) n -> p c n", p=128)
    w_out1_v = w_out1.rearrange("(c p) n -> p c n", p=128)

    P = Prog(nc)
    A = Alloc(nc)
    B = P.buf

    xt = A([128, 8, NT], F32)
    xn = A([128, 8, NT], BF16)
    xsq = A([128, 2, NT], BF16)
    stat = A([128, 2, NT], F32)
    NSLOT = 15
    slots = A([128, NSLOT, NT], F32)
    ybuf = A([128, 12, NT], BF16)
    WA = A([128, 4, 8, 384], BF16)
    WV = A([128, 2, 8, 256], BF16)
    WO = A([128, 5, 12, 128], BF16)
    ident = A([128, 128], BF16)
    ones = A([128, 128], BF16)
    maskP = A([128, 128], BF16)
    maskS = A([128, 128], BF16)
    cst = A([128, 976], F32)
    pvec = A([128, 96], F32)
    der = A([128, 96], F32)
    halo = A([128, 12, 4], BF16)
    hcar = A([128, 12], F32)
    WD = A([128, 2, 12, 128], BF16)
    WBD = A([128, 2, 2, 3, 384], BF16)
    S = A([128, 8, 128], F32)
    Sb = A([128, 8, 128], BF16)
    hstage = A([128, 12, 17], F32)
    cstage = A([128, 12, 17, 3], F32)
    cv0s = A([128, 12, NSEQ, 3], F32)
    ARENA0 = A.ptr

    ps = [nc.alloc_psum_tensor("ps%d" % i, [128, 512], F32) for i in range(8)]
    psb = [B("ps%d" % i) for i in range(8)]
    for b_ in psb:
        b_.excl = True

    cmP = cst[:, 384:896]
    cmS = cst[:, 896:960]
    rowmask = cst[:, 960:976]
    G0, G1, GF, CB, BR, BI, LAM, L0C, L1C, OG = 0, 8, 16, 24, 36, 48, 60, 72, 80, 88
    NBR, NBI, CC, C2, LB, LNOM, T0, T1, T2, T3 = 0, 12, 24, 36, 48, 56, 64, 72, 80, 88

    def ld(dst, src, bname, eng="sp"):
        P.add(eng, lambda e: [e.dma_start(out=dst, in_=src)], W=[bname], dma=B(bname))

    ld(cst[:], cst_d, "cst")
    ld(pvec[:], pvec_d, "pvec")
    ld(cv0s[:], cv0_d, "cv0s")
    P.add("dve", lambda e: e.tensor_copy(ident[:], cst[:, 0:128]), R=["cst"], W=["ident"])
    P.add("dve", lambda e: e.tensor_copy(maskP[:], cst[:, 128:256]), R=["cst"], W=["maskP"])
    P.add("dve", lambda e: e.tensor_copy(maskS[:], cst[:, 256:384]), R=["cst"], W=["maskS"])
    P.add("dve", lambda e: e.memset(ones[:], 1.0), W=["ones"])
    P.add("dve", lambda e: e.memset(hcar[:], 0.0), W=["hcar"])
    P.add("dve", lambda e: e.memset(halo[:], 0.0), W=["halo"])
    P.add("dve", lambda e: e.memset(S[:], 0.0), W=["S"])
    P.add("dve", lambda e: e.memset(Sb[:], 0.0), W=["Sb"])
    P.add("dve", lambda e: e.tensor_scalar(der[:, NBR:NBR + 12], pvec[:, BR:BR + 12], -1.0, None, ALU.mult), R=["pvec"], W=["der_a"])
    P.add("dve", lambda e: e.tensor_scalar(der[:, NBI:NBI + 12], pvec[:, BI:BI + 12], -1.0, None, ALU.mult), R=["pvec"], W=["der_a"])
    P.add("act", lambda e: e.activation(der[:, CC:CC + 12], pvec[:, LAM:LAM + 12], AF.Exp, scale=-1.0), R=["pvec"], W=["der_c"])
    P.add("act", lambda e: e.activation(der[:, CC:CC + 12], der[:, CC:CC + 12], AF.Ln, bias=1.0), R=["der_c"], W=["der_c"])
    P.add("dve", lambda e: e.tensor_scalar(der[:, C2:C2 + 12], der[:, CC:CC + 12], -16.0, None, ALU.mult), R=["der_c"], W=["der_c2"])
    P.add("dve", lambda e: e.tensor_scalar(der[:, CC:CC + 12], der[:, CC:CC + 12], -8.0, None, ALU.mult), R=["der_c", "der_c2"], W=["der_c"])
    P.add("dve", lambda e: e.tensor_tensor(der[:, T0:T0 + 8], pvec[:, L0C:L0C + 8], pvec[:, L1C:L1C + 8], ALU.max), R=["pvec"], W=["der_t0"])
    P.add("dve", lambda e: e.tensor_tensor(der[:, T1:T1 + 8], pvec[:, L0C:L0C + 8], der[:, T0:T0 + 8], ALU.subtract), R=["pvec", "der_t0"], W=["der_t1"])
    P.add("dve", lambda e: e.tensor_tensor(der[:, T2:T2 + 8], pvec[:, L1C:L1C + 8], der[:, T0:T0 + 8], ALU.subtract), R=["pvec", "der_t0"], W=["der_t2"])
    P.add("act", lambda e: e.activation(der[:, T1:T1 + 8], der[:, T1:T1 + 8], AF.Exp), R=["der_t1"], W=["der_t1"])
    P.add("act", lambda e: e.activation(der[:, T2:T2 + 8], der[:, T2:T2 + 8], AF.Exp), R=["der_t2"], W=["der_t2"])
    P.add("dve", lambda e: e.tensor_tensor(der[:, T3:T3 + 8], der[:, T1:T1 + 8], der[:, T2:T2 + 8], ALU.add), R=["der_t1", "der_t2"], W=["der_t3"])
    P.add("dve", lambda e: e.reciprocal(der[:, T3:T3 + 8], der[:, T3:T3 + 8]), R=["der_t3"], W=["der_t3"])
    P.add("dve", lambda e: e.tensor_tensor(der[:, T1:T1 + 8], der[:, T1:T1 + 8], der[:, T3:T3 + 8], ALU.mult), R=["der_t1", "der_t3"], W=["der_t1"])
    P.add("dve", lambda e: e.tensor_tensor(der[:, T2:T2 + 8], der[:, T2:T2 + 8], der[:, T3:T3 + 8], ALU.mult), R=["der_t2", "der_t3"], W=["der_t2"])
    P.add("dve", lambda e: e.tensor_tensor(der[:, T3:T3 + 8], der[:, T1:T1 + 8], der[:, T2:T2 + 8], ALU.add), R=["der_t1", "der_t2", "der_t3"], W=["der_t3"])
    P.add("dve", lambda e: e.tensor_tensor(der[:, LB:LB + 8], der[:, T3:T3 + 8], der[:, T1:T1 + 8], ALU.subtract), R=["der_t1", "der_t3"], W=["der_lb"])
    P.add("act", lambda e: e.activation(der[:, LNOM:LNOM + 8], der[:, LB:LB + 8], AF.Ln, scale=-1.0, bias=1.0), R=["der_lb"], W=["der_lnom"])

    slot_rr = [0]

    def slot():
        i = slot_rr[0] % NSLOT
        slot_rr[0] += 1
        return i

    ps_rr = {}

    def psum(tag, banks):
        i = ps_rr.get(tag, 0)
        ps_rr[tag] = i + 1
        return banks[i % len(banks)]

    def act(out, in_, func, R, Wr, **kw):
        P.add("act", lambda e: e.activation(out, in_, func, **kw), R=R, W=Wr)

    NB = 2

    def norm_sq(c, N):
        sq = xsq[:, c % 2, :N]
        P.add("act", lambda e: e.activation(sq, xt[:, c, :N], AF.Square), R=["xt%d" % c], W=["xsq%d" % (c % 2)])

    def norm_mm(c, N):
        sq = xsq[:, c % 2, :N]
        P.add("pe", lambda e: e.matmul(ps[NB][:, :N], ones[:], sq, start=(c == 0), stop=(c == 7)), R=["ones", "xsq%d" % (c % 2)], W=[psb[NB]])

    def norm_step(c, N):
        norm_sq(c, N)
        norm_mm(c, N)

    def norm_finish(N, gcol, out_final, fs=None, after=None):
        act(stat[:, 0, :N], ps[NB][:, :N], AF.Ln, [psb[NB]], ["stat0"], scale=1.0 / D, bias=EPS)
        act(stat[:, 1, :N], stat[:, 0, :N], AF.Exp, ["stat0"], ["stat1"], scale=-0.5)
        for c in range(8):
            if out_final:
                P.add("dve", lambda e: e.scalar_tensor_tensor(slots[:, fs[c], :N], xt[:, c, :N], pvec[:, gcol + c:gcol + c + 1], stat[:, 1, :N], ALU.mult, ALU.mult),
                      R=["xt%d" % c, "stat1", "pvec"], W=["slot%d" % fs[c]])
            else:
                P.add("dve", lambda e: e.scalar_tensor_tensor(xn[:, c, :N], xt[:, c, :N], pvec[:, gcol + c:gcol + c + 1], stat[:, 1, :N], ALU.mult, ALU.mult),
                      R=["xt%d" % c, "stat1", "pvec"], W=["xn%d" % c])
            if after is not None:
                after(c)

    def norm(N, gcol, out_final, fs=None):
        for c in range(8):
            norm_step(c, N)
        norm_finish(N, gcol, out_final, fs)

    XN = ["xn%d" % c for c in range(8)]

    class WStream:
        def __init__(self):
            self.q = []
            self.idx = {}
            self.nxt = 0
            self.rings = {}

        def ring(self, name, bufnames):
            self.rings[name] = dict(names=bufnames, owner=[None] * len(bufnames), rr=0)

        def plan(self, key, ring, fn, ndma):
            it = dict(key=key, ring=ring, fn=fn, ndma=ndma, slot=None, issued=False)
            self.q.append(it)
            self.idx[key] = it

        def pump(self):
            while self.nxt < len(self.q):
                it = self.q[self.nxt]
                r = self.rings[it["ring"]]
                i = r["rr"] % len(r["names"])
                if r["owner"][i] is not None:
                    break
                r["owner"][i] = it["key"]
                r["rr"] += 1
                it["slot"] = i
                bname = r["names"][i]
                fn = it["fn"]
                P.add("pool", lambda e: fn(e, i), W=[bname], dma=B(bname), ndma=it["ndma"], bar=False)
                it["issued"] = True
                self.nxt += 1

        def get(self, key):
            self.pump()
            it = self.idx[key]
            assert it["issued"], ("weight piece not loadable yet", key)
            return it["slot"], self.rings[it["ring"]]["names"][it["slot"]]

        def release(self, key):
            it = self.idx[key]
            r = self.rings[it["ring"]]
            assert r["owner"][it["slot"]] == key
            r["owner"][it["slot"]] = None
            self.pump()

    WS = WStream()
    WS.ring("A", ["WA%d" % i for i in range(4)])
    WS.ring("O", ["WO%d" % i for i in range(5)])
    WS.ring("D", ["WD%d" % i for i in range(2)])
    WS.ring("B", ["WB%d" % i for i in range(2)])
    WS.ring("V", ["WV0", "WV1"])

    def layer0(ti, N, nseq, T, sample, last_prompt):
        save_ptr = A.ptr
        xbb = A([128, 12, nseq, 3 + T], BF16)
        xcb = A([128, 2, 3, N], BF16)
        xcf = A([128, 2, 3, N], F32)
        if sample:
            h0s = A([128, 12, NSEQ], F32)
            tmp0 = A([128, NSEQ], F32)
            ld(h0s[:], h0_d, "h0s")
        wa = {}
        st = {}

        def p1(g, j, split=False):
            wi, wname = WS.get(("X", ti, g))
            ch = 3 * g + j
            xs = g % 2
            bk = psum("xb", [0, 1])
            P.add("pe", lambda e: [e.matmul(ps[bk][:, :N], WA[:, wi, c, j * 128:(j + 1) * 128], xn[:, c, :N], start=(c == 0), stop=(c == 7)) for c in range(8)],
                  R=[wname] + XN, W=[psb[bk]])
            xbname = "xbb%d" % ch
            src3 = ps[bk][:, :N].rearrange("p (s t) -> p s t", s=nseq)
            if sample:
                P.add("dve", lambda e: e.tensor_copy(xbb[:, ch, :, 0:3], cv0s[:, ch, :, :]), R=["cv0s"], W=[xbname])
            else:
                P.add("dve", lambda e: e.tensor_copy(xbb[:, ch, 0, 0:3], halo[:, ch, 0:3]), R=["halo%d" % ch], W=[xbname])
            P.add("dve", lambda e: e.tensor_copy(xbb[:, ch, :, 3:3 + T], src3), R=[psb[bk]], W=[xbname])
            if sample:
                P.add("dve", lambda e: e.tensor_copy(cstage[:, ch, 1:17, :], src3[:, :, 1:4]), R=[psb[bk]], W=["cstage"])
            else:
                P.add("pool", lambda e: e.tensor_copy(halo[:, ch, 0:3], xbb[:, ch, 0, T:T + 3]), R=[xbname], W=["halo%d" % ch])
                if last_prompt:
                    P.add("dve", lambda e: e.tensor_copy(cstage[:, ch, 0, :], ps[bk][:, N - 3:N]), R=[psb[bk]], W=["cstage"])
            if j == 2:
                WS.release(("X", ti, g))
            if not split:
                p1b(g, j)

        def p1b(g, j):
            di, dname = WS.get(("D", ti, g))
            ch = 3 * g + j
            xs = g % 2
            xbname = "xbb%d" % ch
            cb = psum("xc", [2, 3, 4])
            dst3 = ps[cb][:, :N].rearrange("p (s t) -> p s t", s=nseq)
            P.add("pe", lambda e: [e.matmul(dst3, WD[:, di, j * 4 + k, :], xbb[:, ch, :, k:k + T], start=(k == 0), stop=(k == 3)) for k in range(4)],
                  R=[xbname, dname], W=[psb[cb]])
            P.add("dve", lambda e: e.tensor_scalar(xcf[:, xs, j, :], ps[cb][:, :N], pvec[:, CB + ch:CB + ch + 1], None, ALU.add),
                  R=[psb[cb], "pvec"], W=["xcf%d_%d" % (xs, j)])
            P.add("dve", lambda e: e.tensor_copy(xcb[:, xs, j, :], xcf[:, xs, j, :]), R=["xcf%d_%d" % (xs, j)], W=["xcb%d_%d" % (xs, j)])
            if j == 2:
                WS.release(("D", ti, g))

        def gm(n):
            g, j = divmod(n, 3)
            wi, wname = WS.get(("G", ti, g))
            bi, bdname = WS.get(("B", ti, g))
            xs = g % 2
            XC = ["xcb%d_%d" % (xs, jj) for jj in range(3)]
            rb, ib, gb = 5, 6, 7
            P.add("pe", lambda e: [e.matmul(ps[rb][:, :N], WBD[:, bi, 0, jp, j * 128:(j + 1) * 128], xcb[:, xs, jp, :], start=(jp == 0), stop=(jp == 2)) for jp in range(3)],
                  R=XC + [bdname], W=[psb[rb]])
            P.add("pe", lambda e: [e.matmul(ps[ib][:, :N], WBD[:, bi, 1, jp, j * 128:(j + 1) * 128], xcb[:, xs, jp, :], start=(jp == 0), stop=(jp == 2)) for jp in range(3)],
                  R=XC + [bdname], W=[psb[ib]])
            P.add("pe", lambda e: [e.matmul(ps[gb][:, :N], WA[:, wi, c, j * 128:(j + 1) * 128], xn[:, c, :N], start=(c == 0), stop=(c == 7)) for c in range(8)],
                  R=[wname] + XN, W=[psb[gb]])
            if j == 2:
                WS.release(("G", ti, g))
                WS.release(("B", ti, g))

        def stage_a(n):
            ch = n
            rb, ib, gb = 5, 6, 7
            sl = [slot() for _ in range(5)]
            nm = ["slot%d" % s_ for s_ in sl]
            vv = [slots[:, s_, :N] for s_ in sl]
            st[n] = (nm, vv)
            (n1, n2, n3, n4, n5), (v1, v2, v3, v4, v5) = nm, vv
            act(v1, ps[rb][:, :N], AF.Exp, [psb[rb], "der_a"], [n1], scale=-1.0, bias=der[:, NBR + ch:NBR + ch + 1])
            act(v4, ps[ib][:, :N], AF.Exp, [psb[ib], "der_a"], [n4], scale=-1.0, bias=der[:, NBI + ch:NBI + ch + 1])
            act(v5, ps[gb][:, :N], AF.Exp, [psb[gb]], [n5], scale=-1.0)
            act(v1, v1, AF.Ln, [n1], [n1], bias=1.0)
            act(v4, v4, AF.Ln, [n4], [n4], bias=1.0)
            act(v5, v5, AF.Ln, [n5], [n5], bias=1.0)
            act(v1, v1, AF.Exp, [n1], [n1], scale=-1.0)
            act(v5, v5, AF.Exp, [n5], [n5], scale=-1.0)
            act(v2, v1, AF.Exp, [n1, "der_c"], [n2], scale=der[:, CC + ch:CC + ch + 1])
            P.add("dve", lambda e: e.tensor_tensor(v5, v5, ps[gb][:, :N], ALU.mult), R=[n5, psb[gb]], W=[n5])
            act(v3, v1, AF.Exp, [n1, "der_c2"], [n3], scale=der[:, C2 + ch:C2 + ch + 1])
            act(v3, v3, AF.Ln, [n3], [n3], scale=-1.0, bias=1.0)
            P.add("dve", lambda e: e.scalar_tensor_tensor(v3, v3, 0.5, v4, ALU.mult, ALU.subtract), R=[n3, n4], W=[n3])

        def stage_b(n):
            ch = n
            g, j = divmod(n, 3)
            xs = g % 2
            (n1, n2, n3, n4, n5), (v1, v2, v3, v4, v5) = st[n]
            act(v3, v3, AF.Exp, [n3], [n3])
            P.add("dve", lambda e: e.tensor_tensor(v3, v3, xcf[:, xs, j, :], ALU.mult), R=[n3, "xcf%d_%d" % (xs, j)], W=[n3])
            if sample:
                a3 = v2.rearrange("p (s t) -> p s t", s=nseq)
                b3 = v3.rearrange("p (s t) -> p s t", s=nseq)
                P.add("dve", lambda e: e.tensor_tensor(tmp0[:, :], a3[:, :, 0], h0s[:, ch, :], ALU.mult), R=[n2, "h0s"], W=["tmp0"])
                P.add("dve", lambda e: e.tensor_tensor(b3[:, :, 0], b3[:, :, 0], tmp0[:, :], ALU.add), R=[n3, "tmp0"], W=[n3])
                P.add("dve", lambda e: e.memset(a3[:, :, 0], 0.0), R=["tmp0"], W=[n2])
                P.add("dve", lambda e: e.tensor_tensor_scan(v1, v2, v3, 0.0, ALU.mult, ALU.add), R=[n2, n3], W=[n1])
                h3 = v1.rearrange("p (s t) -> p s t", s=nseq)
                P.add("dve", lambda e: e.tensor_copy(hstage[:, ch, 1:17], h3[:, :, T - 1]), R=[n1], W=["hstage"])
            else:
                P.add("dve", lambda e: e.tensor_tensor_scan(v1, v2, v3, hcar[:, ch:ch + 1], ALU.mult, ALU.add), R=[n2, n3, "hcar%d" % ch], W=[n1])
                P.add("pool", lambda e: e.tensor_copy(hcar[:, ch:ch + 1], v1[:, N - 1:N]), R=[n1], W=["hcar%d" % ch])
                if last_prompt:
                    P.add("pool", lambda e: e.tensor_copy(hstage[:, ch, 0:1], v1[:, N - 1:N]), R=[n1], W=["hstage"])
            P.add("pool" if n < 10 else "dve", lambda e: e.tensor_tensor(ybuf[:, ch, :N], v1, v5, ALU.mult), R=[n1, n5], W=["y%d" % ch])

        for j in range(3):
            p1(0, j, split=True)
        for j in range(3):
            p1b(0, j)
        for n in range(12):
            g, j = divmod(n, 3)
            gm(n)
            if g + 1 < 4:
                p1(g + 1, j)
            stage_a(n)
            if n >= 1:
                stage_b(n - 1)
        stage_b(11)
        YB = ["y%d" % ch for ch in range(12)]
        for c in range(8):
            wo, woname = WS.get(("O0", ti, c))
            bk = psum("xb", [0, 1])
            P.add("pe", lambda e: [e.matmul(ps[bk][:, :N], WO[:, wo, ch, :], ybuf[:, ch, :N], start=(ch == 0), stop=(ch == 11)) for ch in range(12)],
                  R=[woname] + YB, W=[psb[bk]])
            WS.release(("O0", ti, c))
            if c > 0:
                norm_mm(c - 1, N)
            P.add("dve", lambda e: e.tensor_tensor(xt[:, c, :N], xt[:, c, :N], ps[bk][:, :N], ALU.add), R=["xt%d" % c, psb[bk]], W=["xt%d" % c])
            norm_sq(c, N)
        norm_mm(7, N)
        A.ptr = save_ptr

    def layer1(ti, N, nseq, T, sample):
        save_ptr = A.ptr
        nst = max(1, N // 128)
        SW = min(N, 128)
        C = 64 if not sample else T
        nch = N // C
        cps = SW // C
        Qd = A([128, 8, N], BF16)
        Kt = A([128, 8, N], BF16)
        Kl = A([128, 8, N], BF16)
        KlT = A([128, nst, 1024], BF16)
        V = A([128, nst, 1024], BF16)
        sgate = A([128, 8, N], BF16)
        ATb = A([128, 8, 128], BF16)
        osq = A([128, 2, 512], BF16)
        elast = A([128, 8, nch], F32)
        St = A([128, 8, 128], F32)
        if sample:
            NSB = 4
            S0f = A([128, NSB, 8, 128], F32)
            S0b = A([128, NSB, 8, 128], BF16)
            KlTm = A([128, 2, 1024], BF16)
            Sout = A([128, 2, 8, 128], F32)

            def load_s0(j):
                sb = j % NSB
                P.add("sp", lambda e: [e.dma_start(out=S0f[:, sb], in_=s0_d[:, j])], W=["S0f%d" % sb], dma=B("S0f%d" % sb))
                P.add("pool", lambda e: [e.dma_start(out=S0b[:, sb], in_=s0_d[:, j])], W=["S0b%d" % sb], dma=B("S0b%d" % sb))
            for j in range(NSB):
                load_s0(j)
        norm_finish(N, G1, False)
        cm = cmS if sample else cmP
        maskb = maskS if sample else maskP
        st = {}
        QB, FB, GB = [0, 3, 6, 7], [1, 4], [2, 5]

        def pm(h):
            wi, wname = WS.get(("H", ti, h))
            for (bk, off) in ((QB[h % 4], 0), (FB[h % 2], 128), (GB[h % 2], 256)):
                P.add("pe", lambda e: [e.matmul(ps[bk][:, :N], WA[:, wi, c, off:off + 128], xn[:, c, :N], start=(c == 0), stop=(c == 7)) for c in range(8)],
                      R=[wname] + XN, W=[psb[bk]])
            WS.release(("H", ti, h))

        def stage_a(h):
            qb, fb, gb = QB[h % 4], FB[h % 2], GB[h % 2]
            sl = [slot() for _ in range(5)]
            nm = ["slot%d" % s_ for s_ in sl]
            vv = [slots[:, s_, :N] for s_ in sl]
            st[h] = (nm, vv)
            (n1, n2, n3, n4, n5), (v1, v2, v3, v4, v5) = nm, vv
            act(v1, ps[qb][:, :N], AF.Exp, [psb[qb]], [n1], scale=-1.0)
            act(v2, ps[fb][:, :N], AF.Exp, [psb[fb]], [n2], scale=-1.0)
            act(v5, ps[gb][:, :N], AF.Exp, [psb[gb]], [n5], scale=-1.0)
            act(v1, v1, AF.Ln, [n1], [n1], bias=1.0)
            act(v3, v2, AF.Ln, [n2, "der_lb"], [n3], scale=der[:, LB + h:LB + h + 1], bias=1.0)
            act(v2, v2, AF.Ln, [n2], [n2], bias=1.0)
            act(v5, v5, AF.Ln, [n5], [n5], bias=1.0)
            act(v5, v5, AF.Exp, [n5], [n5], scale=-1.0)
            P.add("dve", lambda e: e.tensor_tensor(v3, v3, v2, ALU.subtract), R=[n2, n3], W=[n3])
            P.add("dve", lambda e: e.tensor_tensor(v2, v2, ps[fb][:, :N], ALU.add), R=[n2, psb[fb]], W=[n2])
            P.add("dve", lambda e: e.tensor_tensor_scan(v4, cm[:, :N], v3, 0.0, ALU.mult, ALU.add), R=[n3, "cst"], W=[n4])
            P.add("dve", lambda e: e.tensor_tensor(v2, v2, v4, ALU.add), R=[n2, n4], W=[n2])
            P.add("dve", lambda e: e.tensor_tensor(v1, v4, v1, ALU.subtract), R=[n1, n4], W=[n1])
            P.add("dve", lambda e: e.scalar_tensor_tensor(sgate[:, h, :], ps[gb][:, :N], pvec[:, OG + h:OG + h + 1], v5, ALU.mult, ALU.mult),
                  R=[n5, psb[gb], "pvec"], W=["sgate%d" % h])

        def stage_b(h):
            qb = QB[h % 4]
            (n1, n2, n3, n4, n5), (v1, v2, v3, v4, v5) = st[h]
            c3 = v4.rearrange("p (c t) -> p c t", t=C)
            act(elast[:, h, :], c3[:, :, C - 1], AF.Exp, [n4], ["elast%d" % h])
            act(Kt[:, h, :], v2, AF.Exp, [n2, "der_lnom"], ["Kt%d" % h], scale=-1.0, bias=der[:, LNOM + h:LNOM + h + 1])
            act(v1, v1, AF.Exp, [n1], [n1])
            P.add("dve", lambda e: e.tensor_tensor(Qd[:, h, :], v1, ps[qb][:, :N], ALU.mult), R=[n1, psb[qb]], W=["Qd%d" % h])
            if sample:
                P.add("dve", lambda e: [e.tensor_scalar(Kl[:, h, c * C:(c + 1) * C], Kt[:, h, c * C:(c + 1) * C], elast[:, h, c:c + 1], None, ALU.mult) for c in range(nch)],
                      R=["Kt%d" % h, "elast%d" % h], W=["Kl%d" % h])
            else:
                P.add("dve", lambda e: e.tensor_tensor(Kl[:, h, :].rearrange("p (c t) -> p c t", t=C), Kt[:, h, :].rearrange("p (c t) -> p c t", t=C),
                                                       elast[:, h, :].unsqueeze(2).broadcast_to([128, nch, C]), ALU.mult),
                      R=["Kt%d" % h, "elast%d" % h], W=["Kl%d" % h])

        for h in range(8):
            pm(h)
            stage_a(h)
            if h >= 1:
                stage_b(h - 1)
        stage_b(7)
        vi = 0
        for q in range(4):
            wv, wvname = WS.get(("V", ti, q))
            for s_i in range(nst):
                bk = [4, 5][vi % 2]
                P.add("pe", lambda e: [e.matmul(ps[bk][:SW, 0:256], xn[:, c, s_i * SW:(s_i + 1) * SW], (WV[:, wv, c, :] if q < 2 else WA[:, wv, c, 0:256]), start=(c == 0), stop=(c == 7)) for c in range(8)],
                      R=[wvname] + XN, W=[psb[bk]])
                if vi % 2 == 0:
                    P.add("act", lambda e: e.activation(V[:SW, s_i, q * 256:(q + 1) * 256], ps[bk][:SW, 0:256], AF.Copy), R=[psb[bk]], W=["V%d" % s_i])
                else:
                    P.add("dve", lambda e: e.tensor_copy(V[:SW, s_i, q * 256:(q + 1) * 256], ps[bk][:SW, 0:256]), R=[psb[bk]], W=["V%d" % s_i])
                vi += 1
            WS.release(("V", ti, q))
        KLN = ["Kl%d" % h for h in range(8)]
        for s_i in range(nst):
            for half in range(2):
                bk = [6, 7][vi % 2]
                pst = pstv[bk]
                P.add("pe", lambda e: [e.transpose(pst[:SW, hh * 128:(hh + 1) * 128], Kl[:, half * 4 + hh, s_i * SW:(s_i + 1) * SW], ident[:]) for hh in range(4)],
                      R=KLN + ["ident"], W=[psb[bk]])
                if vi % 2 == 0:
                    P.add("act", lambda e: e.activation(KlT[:SW, s_i, half * 512:(half + 1) * 512], pst[:SW, 0:512], AF.Copy), R=[psb[bk]], W=["KlT%d" % s_i])
                else:
                    P.add("dve", lambda e: e.tensor_copy(KlT[:SW, s_i, half * 512:(half + 1) * 512], pst[:SW, 0:512]), R=[psb[bk]], W=["KlT%d" % s_i])
                vi += 1
        ost = {}

        def opath_a(s_i):
            OB = [6, 7] if s_i % 2 == 0 else [0, 1]
            for half in range(2):
                ob = OB[half]
                nb = 4 + half
                sr, s1_ = slot(), slot()
                ost[(s_i, half)] = (sr, s1_)
                ow = ps[ob][:, :].rearrange("p (h t) -> p h t", h=4)[:, :, :SW]
                osq3 = osq[:, half, :].rearrange("p (h t) -> p h t", h=4)[:, :, :SW]
                nw = ps[nb][:, :].rearrange("p (h t) -> p h t", h=4)[:, :, :SW]
                P.add("act", lambda e: e.activation(osq3, ow, AF.Square), R=[psb[ob]], W=["osq%d" % half])
                if SW == 128:
                    P.add("pe", lambda e: e.matmul(nw, ones[:], osq3, start=True, stop=True, skip_group_check=True), R=["osq%d" % half, "ones"], W=[psb[nb]])
                else:
                    P.add("pe", lambda e: [e.matmul(ps[nb][:, hh * 128:hh * 128 + SW], ones[:], osq[:, half, hh * 128:hh * 128 + SW], start=True, stop=True, skip_group_check=True) for hh in range(4)],
                          R=["osq%d" % half, "ones"], W=[psb[nb]])

        def opath_b1(s_i):
            for half in range(2):
                nb = 4 + half
                sr, s1_ = ost[(s_i, half)]
                nr = "slot%d" % sr
                nw = ps[nb][:, :].rearrange("p (h t) -> p h t", h=4)[:, :, :SW]
                rs3 = slots[:, sr, :].rearrange("p (h t) -> p h t", h=4)[:, :, :SW]
                act(rs3, nw, AF.Ln, [psb[nb]], [nr], scale=1.0 / 128.0, bias=EPS)
                act(rs3, rs3, AF.Exp, [nr], [nr], scale=-0.5)

        def opath_b2(s_i):
            OB = [6, 7] if s_i % 2 == 0 else [0, 1]
            t0_ = s_i * SW
            for half in range(2):
                ob = OB[half]
                sr, s1_ = ost[(s_i, half)]
                nr, n1_ = "slot%d" % sr, "slot%d" % s1_
                ow = ps[ob][:, :].rearrange("p (h t) -> p h t", h=4)[:, :, :SW]
                rs3 = slots[:, sr, :].rearrange("p (h t) -> p h t", h=4)[:, :, :SW]
                t13 = slots[:, s1_, :].rearrange("p (h t) -> p h t", h=4)[:, :, :SW]
                P.add("dve", lambda e: e.tensor_tensor(t13, ow, rs3, ALU.mult), R=[psb[ob], nr], W=[n1_])
                P.add("dve", lambda e: e.tensor_tensor(ybuf[:, half * 4:half * 4 + 4, t0_:t0_ + SW], t13, sgate[:, half * 4:half * 4 + 4, t0_:t0_ + SW], ALU.mult),
                      R=[n1_] + ["sgate%d" % (half * 4 + hh) for hh in range(4)], W=["y1_%d_%d" % (half, s_i)])

        for s_i in range(nst):
            t0 = s_i * SW
            OB = [6, 7] if s_i % 2 == 0 else [0, 1]
            for half in range(2):
                atb = 4 + half
                for hh in range(4):
                    h = half * 4 + hh
                    P.add("pe", lambda e: e.matmul(ps[atb][:SW, hh * 128:hh * 128 + SW], Kt[:, h, t0:t0 + SW], Qd[:, h, t0:t0 + SW], start=True, stop=True, skip_group_check=True),
                          R=["Kt%d" % h, "Qd%d" % h], W=[psb[atb]])
                for hh in range(4):
                    h = half * 4 + hh
                    P.add("dve", lambda e: e.tensor_tensor(ATb[:SW, h, :SW], ps[atb][:SW, hh * 128:hh * 128 + SW], maskb[:SW, :SW], ALU.mult),
                          R=[psb[atb], "maskP", "maskS"], W=["AT%d" % h])
            for half in range(2):
                ob = OB[half]
                for hh in range(4):
                    h = half * 4 + hh
                    P.add("pe", lambda e: e.matmul(ps[ob][:, hh * 128:hh * 128 + SW], V[:SW, s_i, h * 128:(h + 1) * 128], ATb[:SW, h, :SW], start=(hh == 0), stop=False, skip_group_check=True),
                          R=["V%d" % s_i, "AT%d" % h], W=[psb[ob]])
            if s_i > 0:
                opath_a(s_i - 1)
            if not sample:
                for cc in range(cps):
                    ci = s_i * cps + cc
                    p0 = cc * C
                    UB = [2, 3]
                    for half in range(2):
                        hs = slice(half * 4, half * 4 + 4)
                        HN = ["%d" % (half * 4 + hh) for hh in range(4)]
                        ebc = elast[:, hs, ci:ci + 1].broadcast_to([128, 4, 128])
                        P.add("dve", lambda e: e.tensor_tensor(St[:, hs, :], S[:, hs, :], ebc, ALU.mult),
                              R=["S" + x for x in HN] + ["elast" + x for x in HN], W=["St%d" % half])
                    for half in range(2):
                        ob = OB[half]
                        for hh in range(4):
                            h = half * 4 + hh
                            P.add("pe", lambda e: e.matmul(ps[ob][:, hh * 128 + p0:hh * 128 + p0 + C], Sb[:, h, :], Qd[:, h, t0 + p0:t0 + p0 + C], start=False, stop=(cc == cps - 1), skip_group_check=True),
                                  R=["Sb%d" % h, "Qd%d" % h], W=[psb[ob]])
                    for half in range(2):
                        ub = UB[half]
                        for hh in range(4):
                            h = half * 4 + hh
                            P.add("pe", lambda e: e.matmul(ps[ub][:, hh * 128:(hh + 1) * 128], KlT[p0:p0 + C, s_i, h * 128:(h + 1) * 128], V[p0:p0 + C, s_i, h * 128:(h + 1) * 128], start=True, stop=True, skip_group_check=True),
                                  R=["KlT%d" % s_i, "V%d" % s_i], W=[psb[ub]])
                    for half in range(2):
                        ub = UB[half]
                        hs = slice(half * 4, half * 4 + 4)
                        HN = ["%d" % (half * 4 + hh) for hh in range(4)]
                        u3 = ps[ub][:, :].rearrange("p (h v) -> p h v", h=4)
                        P.add("dve", lambda e: e.tensor_tensor(S[:, hs, :], St[:, hs, :], u3, ALU.add),
                              R=[psb[ub], "St%d" % half], W=["S" + x for x in HN])
                        P.add("act", lambda e: e.activation(Sb[:, hs, :], S[:, hs, :], AF.Copy), R=["S" + x for x in HN], W=["Sb" + x for x in HN])
                    if s_i > 0:
                        if cc == 0:
                            opath_b1(s_i - 1)
                        elif cc == cps - 1:
                            opath_b2(s_i - 1)
            else:
                for j in range(NSEQ):
                    sb = j % NSB
                    so = j % 2
                    UB = [2, 3] if j % 2 == 0 else [4, 5]
                    P.add("dve", lambda e: e.tensor_scalar(KlTm[:SW, so, :], KlT[:SW, 0, :], rowmask[:SW, j:j + 1], None, ALU.mult),
                          R=["KlT0", "cst"], W=["KlTm%d" % so])
                    for half in range(2):
                        hs = slice(half * 4, half * 4 + 4)
                        ebc = elast[:, hs, j:j + 1].broadcast_to([128, 4, 128])
                        P.add("dve", lambda e: e.tensor_tensor(St[:, hs, :], S0f[:, sb, hs, :], ebc, ALU.mult),
                              R=["S0f%d" % sb] + ["elast%d" % (half * 4 + hh) for hh in range(4)], W=["St%d" % half])
                    for half in range(2):
                        ob = OB[half]
                        for hh in range(4):
                            h = half * 4 + hh
                            P.add("pe", lambda e: e.matmul(ps[ob][:, hh * 128 + j * T:hh * 128 + (j + 1) * T], S0b[:, sb, h, :], Qd[:, h, j * T:(j + 1) * T], start=False, stop=(j == NSEQ - 1), skip_group_check=True),
                                  R=["S0b%d" % sb, "Qd%d" % h], W=[psb[ob]])
                    for half in range(2):
                        ub = UB[half]
                        for hh in range(4):
                            h = half * 4 + hh
                            P.add("pe", lambda e: e.matmul(ps[ub][:, hh * 128:(hh + 1) * 128], KlTm[:SW, so, h * 128:(h + 1) * 128], V[:SW, 0, h * 128:(h + 1) * 128], start=True, stop=True, skip_group_check=True),
                                  R=["KlTm%d" % so, "V0"], W=[psb[ub]])
                    for half in range(2):
                        ub = UB[half]
                        hs = slice(half * 4, half * 4 + 4)
                        u3 = ps[ub][:, :].rearrange("p (h v) -> p h v", h=4)
                        P.add("dve", lambda e: e.tensor_tensor(Sout[:, so, hs, :], St[:, hs, :], u3, ALU.add),
                              R=[psb[ub], "St%d" % half], W=["Sout%d_%d" % (so, half)])
                    P.add("sp", lambda e: [e.dma_start(out=sout_d[:, 1 + j], in_=Sout[:, so])], R=["Sout%d_0" % so, "Sout%d_1" % so], dma=B("Sout%d" % so))
                    if j + NSB < NSEQ:
                        load_s0(j + NSB)
        opath_a(nst - 1)
        opath_b1(nst - 1)
        opath_b2(nst - 1)
        YB = ["y1_%d_%d" % (half, s_i) for half in range(2) for s_i in range(nst)]
        for c in range(8):
            wo, woname = WS.get(("O1", ti, c))
            bk = psum("xb", [0, 1])
            P.add("pe", lambda e: [e.matmul(ps[bk][:, :N], WO[:, wo, hc, :], ybuf[:, hc, :N], start=(hc == 0), stop=(hc == 7)) for hc in range(8)],
                  R=[woname] + YB, W=[psb[bk]])
            WS.release(("O1", ti, c))
            if c > 0:
                norm_mm(c - 1, N)
            P.add("dve", lambda e: e.tensor_tensor(xt[:, c, :N], xt[:, c, :N], ps[bk][:, :N], ALU.add), R=["xt%d" % c, psb[bk]], W=["xt%d" % c])
            norm_sq(c, N)
        norm_mm(7, N)
        A.ptr = save_ptr

    pstv = {6: ps[6].bitcast(BF16), 7: ps[7].bitcast(BF16)}

    tiles = [(i * NT, NT, 1, NT, False) for i in range(SEQ // NT)] + [(SEQ, NS, NSEQ, 4, True)]
    XT = ["xt%d" % c for c in range(8)]
    for ti in range(len(tiles)):
        for g in range(4):
            WS.plan(("X", ti, g), "A", (lambda g: lambda e, i: [e.dma_start(out=WA[:, i, :, :], in_=w_in0_v[:, :, 384 * g:384 * g + 384])])(g), 1)
            WS.plan(("D", ti, g), "D", (lambda g: lambda e, i: [e.dma_start(out=WD[:, i], in_=diagw_d[:, 12 * g:12 * g + 12, :])])(g), 1)
            WS.plan(("G", ti, g), "A", (lambda g: lambda e, i: [e.dma_start(out=WA[:, i, :, :], in_=w_in0_v[:, :, W + 384 * g:W + 384 * g + 384])])(g), 1)
            WS.plan(("B", ti, g), "B", (lambda g: lambda e, i: [e.dma_start(out=WBD[:, i], in_=wbd_d[:, g])])(g), 1)
        for c in range(8):
            WS.plan(("O0", ti, c), "O", (lambda c: lambda e, i: [e.dma_start(out=WO[:, i, 0:12, :], in_=w_out0_v[:, :, c * 128:(c + 1) * 128])])(c), 1)
        for h in range(8):
            WS.plan(("H", ti, h), "A", (lambda h: lambda e, i: [e.dma_start(out=WA[:, i, :, k * 128:(k + 1) * 128], in_=w_in1_v[:, :, cb_ + h * 128:cb_ + (h + 1) * 128]) for k, cb_ in enumerate((0, 1024, 3072))])(h), 3)
        for q in range(4):
            if q < 2:
                WS.plan(("V", ti, q), "V", (lambda q: lambda e, i: [e.dma_start(out=WV[:, i], in_=w_in1_v[:, :, 2048 + q * 256:2048 + (q + 1) * 256])])(q), 1)
            else:
                WS.plan(("V", ti, q), "A", (lambda q: lambda e, i: [e.dma_start(out=WA[:, i, :, 0:256], in_=w_in1_v[:, :, 2048 + q * 256:2048 + (q + 1) * 256])])(q), 1)
        for c in range(8):
            WS.plan(("O1", ti, c), "O", (lambda c: lambda e, i: [e.dma_start(out=WO[:, i, 0:8, :], in_=w_out1_v[:, :, c * 128:(c + 1) * 128])])(c), 1)

    def load_x_chunk(c, n0, N):
        P.add("sp", lambda e: [e.dma_start(out=xt[:, c, :N], in_=xin_v[:, c, n0:n0 + N])], W=["xt%d" % c], dma=B("xtld%d" % c), bar=False)

    for c in range(8):
        load_x_chunk(c, tiles[0][0], tiles[0][1])
    WS.pump()
    P.barrier()
    norm(tiles[0][1], G0, False)
    for ti, (n0, N, nseq, T, sample) in enumerate(tiles):
        layer0(ti, N, nseq, T, sample, (not sample) and n0 + N == SEQ)
        P.barrier()
        layer1(ti, N, nseq, T, sample)
        fs = [slot() for _ in range(8)]
        nxt = tiles[ti + 1] if ti + 1 < len(tiles) else None
        norm_finish(N, GF, True, fs, after=(lambda c: load_x_chunk(c, nxt[0], nxt[1])) if nxt else None)
        P.add("sp", lambda e: [e.dma_start(out=yout_v[:, c, n0:n0 + N], in_=slots[:, fs[c], :N]) for c in range(8)], R=["slot%d" % s_ for s_ in fs], dma=B("xtst"), ndma=8, bar=False)
        if n0 + N == SEQ:
            P.add("sp", lambda e: [e.dma_start(out=sout_d[:, 0], in_=S[:])], R=["S%d" % h for h in range(8)], dma=B("Sst"))
        if nxt:
            norm(nxt[1], G0, False)
        P.barrier()
    P.add("sp", lambda e: [e.dma_start(out=hout_d, in_=hstage[:])], R=["hstage"], dma=B("hst"))
    P.add("sp", lambda e: [e.dma_start(out=cvout_d, in_=cstage[:])], R=["cstage"], dma=B("cst_out"))
    P.barrier(all_dma=True)
    return nc, P


_CACHE = {}


def _consts():
    c = np.zeros((128, 976), np.float32)
    c[:, 0:128] = np.eye(128, dtype=np.float32)
    s = np.arange(128)[:, None]
    t = np.arange(128)[None, :]
    c[:, 128:256] = ((s // 64 == t // 64) & (s <= t)).astype(np.float32)
    c[:, 256:384] = ((s // 4 == t // 4) & (s <= t)).astype(np.float32)
    cm = np.ones(512, np.float32)
    cm[::64] = 0.0
    c[:, 384:896] = cm[None, :]
    cs = np.ones(64, np.float32)
    cs[::4] = 0.0
    c[:, 896:960] = cs[None, :]
    p = np.arange(128)[:, None]
    j = np.arange(16)[None, :]
    c[:, 960:976] = ((p // 4 == j) & (p < 64)).astype(np.float32)
    return c


def _pc(v, n):
    return np.ascontiguousarray(np.asarray(v, np.float32).reshape(n, 128).T)


def kernel(x_prompt, x_sample, state_lru_h, state_lru_conv, state_hgrn, norm_gain, a_w_in, a_conv_w,
           a_conv_b, a_w_r, a_b_r, a_w_i, a_b_i, a_lambda, a_w_out, b_w_in, b_lb_logits, b_o_gain,
           b_w_out, final_gain):
    f = np.float32
    if "nc" not in _CACHE:
        nc, P = build_nc()
        _assign_and_emit(P)
        _CACHE["nc"] = nc
    nc = _CACHE["nc"]
    pvec = np.zeros((128, 96), f)
    pvec[:, 0:8] = _pc(norm_gain[0], 8)
    pvec[:, 8:16] = _pc(norm_gain[1], 8)
    pvec[:, 16:24] = _pc(final_gain, 8)
    pvec[:, 24:36] = _pc(a_conv_b[0], 12)
    pvec[:, 36:48] = _pc(a_b_r[0], 12)
    pvec[:, 48:60] = _pc(a_b_i[0], 12)
    pvec[:, 60:72] = _pc(a_lambda[0], 12)
    pvec[:, 72:80] = _pc(b_lb_logits[0], 8)
    pvec[:, 80:88] = _pc(b_lb_logits[1], 8)
    pvec[:, 88:96] = _pc(b_o_gain[0], 8)
    diagw = np.zeros((128, 48, 128), f)
    cw = np.asarray(a_conv_w[0], f)
    idx = np.arange(128)
    for ch in range(12):
        for k in range(4):
            diagw[idx, ch * 4 + k, idx] = cw[k, ch * 128:(ch + 1) * 128]
    wbd = np.zeros((128, 4, 2, 3, 384), f)
    for gi, wg in enumerate((np.asarray(a_w_r[0], f), np.asarray(a_w_i[0], f))):
        for g in range(4):
            full = np.zeros((384, 384), f)
            for b in range(4):
                full[96 * b:96 * b + 96, 96 * b:96 * b + 96] = wg[4 * g + b]
            wbd[:, g, gi] = full.reshape(3, 128, 384).transpose(1, 0, 2)
    cst = _consts()
    w_in0 = np.ascontiguousarray(a_w_in[0], f)
    w_out0 = np.ascontiguousarray(a_w_out[0], f)
    w_in1 = np.ascontiguousarray(b_w_in[0], f)
    w_out1 = np.ascontiguousarray(b_w_out[0], f)
    in_maps = []
    for b in range(NCORES):
        sl = slice(16 * b, 16 * b + 16)
        xs = np.asarray(x_sample[sl], f).reshape(64, D)
        xin = np.ascontiguousarray(np.concatenate([np.asarray(x_prompt[b], f), xs], axis=0).T)
        h0 = np.ascontiguousarray(np.asarray(state_lru_h[0, sl], f).reshape(16, 12, 128).transpose(2, 1, 0))
        cv0 = np.ascontiguousarray(np.asarray(state_lru_conv[0, sl], f).reshape(16, 3, 12, 128).transpose(3, 2, 0, 1))
        s0 = np.ascontiguousarray(np.asarray(state_hgrn[0, sl], f).transpose(2, 0, 1, 3))
        in_maps.append({"xin": xin, "w_in0": w_in0, "w_out0": w_out0, "w_in1": w_in1, "w_out1": w_out1,
                        "diagw": diagw, "wbd": wbd, "pvec": pvec, "cst": cst, "h0": h0, "cv0": cv0, "s0": s0})
    res = run_bass_kernel_spmd(nc, in_maps, core_ids=list(range(NCORES)))
    y_prompt = np.zeros((8, SEQ, D), f)
    y_sample = np.zeros((128, 4, D), f)
    hp = np.zeros((1, 8, W), f)
    bp = np.zeros((1, 8, 3, W), f)
    Sp = np.zeros((1, 8, 8, 128, 128), f)
    hs = np.zeros((1, 128, W), f)
    bs = np.zeros((1, 128, 3, W), f)
    Ss = np.zeros((1, 128, 8, 128, 128), f)
    for b in range(NCORES):
        r = res.results[b]
        sl = slice(16 * b, 16 * b + 16)
        yT = r["yout"]
        y_prompt[b] = yT[:, :SEQ].T
        y_sample[sl] = yT[:, SEQ:].T.reshape(16, 4, D)
        ho = r["hout"]
        hp[0, b] = ho[:, :, 0].T.reshape(W)
        hs[0, sl] = ho[:, :, 1:].transpose(2, 1, 0).reshape(16, W)
        co = r["cvout"]
        bp[0, b] = co[:, :, 0, :].transpose(2, 1, 0).reshape(3, W)
        bs[0, sl] = co[:, :, 1:, :].transpose(2, 3, 1, 0).reshape(16, 3, W)
        so = r["sout"]
        Sp[0, b] = so[:, 0].transpose(1, 0, 2)
        Ss[0, sl] = so[:, 1:].transpose(1, 2, 0, 3)
    return (y_prompt, y_sample, hp, bp, Sp, hs, bs, Ss)


def _assign_and_emit(P):
    P.emit()
```
